# Optimizing a Trainium2 kernel written in Bass

```python
import jax, jax.numpy as jnp
from jax import lax
import numpy as np

D_MODEL = 2048
BATCH = 4
SEQ = 2048
DEPTH = 1

CHUNK = 128
GMLP_GROUPS = 8
GMLP_GROUP_DIM = 128
GMLP_WIDTH = GMLP_GROUPS * GMLP_GROUP_DIM
ATTN_HEADS = 8
HEAD_DIM = 128
ATTN_WIDTH = ATTN_HEADS * HEAD_DIM
Q_BLOCK = 128
D_FF = 5632
CONV_WIDTH = 3
PLE_DIM = 256
N_BRANCH = 2
EPS = 1e-6
IN_COLS = 2 * GMLP_WIDTH + 3 * ATTN_WIDTH + ATTN_HEADS + N_BRANCH * D_MODEL

kernel_name = "hybrid_gmlp_fox_convffn_ple_block"


def _rms_norm(x, g):
    xf = x.astype(jnp.float32)
    y = xf * lax.rsqrt(jnp.mean(xf * xf, axis=-1, keepdims=True) + EPS)
    return (y * g.astype(jnp.float32)).astype(x.dtype)


def _layer_norm(x, g, b):
    xf = x.astype(jnp.float32)
    mu = jnp.mean(xf, axis=-1, keepdims=True)
    xc = xf - mu
    y = xc * lax.rsqrt(jnp.mean(xc * xc, axis=-1, keepdims=True) + EPS)
    return (y * g.astype(jnp.float32) + b.astype(jnp.float32)).astype(x.dtype)


def _spatial_gating(u, v, ln_g, ln_b, w_s, b_s):
    B, S, _ = v.shape
    n_chunks = S // CHUNK
    v = _layer_norm(v, ln_g, ln_b)
    vc = v.reshape(B, n_chunks, CHUNK, GMLP_GROUPS, GMLP_GROUP_DIM)
    causal = jnp.tril(jnp.ones((CHUNK, CHUNK), dtype=bool))
    w = jnp.where(causal[None], w_s, jnp.zeros((), w_s.dtype))
    mixed = jnp.einsum('gts,bcsgd->bctgd', w, vc) + b_s.T[None, None, :, :, None]
    return u * mixed.reshape(B, S, GMLP_WIDTH)


def _forgetting_attention(q, k, v, f_logit, q_norm_g, k_norm_g):
    B, S = q.shape[0], q.shape[1]
    q = _rms_norm(q, q_norm_g).transpose(0, 2, 1, 3)
    k = _rms_norm(k, k_norm_g).transpose(0, 2, 1, 3)
    v = v.transpose(0, 2, 1, 3)
    log_f = jax.nn.log_sigmoid(f_logit.astype(jnp.float32))
    cum = jnp.cumsum(log_f, axis=1).transpose(0, 2, 1)
    scale = HEAD_DIM ** -0.5
    outs = []
    for blk in range(S // Q_BLOCK):
        q0 = blk * Q_BLOCK
        q1 = q0 + Q_BLOCK
        qb = q[:, :, q0:q1]
        kb = k[:, :, :q1]
        vb = v[:, :, :q1]
        s = jnp.einsum('bhqd,bhkd->bhqk', qb, kb).astype(jnp.float32) * scale
        s = s + cum[:, :, q0:q1, None] - cum[:, :, None, :q1]
        causal = jnp.arange(q0, q1)[:, None] >= jnp.arange(q1)[None, :]
        s = jnp.where(causal, s, -jnp.inf)
        pr = jax.nn.softmax(s, axis=-1).astype(v.dtype)
        outs.append(jnp.einsum('bhqk,bhkd->bhqd', pr, vb))
    o = jnp.concatenate(outs, axis=2)
    return o.transpose(0, 2, 1, 3).reshape(B, S, ATTN_WIDTH)


def _conv_ffn(h, w_up, conv_w, conv_b, w_down):
    S = h.shape[1]
    up = h @ w_up
    a, b = jnp.split(up, 2, axis=-1)
    ap = jnp.pad(a, ((0, 0), (CONV_WIDTH - 1, 0), (0, 0)))
    conv = conv_b
    for j in range(CONV_WIDTH):
        conv = conv + ap[:, j:j + S] * conv_w[j]
    return (jax.nn.gelu(conv, approximate=True) * b) @ w_down


def setup_inputs(seed: int = 0) -> dict:
    key = jax.random.key(seed)
    ks = jax.random.split(key, 24)
    f32 = jnp.float32
    n = lambda k, shape, s: jax.random.normal(k, shape, f32) * s
    gain = lambda k, shape: 1.0 + 0.05 * jax.random.normal(k, shape, f32)
    return {
        "x": jax.random.normal(ks[0], (BATCH, SEQ, D_MODEL), f32),
        "p": jax.random.normal(ks[1], (DEPTH, BATCH, SEQ, PLE_DIM), f32),
        "norm_mix_g": gain(ks[2], (DEPTH, D_MODEL)),
        "w_in": n(ks[3], (DEPTH, D_MODEL, IN_COLS), D_MODEL ** -0.5),
        "gmlp_ln_g": gain(ks[4], (DEPTH, GMLP_WIDTH)),
        "gmlp_ln_b": n(ks[5], (DEPTH, GMLP_WIDTH), 0.02),
        "gmlp_w_s": n(ks[6], (DEPTH, GMLP_GROUPS, CHUNK, CHUNK), 0.5 * CHUNK ** -0.5),
        "gmlp_b_s": 1.0 + n(ks[7], (DEPTH, GMLP_GROUPS, CHUNK), 0.1),
        "fox_b_f": jax.random.uniform(ks[8], (DEPTH, ATTN_HEADS), f32, 1.0, 4.0),
        "q_norm_g": gain(ks[9], (DEPTH, HEAD_DIM)),
        "k_norm_g": gain(ks[10], (DEPTH, HEAD_DIM)),
        "w_branch_a": n(ks[11], (DEPTH, GMLP_WIDTH, D_MODEL), GMLP_WIDTH ** -0.5),
        "w_branch_b": n(ks[12], (DEPTH, ATTN_WIDTH, D_MODEL), ATTN_WIDTH ** -0.5),
        "w_out": n(ks[13], (DEPTH, D_MODEL, D_MODEL), D_MODEL ** -0.5),
        "norm_ffn_g": gain(ks[14], (DEPTH, D_MODEL)),
        "w_up": n(ks[15], (DEPTH, D_MODEL, 2 * D_FF), D_MODEL ** -0.5),
        "conv_w": n(ks[16], (DEPTH, CONV_WIDTH, D_FF), CONV_WIDTH ** -0.5),
        "conv_b": n(ks[17], (DEPTH, D_FF), 0.02),
        "w_down": n(ks[18], (DEPTH, D_FF, D_MODEL), D_FF ** -0.5),
        "ple_proj": n(ks[19], (DEPTH, PLE_DIM, D_MODEL), PLE_DIM ** -0.5),
        "ple_norm_g": gain(ks[20], (DEPTH, D_MODEL)),
        "ple_gate_norm_g": gain(ks[21], (DEPTH, D_MODEL)),
        "w_ple_gate": n(ks[22], (DEPTH, D_MODEL, D_MODEL), D_MODEL ** -0.5),
    }


def reference(x, p, norm_mix_g, w_in, gmlp_ln_g, gmlp_ln_b, gmlp_w_s, gmlp_b_s, fox_b_f,
              q_norm_g, k_norm_g, w_branch_a, w_branch_b, w_out, norm_ffn_g, w_up,
              conv_w, conv_b, w_down, ple_proj, ple_norm_g, ple_gate_norm_g, w_ple_gate):
    B, S, _ = x.shape
    sizes = [GMLP_WIDTH, GMLP_WIDTH, ATTN_WIDTH, ATTN_WIDTH, ATTN_WIDTH, ATTN_HEADS,
             D_MODEL, D_MODEL]
    points = []
    acc = 0
    for sz in sizes[:-1]:
        acc += sz
        points.append(acc)
    for i in range(DEPTH):
        h = _rms_norm(x, norm_mix_g[i])
        z = h @ w_in[i]
        u, v, q, k, vv, f, g_a, g_b = jnp.split(z, points, axis=-1)
        o_a = _spatial_gating(jax.nn.gelu(u, approximate=False), jax.nn.gelu(v, approximate=False),
                              gmlp_ln_g[i], gmlp_ln_b[i], gmlp_w_s[i], gmlp_b_s[i])
        o_b = _forgetting_attention(q.reshape(B, S, ATTN_HEADS, HEAD_DIM),
                                    k.reshape(B, S, ATTN_HEADS, HEAD_DIM),
                                    vv.reshape(B, S, ATTN_HEADS, HEAD_DIM),
                                    f + fox_b_f[i], q_norm_g[i], k_norm_g[i])
        y = jax.nn.sigmoid(g_a) * (o_a @ w_branch_a[i]) + jax.nn.sigmoid(g_b) * (o_b @ w_branch_b[i])
        x = x + y @ w_out[i]
        x = x + _conv_ffn(_rms_norm(x, norm_ffn_g[i]), w_up[i], conv_w[i], conv_b[i], w_down[i])
        e = _rms_norm(p[i] @ ple_proj[i], ple_norm_g[i])
        gate = jax.nn.sigmoid(_rms_norm(x, ple_gate_norm_g[i]) @ w_ple_gate[i])
        x = x + gate * e
    return x
```

```python
import numpy as np
from contextlib import ExitStack

import concourse.bass as bass
import concourse.mybir as mybir
from concourse.bass_utils import run_bass_kernel_spmd

F32 = mybir.dt.float32
BF16 = mybir.dt.bfloat16
AF = mybir.ActivationFunctionType
ALU = mybir.AluOpType
AX = mybir.AxisListType

NCORES = 8
P = 128
D = 2048
KC = 16
TW = 2048
NB = 16
QB = 9
TQ = QB * P
TO = 1024
DFF = 5632
NFC = 44
NG = 22
H = 8
EPS = 1e-6
SLOT = 4096
NS = 4
NEG = -30000.0

C_TRI = 0
C_E63 = 128
C_ID = 256
C_BF = 384
C_GQ = 512
C_GK = 513
C_CW = 514
C_CB = 646
C_FL = 690
C_ONE = 692
CST_W = 820

Q_TILES = [(0, 512), (512, 512), (1024, 128)]


def _weight_stream_plan():
    plan = []
    for s in range(4):
        plan.append(("v", s))
    for s in range(4):
        plan.append(("u", s))
    for s in range(4):
        plan.append(("k", s))
    for s in range(4):
        plan.append(("vv", s))
    plan.append(("f", 0))
    for s in range(4):
        plan.append(("q", s))
    for c in range(16):
        plan.append(("ga", c))
        plan.append(("gb", c))
    for cg in range(4):
        plan.append(("wo", 2 * cg))
        plan.append(("wo", 2 * cg + 1))
    plan.append(("uc", 0))
    plan.append(("uc", 1))
    for g in range(1, NG):
        plan.append(("uc", 2 * g))
        plan.append(("dn", g - 1))
        plan.append(("uc", 2 * g + 1))
    plan.append(("dn", NG - 1))
    for cg in range(4):
        plan.append(("wg", 2 * cg))
        plan.append(("wg", 2 * cg + 1))
    return plan


PLAN = _weight_stream_plan()
NL = len(PLAN)


def _slot_elems(kind):
    if kind == "f":
        return 16 * 8
    if kind in ("ga", "gb"):
        return 24 * 128
    return 4096


def _build_wstream(w_in, w_branch_a, w_branch_b, w_out, w_up, w_down, w_ple_gate):
    ws = np.zeros((NL, P, SLOT), dtype=np.float32)

    def kcols(w, c0, n):
        K = w.shape[0]
        return w[:, c0:c0 + n].reshape(K // P, P, n).transpose(1, 0, 2)

    for i, (kind, j) in enumerate(PLAN):
        if kind == "v":
            t = kcols(w_in, 1024 + 256 * j, 256)
        elif kind == "u":
            t = kcols(w_in, 256 * j, 256)
        elif kind == "k":
            t = kcols(w_in, 3072 + 256 * j, 256)
        elif kind == "vv":
            t = kcols(w_in, 4096 + 256 * j, 256)
        elif kind == "f":
            t = kcols(w_in, 5120, 8)
        elif kind == "q":
            t = kcols(w_in, 2048 + 256 * j, 256)
        elif kind == "ga":
            t = np.concatenate([kcols(w_in, 5128 + 128 * j, 128), kcols(w_branch_a, 128 * j, 128)], axis=1)
        elif kind == "gb":
            t = np.concatenate([kcols(w_in, 7176 + 128 * j, 128), kcols(w_branch_b, 128 * j, 128)], axis=1)
        elif kind == "wo":
            cg, hf = j // 2, j % 2
            t = kcols(w_out, 512 * cg, 512)[:, 8 * hf:8 * hf + 8, :]
        elif kind == "uc":
            t = np.concatenate([kcols(w_up, 128 * j, 128), kcols(w_up, DFF + 128 * j, 128)], axis=1)
        elif kind == "dn":
            t = w_down[256 * j:256 * j + 256, :].reshape(2, P, D).transpose(1, 0, 2)
        elif kind == "wg":
            cg, hf = j // 2, j % 2
            t = kcols(w_ple_gate, 512 * cg, 512)[:, 8 * hf:8 * hf + 8, :]
        else:
            raise AssertionError(kind)
        t = t.reshape(P, -1)
        ws[i, :, :t.shape[1]] = t
    return ws


class Tracker:
    ENG = ("pe", "act", "dve", "pool", "sp")

    def __init__(self, nc, es):
        self.nc = nc
        self.es = es
        self.sem = {}
        self.cnt = {}
        self.streams = {e: [] for e in self.ENG}
        self.known = {e: {} for e in self.ENG}
        self.lw = {}
        self.rd = {}
        self.tensors = []
        self.atoms_of = {}
        self._inherit = {}
        for e in self.ENG[:4]:
            self._mksem(e)

    def _mksem(self, name):
        if name not in self.sem:
            self.sem[name] = self.es.enter_context(self.nc.semaphore("s_" + name))
            self.cnt[name] = 0
        return self.sem[name]

    def tensor(self, name, lo, hi):
        inherited = {}
        self.ghosts = getattr(self, "ghosts", [])
        for (glo, ghi, gd) in self.ghosts:
            if not (hi <= glo or lo >= ghi):
                for s, v in gd.items():
                    inherited[s] = max(inherited.get(s, 0), v)
        for t in self.tensors:
            if t[3] and not (hi <= t[1] or lo >= t[2]):
                t[3] = False
                gd = dict(self._inherit.get(t[0], {}))
                for a in self.atoms_of.get(t[0], ()):
                    w = self.lw.pop(a, None)
                    if w is not None:
                        gd[w[0]] = max(gd.get(w[0], 0), w[1])
                    for s, v in self.rd.pop(a, {}).items():
                        gd[s] = max(gd.get(s, 0), v)
                self.atoms_of.pop(t[0], None)
                self.ghosts.append((t[1], t[2], gd))
                for s, v in gd.items():
                    inherited[s] = max(inherited.get(s, 0), v)
        self.tensors.append([name, lo, hi, True])
        self.atoms_of[name] = set()
        self._inherit[name] = inherited

    def _touch(self, a):
        name = a[0]
        s = self.atoms_of.get(name)
        if s is not None and a not in s:
            s.add(a)
            inh = self._inherit.get(name)
            if inh:
                self.rd[a] = dict(inh)

    def _deps(self, eng, reads, writes):
        deps = {}

        def add(s, v, kind):
            if s == eng and (eng == "pe" or kind == "war"):
                return
            if v > deps.get(s, 0):
                deps[s] = v

        for a in reads:
            self._touch(a)
            w = self.lw.get(a)
            if w is not None:
                add(w[0], w[1], "raw")
        for a in writes:
            self._touch(a)
            w = self.lw.get(a)
            if w is not None:
                add(w[0], w[1], "waw")
            for s, v in self.rd.get(a, {}).items():
                add(s, v, "war")
        kn = self.known[eng]
        out = []
        for s, v in deps.items():
            if kn.get(s, 0) < v:
                kn[s] = v
                out.append((s, v))
        return out

    def _record(self, reads, writes, tag):
        for a in reads:
            r = self.rd.setdefault(a, {})
            if r.get(tag[0], 0) < tag[1]:
                r[tag[0]] = tag[1]
        for a in writes:
            self.lw[a] = tag
            self.rd[a] = {}

    def op(self, eng, fns, reads=(), writes=()):
        if not isinstance(fns, (list, tuple)):
            fns = [fns]
        waits = self._deps(eng, reads, writes)
        self.cnt[eng] += 1
        val = self.cnt[eng]
        sem = self.sem[eng]
        st = self.streams[eng]
        for s, v in waits:
            st.append(("w", self.sem[s], v))
        for f in fns[:-1]:
            st.append(("i", f, None, 0))
        st.append(("i", fns[-1], sem, 1))
        self._record(reads, writes, (eng, val))

    def dma(self, queue, slot, fn, reads=(), writes=()):
        sem = self._mksem("d_" + slot)
        name = "d_" + slot
        waits = self._deps(queue, reads, writes)
        self.cnt[name] += 16
        val = self.cnt[name]
        st = self.streams[queue]
        for s, v in waits:
            st.append(("w", self.sem[s], v))
        st.append(("i", fn, sem, 16))
        self._record(reads, writes, (name, val))

    def merge(self, dst, srcs):
        r = self.rd.setdefault(dst, {})
        for a in srcs:
            w = self.lw.get(a)
            if w is not None and r.get(w[0], 0) < w[1]:
                r[w[0]] = w[1]
            for s_, v in self.rd.get(a, {}).items():
                if r.get(s_, 0) < v:
                    r[s_] = v

    def wait_all(self, eng, atoms):
        waits = self._deps(eng, atoms, ())
        for s, v in waits:
            self.streams[eng].append(("w", self.sem[s], v))

    def replay(self, eng, e):
        for it in self.streams[eng]:
            if it[0] == "w":
                e.wait_ge(it[1], it[2])
            else:
                ins = it[1](e)
                if it[2] is not None:
                    ins.then_inc(it[2], it[3])


def build_program():
    nc = bass.Bass("TRN2", target_bir_lowering=False)
    xw = nc.dram_tensor("xw", [TW, D], F32, kind="ExternalInput").ap()
    pw = nc.dram_tensor("pw", [P, 2 * TO], F32, kind="ExternalInput").ap()
    wstream = nc.dram_tensor("wstream", [NL, P, SLOT], F32, kind="ExternalInput").ap()
    pproj_d = nc.dram_tensor("pproj", [P, 2 * D], F32, kind="ExternalInput").ap()
    gbc_d = nc.dram_tensor("gbc", [4, P, D], F32, kind="ExternalInput").ap()
    lngb_d = nc.dram_tensor("lngb", [2, P, 1024], F32, kind="ExternalInput").ap()
    wsT_d = nc.dram_tensor("wsT", [P, 1024], F32, kind="ExternalInput").ap()
    cst_d = nc.dram_tensor("cst", [P, CST_W], F32, kind="ExternalInput").ap()
    bsrep_d = nc.dram_tensor("bsrep", [P, 1024], F32, kind="ExternalInput").ap()
    y = nc.dram_tensor("y", [TO, D], F32, kind="ExternalOutput").ap()

    es = ExitStack()
    with es:
        ARENA_B = 165888
        arena = es.enter_context(nc.sbuf_tensor("arena", [P, ARENA_B // 2], BF16))
        ring = es.enter_context(nc.sbuf_tensor("ring", [P, NS, SLOT], BF16))
        cst = es.enter_context(nc.sbuf_tensor("cst_sb", [P, CST_W], F32))
        cbf = es.enter_context(nc.sbuf_tensor("cbf", [P, 4, 128], BF16))
        st = es.enter_context(nc.sbuf_tensor("stats", [P, 160], F32))
        fxob = es.enter_context(nc.sbuf_tensor("fxob", [P, 1024], F32))
        fx = fxob[:, :].rearrange("p (i n) -> p i n", i=8)
        biasK = es.enter_context(nc.sbuf_tensor("biasK", [P, NB, H, 3], F32))
        sel = es.enter_context(nc.sbuf_tensor("sel", [P, H, 128], BF16))
        rsb = es.enter_context(nc.sbuf_tensor("rsb", [P, 4], F32))
        gq2 = es.enter_context(nc.sbuf_tensor("gq2", [P, 2], F32))
        ps = es.enter_context(nc.psum_tensor("ps", [P, 8, 512], F32))

        tr = Tracker(nc, es)

        def bf(lo, n):
            return arena[:, lo // 2: lo // 2 + n]

        def f32(lo, n):
            return arena[:, lo // 2: lo // 2 + 2 * n].bitcast(F32)

        ZA, ZB, ZC, ZD, ZE = 0, 36864, 73728, 110592, 147456
        c32 = cst[:, C_ONE:C_ONE + 128]
        wsb = bf(ZB + 32768, 1024).rearrange("p (g t) -> p g t", g=8)
        bsh = bf(ZB + 28672, 2048).rearrange("p (i n) -> p i n", i=2)

        ident = cbf[:, 0, :]
        ones_bf = cbf[:, 1, :]
        tri_bf = cbf[:, 2, :]

        bank_rr = [0]

        def nbank():
            b = bank_rr[0]
            bank_rr[0] = (b + 1) % 8
            return b

        ring_pos = [0]

        def load_w(expect_kind):
            i = ring_pos[0]
            kind, j = PLAN[i]
            assert kind == expect_kind, (kind, expect_kind)
            r = i % NS
            n = _slot_elems(kind)
            ring_pos[0] += 1
            tr.dma("pool", "ring%d" % r,
                   lambda e, r=r, i=i, n=n: e.dma_start(out=ring[:, r, 0:n], in_=wstream[i, :, 0:n]),
                   reads=(), writes=[("ring", r)])
            return r

        def act(fn, reads, writes):
            tr.op("act", fn, reads, writes)

        def dve(fn, reads, writes):
            tr.op("dve", fn, reads, writes)

        def pe(fns, reads, writes):
            tr.op("pe", fns, reads, writes)

        def mm(out, lhsT, rhs, start, stop):
            return lambda e: e.matmul(out, lhsT, rhs, start=start, stop=stop)

        def tp(out, in_):
            return lambda e: e.transpose(out, in_, ident)

        tr.dma("sp", "cst", lambda e: e.dma_start(out=cst[:], in_=cst_d[:, :]), writes=[("cst",)])
        tr.tensor("wsf", ZC, ZC + 4096)
        tr.tensor("bstmp", ZC + 4096, ZC + 8192)
        tr.tensor("bsf", ZC + 8192, ZC + 12288)
        tr.tensor("bsh", ZB + 28672, ZB + 32768)
        tr.tensor("wsb", ZB + 32768, ZB + 34816)
        wsf = f32(ZC, 1024).rearrange("p (g t) -> p g t", g=8)
        bst = f32(ZC + 4096, 1024)
        bsf = f32(ZC + 8192, 1024)
        tr.dma("sp", "wsf", lambda e: e.dma_start(out=f32(ZC, 1024), in_=wsT_d[:, :]), writes=[("wsf",)])
        tr.dma("sp", "bsf", lambda e: e.dma_start(out=bsf, in_=bsrep_d[:, :]), writes=[("bsf",)])
        dve(lambda e: e.tensor_copy(out=ident, in_=cst[:, C_ID:C_ID + 128]), [("cst",)], [("ident",)])
        dve(lambda e: e.memset(ones_bf, 1.0), [], [("ones_bf",)])
        dve(lambda e: e.tensor_scalar(out=tri_bf, in0=cst[:, C_TRI:C_TRI + 128], scalar1=-1.0, scalar2=30000.0, op0=ALU.add, op1=ALU.mult),
            [("cst",)], [("tri_bf",)])
        for h in range(H):
            dve(lambda e, h=h: e.tensor_scalar(out=sel[:, h, :], in0=c32, scalar1=cst[:, C_ID + h:C_ID + h + 1], scalar2=None, op0=ALU.mult),
                [("cst",)], [("sel",)])
        for g in range(8):
            dve(lambda e, g=g: e.tensor_tensor(out=wsb[:, g, :], in0=wsf[:, g, :], in1=cst[:, C_TRI:C_TRI + 128], op=ALU.mult),
                [("wsf",), ("cst",)], [("wsb",)])
        dve(lambda e: e.tensor_copy(out=bsh[:, 0, :], in_=bsf), [("bsf",)], [("bsh", 0)])
        dve(lambda e: e.tensor_tensor(out=bst, in0=bsf, in1=bsh[:, 0, :], op=ALU.subtract), [("bsf",), ("bsh", 0)], [("bstmp",)])
        dve(lambda e: e.tensor_copy(out=bsh[:, 1, :], in_=bst), [("bstmp",)], [("bsh", 1)])
        dve(lambda e: e.tensor_scalar(out=cbf[:, 3, :], in0=c32, scalar1=cst[:, C_ID:C_ID + 1], scalar2=None, op0=ALU.mult),
            [("cst",)], [("e0",)])
        dve(lambda e: e.tensor_scalar(out=gq2[:, 0:1], in0=cst[:, C_GQ:C_GQ + 1], scalar1=float(128 ** -0.5), scalar2=None, op0=ALU.mult),
            [("cst",)], [("gq2",)])
        dve(lambda e: e.tensor_copy(out=gq2[:, 1:2], in_=cst[:, C_GK:C_GK + 1]), [("cst",)], [("gq2",)])

        S_SS, S_RS = 0, 16
        S_V1, S_VM, S_VQ = 32, 68, 80
        S_REC = 96
        S_ES, S_ER = 104, 136

        def norm_A(it):
            if it.get("pre"):
                it["pre"]()
            scol = it["scol"]
            src_ap, xn, junk, gb = it["src"], it["xn"], it["junk"], it["gb"]
            act(lambda e: e.activation(out=junk, in_=src_ap, func=AF.Square, accum_out=st[:, S_SS + scol:S_SS + scol + 1]),
                it["src_atoms"], list(it["junk_atoms"]) + [("st", S_SS + scol)])
            act(lambda e: e.activation(out=st[:, S_RS + scol:S_RS + scol + 1], in_=st[:, S_SS + scol:S_SS + scol + 1],
                                       func=AF.Sqrt, scale=1.0 / D, bias=EPS),
                [("st", S_SS + scol)], [("st", S_RS + scol)])
            dve(lambda e: e.reciprocal(out=st[:, S_RS + scol:S_RS + scol + 1], in_=st[:, S_RS + scol:S_RS + scol + 1]),
                [("st", S_RS + scol)], [("st", S_RS + scol)])
            dve(lambda e: e.scalar_tensor_tensor(out=xn, in0=src_ap, scalar=st[:, S_RS + scol:S_RS + scol + 1], in1=gb,
                                                 op0=ALU.mult, op1=ALU.mult),
                list(it["src_atoms"]) + [("st", S_RS + scol), it["gb_atom"]], list(it["xn_atoms"]))

        def norm_B(it):
            xn, dst_fn, dst_atoms = it["xn"], it["dst_fn"], it["dst_atoms"]
            for half in range(2):
                b = nbank()
                pb = ps[:, b, :].bitcast(BF16)
                pe([tp(pb[:, k * 128:(k + 1) * 128], xn[:, (half * 8 + k) * 128:(half * 8 + k + 1) * 128]) for k in range(8)],
                   list(it["xn_atoms"]) + [("ident",)], [("ps", b)])
                if half == 0:
                    act(lambda e, pb=pb, half=half: e.activation(out=dst_fn(half * 8), in_=pb.rearrange("p (k t) -> p k t", k=8), func=AF.Copy),
                        [("ps", b)], dst_atoms(half * 8))
                else:
                    dve(lambda e, pb=pb, half=half: e.tensor_copy(out=dst_fn(half * 8), in_=pb.rearrange("p (k t) -> p k t", k=8)),
                        [("ps", b)], dst_atoms(half * 8))

        def norm_phase(items, L=2, hook=None):
            n = len(items)
            for i in range(n + L):
                if i < n:
                    norm_A(items[i])
                if i - L >= 0:
                    norm_B(items[i - L])
                if hook is not None:
                    hook(i)

        tr.tensor("hq", ZA, ZA + 36864)
        tr.tensor("hc", ZB, ZB + 28672)
        hq = bf(ZA, KC * TQ).rearrange("p (k t) -> p k t", k=KC)
        hc = bf(ZB, KC * 896).rearrange("p (k t) -> p k t", k=KC)
        tr.tensor("p1tmp", ZD, ZD + 36864)
        tr.tensor("p1tmp2", ZC + 12288, ZC + 24576)
        xblk = [f32(ZD + i * 8192, D) for i in range(3)]
        xnb = [bf(ZD + 24576 + i * 4096, D) for i in range(3)]
        gb1 = f32(ZC + 12288, D)
        junk1 = bf(ZC + 20480, D)
        tr.dma("sp", "gb1", lambda e: e.dma_start(out=gb1, in_=gbc_d[0, :, :]), writes=[("p1tmp2", "gb")])

        def hwin(kc, wb):
            if wb < 7:
                return hc[:, kc, wb * 128:(wb + 1) * 128]
            return hq[:, kc, (wb - 7) * 128:(wb - 6) * 128]

        def hwin_atom(kc, wb):
            return ("hc", kc, wb) if wb < 7 else ("hq", kc, wb - 7)

        items = []
        for n_i, wb in enumerate(list(range(7, NB)) + list(range(0, 7))):
            xi = n_i % 3
            xb = xblk[xi]
            if wb < 7:
                dst = lambda k0, wb=wb: hc[:, k0:k0 + 8, wb * 128:(wb + 1) * 128]
            else:
                dst = lambda k0, wb=wb: hq[:, k0:k0 + 8, (wb - 7) * 128:(wb - 6) * 128]
            items.append(dict(
                pre=lambda xb=xb, wb=wb, xi=xi: tr.dma("sp", "xb%d" % xi, lambda e: e.dma_start(out=xb, in_=xw[wb * 128:(wb + 1) * 128, :]),
                                                      writes=[("p1tmp", "xb", xi)]),
                src=xb, src_atoms=[("p1tmp", "xb", xi)], gb=gb1, gb_atom=("p1tmp2", "gb"),
                xn=xnb[xi], xn_atoms=[("p1tmp", "xn", xi)], junk=junk1, junk_atoms=[("p1tmp2", "junk")],
                dst_fn=dst, dst_atoms=lambda k0, wb=wb: [hwin_atom(k0 + k, wb) for k in range(8)], scol=n_i))
        tr.tensor("pre12", ZC + 24576, ZC + 36864)
        pT0 = bf(ZC + 24576, 2 * TO).rearrange("p (k t) -> p k t", k=2)
        ppj0 = bf(ZC + 28672, 2 * D).rearrange("p (k n) -> p k n", k=2)
        tr.dma("pool", "pT0", lambda e: e.dma_start(out=bf(ZC + 24576, 2 * TO), in_=pw[:, :]), writes=[("pre12", "pT")])
        tr.dma("pool", "pp0", lambda e: e.dma_start(out=bf(ZC + 28672, 2 * D), in_=pproj_d[:, :]), writes=[("pre12", "pp")])

        def e_stats(tb):
            for cg in range(4):
                b = nbank()
                pe([mm(ps[:, b, :], pT0[:, k, tb * 128:(tb + 1) * 128], ppj0[:, k, cg * 512:(cg + 1) * 512], k == 0, k == 1) for k in range(2)],
                   [("pre12", "pT"), ("pre12", "pp")], [("ps", b)])
                act(lambda e, b=b, tb=tb, cg=cg: e.activation(out=junk1[:, 0:512], in_=ps[:, b, :], func=AF.Square,
                                                              accum_out=st[:, S_ES + tb * 4 + cg:S_ES + tb * 4 + cg + 1]),
                    [("ps", b)], [("p1tmp2", "junk"), ("st", S_ES + tb * 4 + cg)])
            dve(lambda e, tb=tb: e.tensor_reduce(out=st[:, S_ER + tb:S_ER + tb + 1], in_=st[:, S_ES + tb * 4:S_ES + tb * 4 + 4], axis=AX.X, op=ALU.add),
                [("st", S_ES + tb * 4 + cg) for cg in range(4)], [("st", S_ER + tb)])
            act(lambda e, tb=tb: e.activation(out=st[:, S_ER + tb:S_ER + tb + 1], in_=st[:, S_ER + tb:S_ER + tb + 1], func=AF.Sqrt, scale=1.0 / D, bias=EPS),
                [("st", S_ER + tb)], [("st", S_ER + tb)])
            dve(lambda e, tb=tb: e.reciprocal(out=st[:, S_ER + tb:S_ER + tb + 1], in_=st[:, S_ER + tb:S_ER + tb + 1]), [("st", S_ER + tb)], [("st", S_ER + tb)])

        norm_phase(items, hook=lambda i: e_stats(i - 3) if 3 <= i < 11 else None)

        tr.tensor("gv", ZC, ZC + 36864)
        tr.tensor("vn", ZD, ZD + 18432)
        tr.tensor("lngb", ZE, ZE + 8192)
        gv = f32(ZC, QB * 1024).rearrange("p (b n) -> p b n", b=QB)
        vn = bf(ZD, QB * 1024).rearrange("p (b n) -> p b n", b=QB)
        lnG = f32(ZE, 1024)
        lnB = f32(ZE + 4096, 1024)
        tr.dma("sp", "lng", lambda e: e.dma_start(out=lnG, in_=lngb_d[0, :, :]), writes=[("lngb", 0)])
        tr.dma("sp", "lnb", lambda e: e.dma_start(out=lnB, in_=lngb_d[1, :, :]), writes=[("lngb", 1)])
        S_M2 = 144

        def ln_block(qb):
            gva = [("gv", qb, s) for s in range(4)]
            vm, vq, m2 = st[:, S_VM + qb:S_VM + qb + 1], st[:, S_VQ + qb:S_VQ + qb + 1], st[:, S_M2 + qb:S_M2 + qb + 1]
            dve(lambda e: e.tensor_reduce(out=vm, in_=st[:, S_V1 + qb * 4:S_V1 + qb * 4 + 4], axis=AX.X, op=ALU.add),
                [("st", S_V1 + qb * 4 + s) for s in range(4)], [("st", S_VM + qb)])
            dve(lambda e: e.tensor_scalar(out=vm, in0=vm, scalar1=-1.0 / 1024, scalar2=None, op0=ALU.mult),
                [("st", S_VM + qb)], [("st", S_VM + qb)])
            act(lambda e: e.activation(out=vn[:, qb, :], in_=gv[:, qb, :], func=AF.Square, accum_out=vq),
                gva, [("vn", qb), ("st", S_VQ + qb)])
            dve(lambda e: e.tensor_tensor(out=m2, in0=vm, in1=vm, op=ALU.mult), [("st", S_VM + qb)], [("st", S_M2 + qb)])
            dve(lambda e: e.tensor_scalar(out=vq, in0=vq, scalar1=1.0 / 1024, scalar2=m2, op0=ALU.mult, op1=ALU.subtract),
                [("st", S_VQ + qb), ("st", S_M2 + qb)], [("st", S_VQ + qb)])
            act(lambda e: e.activation(out=vq, in_=vq, func=AF.Sqrt, scale=1.0, bias=EPS), [("st", S_VQ + qb)], [("st", S_VQ + qb)])
            dve(lambda e: e.reciprocal(out=vq, in_=vq), [("st", S_VQ + qb)], [("st", S_VQ + qb)])
            dve(lambda e: e.scalar_tensor_tensor(out=gv[:, qb, :], in0=gv[:, qb, :], scalar=vm, in1=lnG, op0=ALU.add, op1=ALU.mult),
                gva + [("st", S_VM + qb), ("lngb", 0)], gva)
            dve(lambda e: e.scalar_tensor_tensor(out=vn[:, qb, :], in0=gv[:, qb, :], scalar=vq, in1=lnB, op0=ALU.mult, op1=ALU.add),
                gva + [("st", S_VQ + qb), ("lngb", 1)], [("vn", qb)])

        for s in range(4):
            r = load_w("v")
            wv = ring[:, r, :].rearrange("p (k n) -> p k n", k=KC)
            for qb in range(QB):
                b = nbank()
                pe([mm(ps[:, b, 0:256], hq[:, kc, qb * 128:(qb + 1) * 128], wv[:, kc, :], kc == 0, kc == KC - 1) for kc in range(KC)],
                   [("ring", r)] + [("hq", kc, qb) for kc in range(KC)], [("ps", b)])
                act(lambda e, b=b, qb=qb, s=s: e.activation(out=gv[:, qb, s * 256:(s + 1) * 256], in_=ps[:, b, 0:256], func=AF.Gelu,
                                                            accum_out=st[:, S_V1 + qb * 4 + s:S_V1 + qb * 4 + s + 1]),
                    [("ps", b)], [("gv", qb, s), ("st", S_V1 + qb * 4 + s)])
                if s == 3:
                    ln_block(qb)
        tr.tensor("oa", ZE, ZE + 18432)
        oa = bf(ZE, 8 * TQ).rearrange("p (g t) -> p g t", g=8)
        for s in range(4):
            r = load_w("u")
            wu = ring[:, r, :].rearrange("p (k n) -> p k n", k=KC)
            for gg in range(2):
                g = 2 * s + gg
                for ti, (t0, tn) in enumerate(Q_TILES):
                    b = nbank()
                    pe([mm(ps[:, b, 0:tn], wu[:, kc, gg * 128:(gg + 1) * 128], hq[:, kc, t0:t0 + tn], kc == 0, kc == KC - 1) for kc in range(KC)],
                       [("ring", r)] + [("hq", kc, qb) for kc in range(KC) for qb in range(t0 // 128, (t0 + tn) // 128)], [("ps", b)])
                    act(lambda e, b=b, g=g, t0=t0, tn=tn: e.activation(out=oa[:, g, t0:t0 + tn], in_=ps[:, b, 0:tn], func=AF.Gelu),
                        [("ps", b)], [("oa", g, qb) for qb in range(t0 // 128, (t0 + tn) // 128)])

        for qb in range(QB):
            for g0 in (0, 4):
                b = nbank()
                fns = []
                for g in range(g0, g0 + 4):
                    o = ps[:, b, (g - g0) * 128:(g - g0 + 1) * 128]
                    fns.append(mm(o, vn[:, qb, g * 128:(g + 1) * 128], wsb[:, g, :], True, False))
                    fns.append(mm(o, cbf[:, 3, :], bsh[:, 0, g * 128:(g + 1) * 128], False, False))
                    fns.append(mm(o, cbf[:, 3, :], bsh[:, 1, g * 128:(g + 1) * 128], False, True))
                pe(fns, [("vn", qb), ("wsb",), ("bsh", 0), ("bsh", 1), ("e0",)], [("ps", b)])
                dve(lambda e, b=b, g0=g0, qb=qb: e.tensor_tensor(out=oa[:, g0:g0 + 4, qb * 128:(qb + 1) * 128],
                                                                  in0=ps[:, b, :].rearrange("p (g t) -> p g t", g=4),
                                                                  in1=oa[:, g0:g0 + 4, qb * 128:(qb + 1) * 128], op=ALU.mult),
                    [("ps", b)] + [("oa", g, qb) for g in range(g0, g0 + 4)], [("oa", g, qb) for g in range(g0, g0 + 4)])

        tr.tensor("kT", ZC, ZC + 32768)
        tr.tensor("va", ZD, ZD + 33280)
        tr.tensor("rtm", ZC + 32768, ZC + 36864)
        tr.tensor("atmp", ZD + 33280, ZD + 36864)
        tr.tensor("fx", 10 ** 6, 10 ** 6 + 4096)
        kT = bf(ZC, H * TW).rearrange("p (h t) -> p h t", h=H)
        va = bf(ZD, NB * H * 130).rearrange("p (b h d) -> p b h d", b=NB, h=H)
        sqt = [bf(ZD + 33280 + i * 1024, 512) for i in range(2)]
        rtm = [f32(ZC + 32768 + i * 2048, 512) for i in range(2)]
        ptb = [bf(ZD + 35328 + i * 256, 128) for i in range(6)]
        W_TILES = [(0, 512), (512, 384)]

        qk_pend = [None]
        qk_cnt = [0]

        def qk_norm_tile(b, n, gcol, out_ap, out_atoms, cnt=None):
            i2 = qk_cnt[0] % 2
            qk_cnt[0] += 1
            act(lambda e: e.activation(out=sqt[i2][:, 0:n], in_=ps[:, b, 0:n], func=AF.Square), [("ps", b)], [("atmp", "sq", i2)])

            def tail():
                b2 = nbank()
                pe([mm(ps[:, b2, 0:n], ones_bf, sqt[i2][:, 0:n], True, True)], [("atmp", "sq", i2), ("ones_bf",)], [("ps", b2)])
                act(lambda e: e.activation(out=rtm[i2][:, 0:n], in_=ps[:, b2, 0:n], func=AF.Ln, scale=1.0 / 128, bias=EPS),
                    [("ps", b2)], [("rtm", i2)])
                act(lambda e: e.activation(out=rtm[i2][:, 0:n], in_=rtm[i2][:, 0:n], func=AF.Exp, scale=-0.5), [("rtm", i2)], [("rtm", i2)])
                dve(lambda e: e.scalar_tensor_tensor(out=out_ap, in0=ps[:, b, 0:n], scalar=gq2[:, gcol:gcol + 1], in1=rtm[i2][:, 0:n],
                                                     op0=ALU.mult, op1=ALU.mult),
                    [("ps", b), ("rtm", i2), ("gq2",)], out_atoms)

            if qk_pend[0] is not None:
                qk_pend[0]()
            qk_pend[0] = tail

        def qk_flush():
            if qk_pend[0] is not None:
                qk_pend[0]()
                qk_pend[0] = None

        cnt = 0
        for s in range(4):
            r = load_w("k")
            wk = ring[:, r, :].rearrange("p (k n) -> p k n", k=KC)
            for hh in range(2):
                h = 2 * s + hh
                tiles = [("c", t0, tn) for (t0, tn) in W_TILES] + [("q", t0, tn) for (t0, tn) in Q_TILES]
                for (src, t0, tn) in tiles:
                    b = nbank()
                    if src == "c":
                        rhs = lambda kc: hc[:, kc, t0:t0 + tn]
                        ratoms = [("hc", kc, wb) for kc in range(KC) for wb in range(t0 // 128, (t0 + tn) // 128)]
                        w0 = t0
                    else:
                        rhs = lambda kc: hq[:, kc, t0:t0 + tn]
                        ratoms = [("hq", kc, qb) for kc in range(KC) for qb in range(t0 // 128, (t0 + tn) // 128)]
                        w0 = 896 + t0
                    pe([mm(ps[:, b, 0:tn], wk[:, kc, hh * 128:(hh + 1) * 128], rhs(kc), kc == 0, kc == KC - 1) for kc in range(KC)],
                       [("ring", r)] + ratoms, [("ps", b)])
                    qk_norm_tile(b, tn, 1, kT[:, h, w0:w0 + tn], [("kT", h, wb) for wb in range(w0 // 128, (w0 + tn) // 128)], cnt)
                    cnt += 1
        qk_flush()
        dve(lambda e: e.memset(va[:, :, :, 128:130], 1.0), [], [("va", "ones")])
        for s in range(4):
            r = load_w("vv")
            wvv = ring[:, r, :].rearrange("p (k n) -> p k n", k=KC)
            for wb in range(NB):
                b = nbank()
                pe([mm(ps[:, b, 0:256], hwin(kc, wb), wvv[:, kc, :], kc == 0, kc == KC - 1) for kc in range(KC)],
                   [("ring", r)] + [hwin_atom(kc, wb) for kc in range(KC)], [("ps", b)])
                act(lambda e, b=b, wb=wb, s=s: e.activation(out=va[:, wb, 2 * s:2 * s + 2, 0:128],
                                                            in_=ps[:, b, 0:256].rearrange("p (h d) -> p h d", h=2), func=AF.Copy),
                    [("ps", b)], [("va", wb, 2 * s), ("va", wb, 2 * s + 1)])
        r = load_w("f")
        wf = ring[:, r, 0:128].rearrange("p (k n) -> p k n", k=KC)
        bF = nbank()
        fns = []
        for wb in range(NB):
            for kc in range(KC):
                fns.append(mm(ps[:, bF, wb * 8:(wb + 1) * 8], hwin(kc, wb), wf[:, kc, :], kc == 0, kc == KC - 1))
        pe(fns, [("ring", r)] + [hwin_atom(kc, wb) for kc in range(KC) for wb in range(NB)], [("ps", bF)])
        nl, cin, tot, mid, pre, cumN, refN = (fx[:, i, :] for i in range(7))
        dve(lambda e: e.tensor_tensor(out=nl, in0=ps[:, bF, 0:128], in1=cst[:, C_BF:C_BF + 128], op=ALU.add), [("ps", bF), ("cst",)], [("fx", 0)])
        act(lambda e: e.activation(out=nl, in_=nl, func=AF.Exp, scale=-1.0), [("fx", 0)], [("fx", 0)])
        act(lambda e: e.activation(out=nl, in_=nl, func=AF.Ln, bias=1.0), [("fx", 0)], [("fx", 0)])
        for i, lhs in ((1, cst[:, C_TRI:C_TRI + 128]), (2, c32), (3, cst[:, C_E63:C_E63 + 128])):
            b = nbank()
            pe([mm(ps[:, b, 0:128], lhs, nl, True, True)], [("fx", 0), ("cst",)], [("ps", b)])
            act(lambda e, b=b, i=i: e.activation(out=fx[:, i, :], in_=ps[:, b, 0:128], func=AF.Copy), [("ps", b)], [("fx", i)])
        dve(lambda e: e.memset(pre[:, 0:8], 0.0), [], [("fx", 4)])
        for wb in range(1, NB):
            dve(lambda e, wb=wb: e.tensor_tensor(out=pre[:, wb * 8:(wb + 1) * 8], in0=pre[:, (wb - 1) * 8:wb * 8], in1=tot[:, (wb - 1) * 8:wb * 8], op=ALU.add),
                [("fx", 4), ("fx", 2)], [("fx", 4)])
        dve(lambda e: e.tensor_tensor(out=cumN, in0=cin, in1=pre, op=ALU.add), [("fx", 1), ("fx", 4)], [("fx", 5)])
        dve(lambda e: e.tensor_tensor(out=refN, in0=mid, in1=pre, op=ALU.add), [("fx", 3), ("fx", 4)], [("fx", 6)])
        WBMID = [7, 10, 14]
        cum3 = cumN.rearrange("p (i h) -> p i h", h=H)
        for T in range(3):
            dve(lambda e, T=T: e.tensor_tensor(out=biasK[:, :, :, T], in0=cum3,
                                               in1=refN[:, WBMID[T] * 8:(WBMID[T] + 1) * 8].unsqueeze(1).to_broadcast([P, NB, H]), op=ALU.subtract),
                [("fx", 5), ("fx", 6)], [("biasK", T)])
            if T >= 1:
                dve(lambda e, T=T: e.tensor_scalar(out=biasK[:, 0:8, :, T], in0=biasK[:, 0:8, :, T], scalar1=cst[:, C_FL:C_FL + 1], scalar2=None, op0=ALU.add),
                    [("biasK", T), ("cst",)], [("biasK", T)])

        tr.tensor("qT", ZB, ZB + 18432)
        qT = bf(ZB, H * TQ).rearrange("p (h t) -> p h t", h=H)
        for s in range(4):
            r = load_w("q")
            wq = ring[:, r, :].rearrange("p (k n) -> p k n", k=KC)
            for hh in range(2):
                h = 2 * s + hh
                for (t0, tn) in Q_TILES:
                    b = nbank()
                    pe([mm(ps[:, b, 0:tn], wq[:, kc, hh * 128:(hh + 1) * 128], hq[:, kc, t0:t0 + tn], kc == 0, kc == KC - 1) for kc in range(KC)],
                       [("ring", r)] + [("hq", kc, qb) for kc in range(KC) for qb in range(t0 // 128, (t0 + tn) // 128)], [("ps", b)])
                    qk_norm_tile(b, tn, 0, qT[:, h, t0:t0 + tn], [("qT", h, qb) for qb in range(t0 // 128, (t0 + tn) // 128)], cnt)
                    cnt += 1

        qk_flush()
        tr.tensor("obT", ZB + 18432, ZB + 36864)
        tr.tensor("crow", ZD + 33280, ZD + 35584)
        tr.tensor("obh", ZD + 35584, ZD + 36608)
        tr.tensor("pt", ZC + 32768, ZC + 35840)
        obT = bf(ZB + 18432, H * TQ).rearrange("p (h t) -> p h t", h=H)
        crow = bf(ZD + 33280, TQ)
        obh = [bf(ZD + 35584 + i * 256, 128) for i in range(4)]
        ptb = [bf(ZC + 32768 + i * 1024, 512) for i in range(3)]
        TILES = [(0, 1), (1, 5), (5, 9)]
        idf = cst[:, C_ID:C_ID + 128]

        def tp32(out, in_):
            return lambda e: e.transpose(out, in_, idf)

        dve(lambda e: e.memset(crow, 0.0), [], [("crow",)])
        ba, bb, bc = nbank(), nbank(), nbank()
        pe([tp32(ps[0:8, ba, k * 128:(k + 1) * 128], cumN[:, (7 + k) * 8:(8 + k) * 8]) for k in range(4)], [("fx", 5), ("cst",)], [("ps", ba)])
        pe([tp32(ps[0:8, bb, k * 128:(k + 1) * 128], cumN[:, (11 + k) * 8:(12 + k) * 8]) for k in range(4)], [("fx", 5), ("cst",)], [("ps", bb)])
        pe([tp32(ps[0:8, bc, 0:128], cumN[:, 15 * 8:16 * 8])]
           + [tp32(ps[0:8, bc, (1 + T) * 128:(2 + T) * 128], refN[:, WBMID[T] * 8:(WBMID[T] + 1) * 8]) for T in range(3)],
           [("fx", 5), ("fx", 6), ("cst",)], [("ps", bc)])
        dve(lambda e: e.tensor_copy(out=rsb[0:8, 0:3], in_=ps[0:8, bc, 128:512].rearrange("p (t n) -> p t n", t=3)[:, :, 0]), [("ps", bc)], [("rsb",)])
        for qb in range(QB):
            bank, k = (ba, qb) if qb < 4 else ((bb, qb - 4) if qb < 8 else (bc, 0))
            T = 0 if qb == 0 else (1 if qb < 5 else 2)
            dve(lambda e, bank=bank, k=k, T=T, qb=qb: e.tensor_scalar(out=crow[0:8, qb * 128:(qb + 1) * 128], in0=ps[0:8, bank, k * 128:(k + 1) * 128],
                                                                       scalar1=-1.0, scalar2=rsb[0:8, T:T + 1], op0=ALU.mult, op1=ALU.add),
                [("ps", bank), ("rsb",), ("crow",)], [("crow",)])

        units = []
        for T, (q0, q1) in enumerate(TILES):
            for h in range(H):
                for i in range(7 + q1):
                    units.append((T, h, i, max(q0, i - 7), q1, q0))
        sbank = {}
        s_rr = [0]
        p_rr = [0]
        o_rr = [0]
        t_rr = [0]
        S_BANKS = (0, 1, 7)
        pbt6 = ps[:, 6, :].bitcast(BF16)
        for k in range(8):
            tr.merge(("pst", k), [("ps", 6)])

        def emit_S(u):
            T, h, i, qlo, q1, q0 = u
            n = (q1 - qlo) * 128
            b = S_BANKS[s_rr[0] % 3]
            s_rr[0] += 1
            sbank[u] = b
            diag = (i - 7) >= q0
            fns = [mm(ps[:, b, 0:n], kT[:, h, i * 128:(i + 1) * 128], qT[:, h, qlo * 128:q1 * 128], True, False),
                   mm(ps[:, b, 0:n], sel[:, h, :], crow[:, qlo * 128:q1 * 128], False, not diag)]
            if diag:
                fns.append(mm(ps[:, b, 0:128], ident, tri_bf, False, True))
            pe(fns, [("kT", h, i), ("crow",), ("sel",), ("ident",), ("tri_bf",)] + [("qT", h, qb) for qb in range(qlo, q1)], [("ps", b)])

        def emit_PV(u):
            T, h, i, qlo, q1, q0 = u
            n = (q1 - qlo) * 128
            b = sbank[u]
            pi = p_rr[0] % 3
            p_rr[0] += 1
            act(lambda e: e.activation(out=ptb[pi][:, 0:n], in_=ps[:, b, 0:n], func=AF.Exp, bias=biasK[:, i, h, T:T + 1]),
                [("ps", b), ("biasK", T)], [("pt", pi)])
            for qb in range(qlo, q1):
                j = 7 + qb
                bo = 2 + (qb - q0)
                pe([mm(ps[:, bo, 0:129], ptb[pi][:, (qb - qlo) * 128:(qb - qlo + 1) * 128], va[:, i, h, 0:129], i == 0, i == j)],
                   [("pt", pi), ("va", i, h), ("va", "ones")], [("ps", bo)])
                if i == j:
                    rc = S_REC + o_rr[0] % 8
                    oi = o_rr[0] % 4
                    o_rr[0] += 1
                    dve(lambda e, bo=bo, rc=rc: e.reciprocal(out=st[:, rc:rc + 1], in_=ps[:, bo, 128:129]), [("ps", bo)], [("st", rc)])
                    dve(lambda e, bo=bo, rc=rc, oi=oi: e.tensor_scalar(out=obh[oi], in0=ps[:, bo, 0:128], scalar1=st[:, rc:rc + 1], scalar2=None, op0=ALU.mult),
                        [("ps", bo), ("st", rc)], [("obh", oi)])
                    k = t_rr[0] % 8
                    t_rr[0] += 1

                    def fin(k=k, oi=oi, h=h, qb=qb):
                        pe([tp(pbt6[:, k * 128:(k + 1) * 128], obh[oi])], [("obh", oi), ("ident",)], [("pst", k)])
                        dve(lambda e: e.tensor_copy(out=obT[:, h, qb * 128:(qb + 1) * 128], in_=pbt6[:, k * 128:(k + 1) * 128]),
                            [("pst", k)], [("obT", h, qb)])
                    fin_pend.append(fin)

        fin_pend = []
        emit_S(units[0])
        for ui, u in enumerate(units):
            if ui + 1 < len(units):
                emit_S(units[ui + 1])
            ready = list(fin_pend)
            del fin_pend[:]
            emit_PV(u)
            for f_ in ready:
                f_()
        for f_ in fin_pend:
            f_()
        tr.merge(("ps", 6), [("pst", k) for k in range(8)])

        tr.tensor("yT", ZC, ZC + 36864)
        tr.tensor("gtmp", ZB, ZB + 8192)
        yT = bf(ZC, KC * TQ).rearrange("p (k t) -> p k t", k=KC)
        sgt = [f32(ZB + i * 2048, 512) for i in range(4)]
        gcnt = 0
        for c in range(16):
            ra = load_w("ga")
            rb = load_w("gb")
            wa = ring[:, ra, 0:3072].rearrange("p (k n) -> p k n", k=24)
            wb_ = ring[:, rb, 0:3072].rearrange("p (k n) -> p k n", k=24)
            for (t0, tn) in ((126, 342), (468, 342), (810, 342)):
                qbs = range(t0 // 128, (t0 + tn - 1) // 128 + 1)
                hqa = [("hq", kc, qb) for kc in range(KC) for qb in qbs]
                b1, b2, b3, b4 = nbank(), nbank(), nbank(), nbank()
                pe([mm(ps[:, b1, 0:tn], wa[:, kc, :], hq[:, kc, t0:t0 + tn], kc == 0, kc == KC - 1) for kc in range(KC)],
                   [("ring", ra)] + hqa, [("ps", b1)])
                pe([mm(ps[:, b2, 0:tn], wa[:, 16 + kc, :], oa[:, kc, t0:t0 + tn], kc == 0, kc == 7) for kc in range(8)],
                   [("ring", ra)] + [("oa", g, qb) for g in range(8) for qb in qbs], [("ps", b2)])
                pe([mm(ps[:, b3, 0:tn], wb_[:, kc, :], hq[:, kc, t0:t0 + tn], kc == 0, kc == KC - 1) for kc in range(KC)],
                   [("ring", rb)] + hqa, [("ps", b3)])
                pe([mm(ps[:, b4, 0:tn], wb_[:, 16 + kc, :], obT[:, kc, t0:t0 + tn], kc == 0, kc == 7) for kc in range(8)],
                   [("ring", rb)] + [("obT", hh, qb) for hh in range(8) for qb in qbs], [("ps", b4)])
                s1, s2 = sgt[(gcnt % 2) * 2], sgt[(gcnt % 2) * 2 + 1]
                a1, a2 = ("gtmp", (gcnt % 2) * 2), ("gtmp", (gcnt % 2) * 2 + 1)
                gcnt += 1
                act(lambda e, b1=b1, s1=s1, tn=tn: e.activation(out=s1[:, 0:tn], in_=ps[:, b1, 0:tn], func=AF.Sigmoid), [("ps", b1)], [a1])
                act(lambda e, b3=b3, s2=s2, tn=tn: e.activation(out=s2[:, 0:tn], in_=ps[:, b3, 0:tn], func=AF.Sigmoid), [("ps", b3)], [a2])
                dve(lambda e, b2=b2, s1=s1, tn=tn: e.tensor_tensor(out=s1[:, 0:tn], in0=ps[:, b2, 0:tn], in1=s1[:, 0:tn], op=ALU.mult), [("ps", b2), a1], [a1])
                dve(lambda e, b4=b4, s2=s2, tn=tn: e.tensor_tensor(out=s2[:, 0:tn], in0=ps[:, b4, 0:tn], in1=s2[:, 0:tn], op=ALU.mult), [("ps", b4), a2], [a2])
                dve(lambda e, s1=s1, s2=s2, c=c, t0=t0, tn=tn: e.tensor_tensor(out=yT[:, c, t0:t0 + tn], in0=s1[:, 0:tn], in1=s2[:, 0:tn], op=ALU.add),
                    [a1, a2], [("yT", c, qb) for qb in qbs])

        tr.tensor("x1", ZA, ZA + 65536)
        tr.tensor("xh", ZA + 65536, ZA + 73728)
        tr.tensor("xp", ZE, ZE + 8192)
        x1 = f32(ZA, 8 * D).rearrange("p (b n) -> p b n", b=8)
        xh = f32(ZA + 65536, D)
        xp = [f32(ZE + i * 2048, 512) for i in range(4)]
        xc = 0
        for cg in range(4):
            r0 = load_w("wo")
            r1 = load_w("wo")
            wo = [ring[:, r0, :].rearrange("p (k n) -> p k n", k=8), ring[:, r1, :].rearrange("p (k n) -> p k n", k=8)]
            for qb in range(QB):
                xi = xc % 4
                xc += 1
                tr.dma("sp", "xp%d" % xi,
                       lambda e, xi=xi, qb=qb, cg=cg: e.dma_start(out=xp[xi], in_=xw[(7 + qb) * 128:(8 + qb) * 128, cg * 512:(cg + 1) * 512]),
                       writes=[("xp", xi)])
                b = nbank()
                pe([mm(ps[:, b, :], yT[:, kc, qb * 128:(qb + 1) * 128], wo[kc // 8][:, kc % 8, :], kc == 0, kc == KC - 1) for kc in range(KC)],
                   [("ring", r0), ("ring", r1)] + [("yT", kc, qb) for kc in range(KC)], [("ps", b)])
                if qb == 0:
                    o, oat = xh[:, cg * 512:(cg + 1) * 512], ("xh", cg)
                else:
                    o, oat = x1[:, qb - 1, cg * 512:(cg + 1) * 512], ("x1", qb - 1, cg)
                dve(lambda e, b=b, o=o, xi=xi: e.tensor_tensor(out=o, in0=ps[:, b, :], in1=xp[xi], op=ALU.add), [("ps", b), ("xp", xi)], [oat])

        tr.tensor("h2", ZD, ZD + 36864)
        tr.tensor("ctmp", ZC, ZC + 36864)
        h2 = bf(ZD, KC * TQ).rearrange("p (k t) -> p k t", k=KC)
        xn2 = [bf(ZC, D), bf(ZC + 4096, D), bf(ZC + 24608 + 4096, D)]
        xn2_atoms = [[("ctmp", "xn", 0)], [("ctmp", "xn", 1)], [("ctmp", "tA", 1)]]
        junk2 = bf(ZC + 24608, D)
        gb2 = f32(ZC + 8192, D)
        tr.dma("sp", "gb2", lambda e: e.dma_start(out=gb2, in_=gbc_d[1, :, :]), writes=[("ctmp", "gb")])
        items = []
        for blk in range(QB):
            src = xh if blk == 0 else x1[:, blk - 1, :]
            satoms = [("xh", cg) for cg in range(4)] if blk == 0 else [("x1", blk - 1, cg) for cg in range(4)]
            items.append(dict(src=src, src_atoms=satoms, gb=gb2, gb_atom=("ctmp", "gb"), xn=xn2[blk % 3], xn_atoms=xn2_atoms[blk % 3],
                              junk=junk2, junk_atoms=[("ctmp", "tA", 0)],
                              dst_fn=lambda k0, blk=blk: h2[:, k0:k0 + 8, blk * 128:(blk + 1) * 128],
                              dst_atoms=lambda k0, blk=blk: [("h2", k0 + k, blk) for k in range(8)], scol=blk))
        norm_phase(items)

        a_sb = [f32(ZC + 16384 + i * 4112, 1028) for i in range(2)]
        tA = [f32(ZC + 24608 + i * 4096, 1024) for i in range(2)]
        tr.tensor("hid", ZE + 8192, ZE + 16384)
        hid = [bf(ZE + 8192 + i * 4096, 2048).rearrange("p (c t) -> p c t", c=2) for i in range(2)]
        tB = [f32(ZE + i * 4096, 1024) for i in range(2)]
        tr.tensor("tB", ZE, ZE + 8192)
        cw = cst[:, C_CW:C_CW + 132].rearrange("p (c j) -> p c j", j=3)
        cb = cst[:, C_CB:C_CB + 44]
        fcnt = [0]

        BA0, BA1, BAH, BB0, BB1 = 0, 1, 2, 3, 4
        d_rr = [0]

        def U_chunk_steps(g, cc):
            c = 2 * g + cc
            hb = hid[g % 2]
            i2 = c % 2
            asb, ta, tb_ = a_sb[i2], tA[i2], tB[i2]
            aat, tat, tbt = ("ctmp", "a", i2), ("ctmp", "tA", i2), ("tB", i2)
            box = {}

            def agroup(b, t0, tn):
                w = box["w"]
                pe([mm(ps[:, b, 0:tn], w[:, kc, :], h2[:, kc, t0:t0 + tn], kc == 0, kc == KC - 1) for kc in range(KC)],
                   [("ring", box["r"])] + [("h2", kc, blk) for kc in range(KC) for blk in range(t0 // 128, (t0 + tn - 1) // 128 + 1)], [("ps", b)])

            def bgroup(b, t0, tn):
                w = box["w"]
                pe([mm(ps[:, b, 0:tn], w[:, 16 + kc, :], h2[:, kc, t0:t0 + tn], kc == 0, kc == KC - 1) for kc in range(KC)],
                   [("ring", box["r"])] + [("h2", kc, blk) for kc in range(KC) for blk in range(t0 // 128, (t0 + tn) // 128)], [("ps", b)])

            def s0():
                box["r"] = load_w("uc")
                box["w"] = ring[:, box["r"], :].rearrange("p (k n) -> p k n", k=32)
                agroup(BA0, 126, 342)
                agroup(BA1, 468, 342)
                act(lambda e: e.activation(out=asb[:, 0:342], in_=ps[:, BA0, 0:342], func=AF.Copy), [("ps", BA0)], [aat])
                act(lambda e: e.activation(out=asb[:, 0:2], in_=asb[:, 0:2], func=AF.Copy, scale=cst[:, C_FL + 1:C_FL + 2]), [aat, ("cst",)], [aat])
                act(lambda e: e.activation(out=asb[:, 342:684], in_=ps[:, BA1, 0:342], func=AF.Copy), [("ps", BA1)], [aat])

            def s1():
                agroup(BAH, 810, 342)
                act(lambda e: e.activation(out=asb[:, 684:1026], in_=ps[:, BAH, 0:342], func=AF.Copy), [("ps", BAH)], [aat])

            def s1post():
                dve(lambda e: e.tensor_scalar(out=ta, in0=asb[:, 2:1026], scalar1=cw[:, c, 2:3], scalar2=cb[:, c:c + 1], op0=ALU.mult, op1=ALU.add),
                    [aat, ("cst",)], [tat])
                dve(lambda e: e.scalar_tensor_tensor(out=tb_, in0=asb[:, 1:1025], scalar=cw[:, c, 1:2], in1=ta, op0=ALU.mult, op1=ALU.add),
                    [aat, tat, ("cst",)], [tbt])
                dve(lambda e: e.scalar_tensor_tensor(out=ta, in0=asb[:, 0:1024], scalar=cw[:, c, 0:1], in1=tb_, op0=ALU.mult, op1=ALU.add),
                    [aat, tbt, ("cst",)], [tat])
                act(lambda e: e.activation(out=tb_, in_=ta, func=AF.Gelu_apprx_tanh), [tat], [tbt])

            def s2():
                bgroup(BB0, 128, 512)

            def s2post():
                dve(lambda e: e.tensor_tensor(out=hb[:, cc, 0:512], in0=ps[:, BB0, :], in1=tb_[:, 0:512], op=ALU.mult),
                    [("ps", BB0), tbt], [("hid", g % 2, cc, 0)])

            def s3():
                bgroup(BB1, 640, 512)

            def s3post():
                dve(lambda e: e.tensor_tensor(out=hb[:, cc, 512:1024], in0=ps[:, BB1, :], in1=tb_[:, 512:1024], op=ALU.mult),
                    [("ps", BB1), tbt], [("hid", g % 2, cc, 1)])

            return [(s0, None), (s1, s1post), (s2, s2post), (s3, s3post)]

        def D_steps(g):
            box = {}
            hb = hid[g % 2]

            def piece(tb, cg):
                def f():
                    if "r" not in box:
                        box["r"] = load_w("dn")
                        box["w"] = ring[:, box["r"], :].rearrange("p (c n) -> p c n", c=2)
                    wd = box["w"]
                    b = 5 + d_rr[0] % 3
                    d_rr[0] += 1
                    pe([mm(ps[:, b, :], hb[:, cc, tb * 128:(tb + 1) * 128], wd[:, cc, cg * 512:(cg + 1) * 512], cc == 0, cc == 1) for cc in range(2)],
                       [("ring", box["r"])] + [("hid", g % 2, cc, tb // 4) for cc in range(2)], [("ps", b)])
                    dve(lambda e: e.tensor_tensor(out=x1[:, tb, cg * 512:(cg + 1) * 512], in0=ps[:, b, :],
                                                  in1=x1[:, tb, cg * 512:(cg + 1) * 512], op=ALU.add),
                        [("ps", b), ("x1", tb, cg)], [("x1", tb, cg)])
                return f
            return [piece(tb, cg) for tb in range(8) for cg in range(4)]

        def p12_prelude():
            tr.tensor("h3", ZD, ZD + 32768)
            tr.tensor("c12", ZC, ZC + 36864)
            tr.tensor("pp", ZE, ZE + 8192)
            tr.dma("sp", "gb3", lambda e: e.dma_start(out=gb3, in_=gbc_d[2, :, :]), writes=[("c12", "gb")])
            tr.dma("pool", "pp", lambda e: e.dma_start(out=bf(ZE, 2 * D), in_=pproj_d[:, :]), writes=[("pp",)])
            tr.dma("pool", "pT", lambda e: e.dma_start(out=bf(ZC + 16384, 2 * TO), in_=pw[:, :]), writes=[("c12", "pT", tb) for tb in range(8)])

        h3 = bf(ZD, KC * TO).rearrange("p (k t) -> p k t", k=KC)
        xn3 = [bf(ZC, D), bf(ZC + 4096, D), bf(ZC + 27648, D)]
        xn3_atoms = [[("c12", "xn", 0)], [("c12", "xn", 1)], [("c12", "te", 0), ("c12", "te", 1)]]
        junk3 = bf(ZC + 23552, D)
        gb3 = f32(ZC + 8192, D)
        pT = bf(ZC + 16384, 2 * TO).rearrange("p (k t) -> p k t", k=2)
        pf = [f32(ZC + 20480 + i * 1024, 256) for i in range(2)]
        pb_ = [bf(ZC + 22528 + i * 512, 256) for i in range(2)]
        sg = [f32(ZC + 23552 + i * 2048, 512) for i in range(2)]
        te = [f32(ZC + 27648 + i * 2048, 512) for i in range(2)]
        gpl = [f32(ZC + 31744 + i * 2048, 512) for i in range(2)]
        ppj = bf(ZE, 2 * D).rearrange("p (k n) -> p k n", k=2)

        for g in range(NG + 1):
            usteps = (U_chunk_steps(g, 0) + U_chunk_steps(g, 1)) if g < NG else []
            dsteps = D_steps(g - 1) if g >= 1 else []
            if g == NG:
                p12_prelude()
            if usteps:
                per = len(dsteps) // len(usteps)
                for i, (u, post) in enumerate(usteps):
                    u()
                    for d in dsteps[i * per:(i + 1) * per]:
                        d()
                    if post is not None:
                        post()
            else:
                for d in dsteps:
                    d()

        items = []
        for tb in range(8):
            items.append(dict(src=x1[:, tb, :], src_atoms=[("x1", tb, cg) for cg in range(4)], gb=gb3, gb_atom=("c12", "gb"),
                              xn=xn3[tb % 3], xn_atoms=xn3_atoms[tb % 3], junk=junk3, junk_atoms=[("c12", "sg", 0), ("c12", "sg", 1)],
                              dst_fn=lambda k0, tb=tb: h3[:, k0:k0 + 8, tb * 128:(tb + 1) * 128],
                              dst_atoms=lambda k0, tb=tb: [("h3", k0 + k, tb) for k in range(8)], scol=tb))
        norm_phase(items)
        for cg in range(4):
            r0 = load_w("wg")
            r1 = load_w("wg")
            wg = [ring[:, r0, :].rearrange("p (k n) -> p k n", k=8), ring[:, r1, :].rearrange("p (k n) -> p k n", k=8)]
            gi = cg % 2
            tr.dma("sp", "gpl%d" % gi, lambda e, gi=gi, cg=cg: e.dma_start(out=gpl[gi], in_=gbc_d[3, :, cg * 512:(cg + 1) * 512]), writes=[("c12", "gpl", gi)])
            for tb in range(8):
                b1, b2 = nbank(), nbank()
                pe([mm(ps[:, b1, :], h3[:, kc, tb * 128:(tb + 1) * 128], wg[kc // 8][:, kc % 8, :], kc == 0, kc == KC - 1) for kc in range(KC)],
                   [("ring", r0), ("ring", r1)] + [("h3", kc, tb) for kc in range(KC)], [("ps", b1)])
                pe([mm(ps[:, b2, :], pT[:, k, tb * 128:(tb + 1) * 128], ppj[:, k, cg * 512:(cg + 1) * 512], k == 0, k == 1) for k in range(2)],
                   [("c12", "pT", tb), ("pp",)], [("ps", b2)])
                i2 = tb % 2
                act(lambda e, b1=b1, i2=i2: e.activation(out=sg[i2], in_=ps[:, b1, :], func=AF.Sigmoid), [("ps", b1)], [("c12", "sg", i2)])
                dve(lambda e, b2=b2, i2=i2, tb=tb, gi=gi: e.scalar_tensor_tensor(out=te[i2], in0=ps[:, b2, :], scalar=st[:, S_ER + tb:S_ER + tb + 1], in1=gpl[gi],
                                                                                 op0=ALU.mult, op1=ALU.mult),
                    [("ps", b2), ("st", S_ER + tb), ("c12", "gpl", gi)], [("c12", "te", i2)])
                dve(lambda e, i2=i2: e.tensor_tensor(out=te[i2], in0=te[i2], in1=sg[i2], op=ALU.mult), [("c12", "te", i2), ("c12", "sg", i2)], [("c12", "te", i2)])
                dve(lambda e, i2=i2, tb=tb, cg=cg: e.tensor_tensor(out=x1[:, tb, cg * 512:(cg + 1) * 512], in0=x1[:, tb, cg * 512:(cg + 1) * 512], in1=te[i2], op=ALU.add),
                    [("c12", "te", i2), ("x1", tb, cg)], [("x1", tb, cg)])
                if cg == 3:
                    tr.dma("sp", "out%d" % tb, lambda e, tb=tb: e.dma_start(out=y[tb * 128:(tb + 1) * 128, :], in_=x1[:, tb, :]),
                           reads=[("x1", tb, c4) for c4 in range(4)], writes=[("yout", tb)])
        tr.wait_all("sp", [("yout", tb) for tb in range(8)])
        assert ring_pos[0] == NL

        with nc.Block() as block:
            @block.sync
            def _(e):
                tr.replay("sp", e)

            @block.gpsimd
            def _(e):
                tr.replay("pool", e)

            @block.tensor
            def _(e):
                tr.replay("pe", e)

            @block.scalar
            def _(e):
                tr.replay("act", e)

            @block.vector
            def _(e):
                tr.replay("dve", e)
    return nc


_CACHE = {}


def kernel(x, p, norm_mix_g, w_in, gmlp_ln_g, gmlp_ln_b, gmlp_w_s, gmlp_b_s, fox_b_f,
           q_norm_g, k_norm_g, w_branch_a, w_branch_b, w_out, norm_ffn_g, w_up,
           conv_w, conv_b, w_down, ple_proj, ple_norm_g, ple_gate_norm_g, w_ple_gate):
    f = lambda a: np.ascontiguousarray(np.asarray(a, dtype=np.float32))
    x, p = f(x), f(p)
    w_in, w_branch_a, w_branch_b, w_out = f(w_in)[0], f(w_branch_a)[0], f(w_branch_b)[0], f(w_out)[0]
    w_up, w_down, w_ple_gate, ple_proj = f(w_up)[0], f(w_down)[0], f(w_ple_gate)[0], f(ple_proj)[0]

    ws = _build_wstream(w_in, w_branch_a, w_branch_b, w_out, w_up, w_down, w_ple_gate)
    pproj = np.ascontiguousarray(ple_proj.reshape(2, P, D).transpose(1, 0, 2).reshape(P, 2 * D))
    rep = lambda v: np.ascontiguousarray(np.broadcast_to(np.asarray(v, np.float32).reshape(1, -1), (P, np.asarray(v).size)))
    gbc = np.stack([rep(f(norm_mix_g)[0]), rep(f(norm_ffn_g)[0]), rep(f(ple_gate_norm_g)[0]), rep(f(ple_norm_g)[0])])
    lngb = np.stack([rep(f(gmlp_ln_g)[0]), rep(f(gmlp_ln_b)[0])])
    wsT = np.ascontiguousarray(f(gmlp_w_s)[0].transpose(2, 0, 1).reshape(P, 1024))
    cst = np.zeros((P, CST_W), np.float32)
    ii = np.arange(P)
    cst[:, C_TRI:C_TRI + 128] = (ii[None, :] >= ii[:, None]).astype(np.float32)
    cst[:, C_E63:C_E63 + 128] = (ii[:, None] <= 63).astype(np.float32)
    cst[:, C_ID:C_ID + 128] = np.eye(P, dtype=np.float32)
    cst[:, C_ONE:C_ONE + 128] = 1.0
    bsrep = rep(f(gmlp_b_s)[0].reshape(-1))
    cst[:, C_BF:C_BF + 128] = rep(np.tile(f(fox_b_f)[0], NB))
    cst[:, C_GQ] = f(q_norm_g)[0]
    cst[:, C_GK] = f(k_norm_g)[0]
    cst[:, C_CW:C_CW + 132] = f(conv_w)[0].reshape(3, NFC, P).transpose(2, 1, 0).reshape(P, 132)
    cst[:, C_CB:C_CB + 44] = f(conv_b)[0].reshape(NFC, P).T

    in_maps = []
    for c in range(NCORES):
        b, hf = c // 2, c % 2
        if hf == 1:
            xwc = x[b]
        else:
            xwc = np.concatenate([x[b, :1024], x[b, :1024]], axis=0)
        cc = cst.copy()
        cc[:, C_FL] = 0.0 if hf == 1 else NEG
        cc[:, C_FL + 1] = 1.0 if hf == 1 else 0.0
        in_maps.append({
            "xw": np.ascontiguousarray(xwc), "pw": np.ascontiguousarray(p[0, b, hf * 1024:(hf + 1) * 1024].T.reshape(2, P, TO).transpose(1, 0, 2).reshape(P, 2 * TO)),
            "wstream": ws, "pproj": pproj, "gbc": gbc, "lngb": lngb, "wsT": wsT, "cst": cc, "bsrep": bsrep,
        })
    if "nc" not in _CACHE:
        _CACHE["nc"] = build_program()
    res = run_bass_kernel_spmd(_CACHE["nc"], in_maps, core_ids=list(range(NCORES)))
    out = np.empty((4, 2048, D), np.float32)
    for c in range(NCORES):
        b, hf = c // 2, c % 2
        out[b, hf * 1024:(hf + 1) * 1024] = res.results[c]["y"]
    return out
```

```python
import numpy as np
from contextlib import ExitStack

import concourse.bass as bass
import concourse.mybir as mybir
from concourse.bass_utils import run_bass_kernel_spmd

F32 = mybir.dt.float32
BF16 = mybir.dt.bfloat16
AF = mybir.ActivationFunctionType
ALU = mybir.AluOpType
AX = mybir.AxisListType

NCORES = 8
P = 128
D = 2048
KC = 16
TW = 2048
NB = 16
QB = 9
TQ = QB * P
TO = 1024
DFF = 5632
NFC = 44
NG = 22
H = 8
EPS = 1e-6
SLOT = 4096
NS = 4
NEG = -30000.0

C_TRI = 0
C_E63 = 128
C_ID = 256
C_BF = 384
C_GQ = 512
C_GK = 513
C_CW = 514
C_CB = 646
C_FL = 690
C_ONE = 692
CST_W = 820

Q_TILES = [(0, 512), (512, 512), (1024, 128)]


def _weight_stream_plan():
    plan = []
    for s in range(4):
        plan.append(("v", s))
    for s in range(4):
        plan.append(("u", s))
    for s in range(4):
        plan.append(("k", s))
    for s in range(4):
        plan.append(("vv", s))
    plan.append(("f", 0))
    for s in range(4):
        plan.append(("q", s))
    for c in range(16):
        plan.append(("ga", c))
        plan.append(("gb", c))
    for cg in range(4):
        plan.append(("wo", 2 * cg))
        plan.append(("wo", 2 * cg + 1))
    plan.append(("uc", 0))
    plan.append(("uc", 1))
    for g in range(1, NG):
        plan.append(("uc", 2 * g))
        plan.append(("dn", g - 1))
        plan.append(("uc", 2 * g + 1))
    plan.append(("dn", NG - 1))
    for cg in range(4):
        plan.append(("wg", 2 * cg))
        plan.append(("wg", 2 * cg + 1))
    return plan


PLAN = _weight_stream_plan()
NL = len(PLAN)


def _slot_elems(kind):
    if kind == "f":
        return 16 * 8
    if kind in ("ga", "gb"):
        return 24 * 128
    return 4096


def _build_wstream(w_in, w_branch_a, w_branch_b, w_out, w_up, w_down, w_ple_gate):
    ws = np.zeros((NL, P, SLOT), dtype=np.float32)

    def kcols(w, c0, n):
        K = w.shape[0]
        return w[:, c0:c0 + n].reshape(K // P, P, n).transpose(1, 0, 2)

    for i, (kind, j) in enumerate(PLAN):
        if kind == "v":
            t = kcols(w_in, 1024 + 256 * j, 256)
        elif kind == "u":
            t = kcols(w_in, 256 * j, 256)
        elif kind == "k":
            t = kcols(w_in, 3072 + 256 * j, 256)
        elif kind == "vv":
            t = kcols(w_in, 4096 + 256 * j, 256)
        elif kind == "f":
            t = kcols(w_in, 5120, 8)
        elif kind == "q":
            t = kcols(w_in, 2048 + 256 * j, 256)
        elif kind == "ga":
            t = np.concatenate([kcols(w_in, 5128 + 128 * j, 128), kcols(w_branch_a, 128 * j, 128)], axis=1)
        elif kind == "gb":
            t = np.concatenate([kcols(w_in, 7176 + 128 * j, 128), kcols(w_branch_b, 128 * j, 128)], axis=1)
        elif kind == "wo":
            cg, hf = j // 2, j % 2
            t = kcols(w_out, 512 * cg, 512)[:, 8 * hf:8 * hf + 8, :]
        elif kind == "uc":
            t = np.concatenate([kcols(w_up, 128 * j, 128), kcols(w_up, DFF + 128 * j, 128)], axis=1)
        elif kind == "dn":
            t = w_down[256 * j:256 * j + 256, :].reshape(2, P, D).transpose(1, 0, 2)
        elif kind == "wg":
            cg, hf = j // 2, j % 2
            t = kcols(w_ple_gate, 512 * cg, 512)[:, 8 * hf:8 * hf + 8, :]
        else:
            raise AssertionError(kind)
        t = t.reshape(P, -1)
        ws[i, :, :t.shape[1]] = t
    return ws


class Tracker:
    ENG = ("pe", "act", "dve", "pool", "sp")

    def __init__(self, nc, es):
        self.nc = nc
        self.es = es
        self.sem = {}
        self.cnt = {}
        self.streams = {e: [] for e in self.ENG}
        self.known = {e: {} for e in self.ENG}
        self.lw = {}
        self.rd = {}
        self.tensors = []
        self.atoms_of = {}
        self._inherit = {}
        for e in self.ENG[:4]:
            self._mksem(e)

    def _mksem(self, name):
        if name not in self.sem:
            self.sem[name] = self.es.enter_context(self.nc.semaphore("s_" + name))
            self.cnt[name] = 0
        return self.sem[name]

    def tensor(self, name, lo, hi):
        inherited = {}
        self.ghosts = getattr(self, "ghosts", [])
        for (glo, ghi, gd) in self.ghosts:
            if not (hi <= glo or lo >= ghi):
                for s, v in gd.items():
                    inherited[s] = max(inherited.get(s, 0), v)
        for t in self.tensors:
            if t[3] and not (hi <= t[1] or lo >= t[2]):
                t[3] = False
                gd = dict(self._inherit.get(t[0], {}))
                for a in self.atoms_of.get(t[0], ()):
                    w = self.lw.pop(a, None)
                    if w is not None:
                        gd[w[0]] = max(gd.get(w[0], 0), w[1])
                    for s, v in self.rd.pop(a, {}).items():
                        gd[s] = max(gd.get(s, 0), v)
                self.atoms_of.pop(t[0], None)
                self.ghosts.append((t[1], t[2], gd))
                for s, v in gd.items():
                    inherited[s] = max(inherited.get(s, 0), v)
        self.tensors.append([name, lo, hi, True])
        self.atoms_of[name] = set()
        self._inherit[name] = inherited

    def _touch(self, a):
        name = a[0]
        s = self.atoms_of.get(name)
        if s is not None and a not in s:
            s.add(a)
            inh = self._inherit.get(name)
            if inh:
                self.rd[a] = dict(inh)

    def _deps(self, eng, reads, writes):
        deps = {}

        def add(s, v, kind):
            if s == eng and (eng == "pe" or kind == "war"):
                return
            if v > deps.get(s, 0):
                deps[s] = v

        for a in reads:
            self._touch(a)
            w = self.lw.get(a)
            if w is not None:
                add(w[0], w[1], "raw")
        for a in writes:
            self._touch(a)
            w = self.lw.get(a)
            if w is not None:
                add(w[0], w[1], "waw")
            for s, v in self.rd.get(a, {}).items():
                add(s, v, "war")
        kn = self.known[eng]
        out = []
        for s, v in deps.items():
            if kn.get(s, 0) < v:
                kn[s] = v
                out.append((s, v))
        return out

    def _record(self, reads, writes, tag):
        for a in reads:
            r = self.rd.setdefault(a, {})
            if r.get(tag[0], 0) < tag[1]:
                r[tag[0]] = tag[1]
        for a in writes:
            self.lw[a] = tag
            self.rd[a] = {}

    def op(self, eng, fns, reads=(), writes=()):
        if not isinstance(fns, (list, tuple)):
            fns = [fns]
        waits = self._deps(eng, reads, writes)
        self.cnt[eng] += 1
        val = self.cnt[eng]
        sem = self.sem[eng]
        st = self.streams[eng]
        for s, v in waits:
            st.append(("w", self.sem[s], v))
        for f in fns[:-1]:
            st.append(("i", f, None, 0))
        st.append(("i", fns[-1], sem, 1))
        self._record(reads, writes, (eng, val))

    def dma(self, queue, slot, fn, reads=(), writes=()):
        sem = self._mksem("d_" + slot)
        name = "d_" + slot
        waits = self._deps(queue, reads, writes)
        self.cnt[name] += 16
        val = self.cnt[name]
        st = self.streams[queue]
        for s, v in waits:
            st.append(("w", self.sem[s], v))
        st.append(("i", fn, sem, 16))
        self._record(reads, writes, (name, val))

    def merge(self, dst, srcs):
        r = self.rd.setdefault(dst, {})
        for a in srcs:
            w = self.lw.get(a)
            if w is not None and r.get(w[0], 0) < w[1]:
                r[w[0]] = w[1]
            for s_, v in self.rd.get(a, {}).items():
                if r.get(s_, 0) < v:
                    r[s_] = v

    def wait_all(self, eng, atoms):
        waits = self._deps(eng, atoms, ())
        for s, v in waits:
            self.streams[eng].append(("w", self.sem[s], v))

    def replay(self, eng, e):
        for it in self.streams[eng]:
            if it[0] == "w":
                e.wait_ge(it[1], it[2])
            else:
                ins = it[1](e)
                if it[2] is not None:
                    ins.then_inc(it[2], it[3])


def build_program():
    nc = bass.Bass("TRN2", target_bir_lowering=False)
    xw = nc.dram_tensor("xw", [TW, D], F32, kind="ExternalInput").ap()
    pw = nc.dram_tensor("pw", [P, 2 * TO], F32, kind="ExternalInput").ap()
    wstream = nc.dram_tensor("wstream", [NL, P, SLOT], F32, kind="ExternalInput").ap()
    pproj_d = nc.dram_tensor("pproj", [P, 2 * D], F32, kind="ExternalInput").ap()
    gbc_d = nc.dram_tensor("gbc", [4, P, D], F32, kind="ExternalInput").ap()
    lngb_d = nc.dram_tensor("lngb", [2, P, 1024], F32, kind="ExternalInput").ap()
    wsT_d = nc.dram_tensor("wsT", [P, 1024], F32, kind="ExternalInput").ap()
    cst_d = nc.dram_tensor("cst", [P, CST_W], F32, kind="ExternalInput").ap()
    bsrep_d = nc.dram_tensor("bsrep", [P, 1024], F32, kind="ExternalInput").ap()
    y = nc.dram_tensor("y", [TO, D], F32, kind="ExternalOutput").ap()

    es = ExitStack()
    with es:
        ARENA_B = 165888
        arena = es.enter_context(nc.sbuf_tensor("arena", [P, ARENA_B // 2], BF16))
        ring = es.enter_context(nc.sbuf_tensor("ring", [P, NS, SLOT], BF16))
        cst = es.enter_context(nc.sbuf_tensor("cst_sb", [P, CST_W], F32))
        cbf = es.enter_context(nc.sbuf_tensor("cbf", [P, 4, 128], BF16))
        st = es.enter_context(nc.sbuf_tensor("stats", [P, 160], F32))
        fxob = es.enter_context(nc.sbuf_tensor("fxob", [P, 1024], F32))
        fx = fxob[:, :].rearrange("p (i n) -> p i n", i=8)
        biasK = es.enter_context(nc.sbuf_tensor("biasK", [P, NB, H, 3], F32))
        sel = es.enter_context(nc.sbuf_tensor("sel", [P, H, 128], BF16))
        rsb = es.enter_context(nc.sbuf_tensor("rsb", [P, 4], F32))
        gq2 = es.enter_context(nc.sbuf_tensor("gq2", [P, 2], F32))
        ps = es.enter_context(nc.psum_tensor("ps", [P, 8, 512], F32))

        tr = Tracker(nc, es)

        def bf(lo, n):
            return arena[:, lo // 2: lo // 2 + n]

        def f32(lo, n):
            return arena[:, lo // 2: lo // 2 + 2 * n].bitcast(F32)

        ZA, ZB, ZC, ZD, ZE = 0, 36864, 73728, 110592, 147456
        c32 = cst[:, C_ONE:C_ONE + 128]
        wsb = bf(ZB + 32768, 1024).rearrange("p (g t) -> p g t", g=8)
        bsh = bf(ZB + 28672, 2048).rearrange("p (i n) -> p i n", i=2)

        ident = cbf[:, 0, :]
        ones_bf = cbf[:, 1, :]
        tri_bf = cbf[:, 2, :]

        bank_rr = [0]

        def nbank():
            b = bank_rr[0]
            bank_rr[0] = (b + 1) % 8
            return b

        ring_pos = [0]

        def load_w(expect_kind):
            i = ring_pos[0]
            kind, j = PLAN[i]
            assert kind == expect_kind, (kind, expect_kind)
            r = i % NS
            n = _slot_elems(kind)
            ring_pos[0] += 1
            tr.dma("pool", "ring%d" % r,
                   lambda e, r=r, i=i, n=n: e.dma_start(out=ring[:, r, 0:n], in_=wstream[i, :, 0:n]),
                   reads=(), writes=[("ring", r)])
            return r

        def act(fn, reads, writes):
            tr.op("act", fn, reads, writes)

        def dve(fn, reads, writes):
            tr.op("dve", fn, reads, writes)

        def pe(fns, reads, writes):
            tr.op("pe", fns, reads, writes)

        def mm(out, lhsT, rhs, start, stop):
            return lambda e: e.matmul(out, lhsT, rhs, start=start, stop=stop)

        def tp(out, in_):
            return lambda e: e.transpose(out, in_, ident)

        tr.dma("sp", "cst", lambda e: e.dma_start(out=cst[:], in_=cst_d[:, :]), writes=[("cst",)])
        tr.tensor("wsf", ZC, ZC + 4096)
        tr.tensor("bstmp", ZC + 4096, ZC + 8192)
        tr.tensor("bsf", ZC + 8192, ZC + 12288)
        tr.tensor("bsh", ZB + 28672, ZB + 32768)
        tr.tensor("wsb", ZB + 32768, ZB + 34816)
        wsf = f32(ZC, 1024).rearrange("p (g t) -> p g t", g=8)
        bst = f32(ZC + 4096, 1024)
        bsf = f32(ZC + 8192, 1024)
        tr.dma("sp", "wsf", lambda e: e.dma_start(out=f32(ZC, 1024), in_=wsT_d[:, :]), writes=[("wsf",)])
        tr.dma("sp", "bsf", lambda e: e.dma_start(out=bsf, in_=bsrep_d[:, :]), writes=[("bsf",)])
        dve(lambda e: e.tensor_copy(out=ident, in_=cst[:, C_ID:C_ID + 128]), [("cst",)], [("ident",)])
        dve(lambda e: e.memset(ones_bf, 1.0), [], [("ones_bf",)])
        dve(lambda e: e.tensor_scalar(out=tri_bf, in0=cst[:, C_TRI:C_TRI + 128], scalar1=-1.0, scalar2=30000.0, op0=ALU.add, op1=ALU.mult),
            [("cst",)], [("tri_bf",)])
        for h in range(H):
            dve(lambda e, h=h: e.tensor_scalar(out=sel[:, h, :], in0=c32, scalar1=cst[:, C_ID + h:C_ID + h + 1], scalar2=None, op0=ALU.mult),
                [("cst",)], [("sel",)])
        for g in range(8):
            dve(lambda e, g=g: e.tensor_tensor(out=wsb[:, g, :], in0=wsf[:, g, :], in1=cst[:, C_TRI:C_TRI + 128], op=ALU.mult),
                [("wsf",), ("cst",)], [("wsb",)])
        dve(lambda e: e.tensor_copy(out=bsh[:, 0, :], in_=bsf), [("bsf",)], [("bsh", 0)])
        dve(lambda e: e.tensor_tensor(out=bst, in0=bsf, in1=bsh[:, 0, :], op=ALU.subtract), [("bsf",), ("bsh", 0)], [("bstmp",)])
        dve(lambda e: e.tensor_copy(out=bsh[:, 1, :], in_=bst), [("bstmp",)], [("bsh", 1)])
        dve(lambda e: e.tensor_scalar(out=cbf[:, 3, :], in0=c32, scalar1=cst[:, C_ID:C_ID + 1], scalar2=None, op0=ALU.mult),
            [("cst",)], [("e0",)])
        dve(lambda e: e.tensor_scalar(out=gq2[:, 0:1], in0=cst[:, C_GQ:C_GQ + 1], scalar1=float(128 ** -0.5), scalar2=None, op0=ALU.mult),
            [("cst",)], [("gq2",)])
        dve(lambda e: e.tensor_copy(out=gq2[:, 1:2], in_=cst[:, C_GK:C_GK + 1]), [("cst",)], [("gq2",)])

        S_SS, S_RS = 0, 16
        S_V1, S_VM, S_VQ = 32, 68, 80
        S_REC = 96
        S_ES, S_ER = 104, 136

        def norm_A(it):
            if it.get("pre"):
                it["pre"]()
            scol = it["scol"]
            src_ap, xn, junk, gb = it["src"], it["xn"], it["junk"], it["gb"]
            act(lambda e: e.activation(out=junk, in_=src_ap, func=AF.Square, accum_out=st[:, S_SS + scol:S_SS + scol + 1]),
                it["src_atoms"], list(it["junk_atoms"]) + [("st", S_SS + scol)])
            act(lambda e: e.activation(out=st[:, S_RS + scol:S_RS + scol + 1], in_=st[:, S_SS + scol:S_SS + scol + 1],
                                       func=AF.Sqrt, scale=1.0 / D, bias=EPS),
                [("st", S_SS + scol)], [("st", S_RS + scol)])
            dve(lambda e: e.reciprocal(out=st[:, S_RS + scol:S_RS + scol + 1], in_=st[:, S_RS + scol:S_RS + scol + 1]),
                [("st", S_RS + scol)], [("st", S_RS + scol)])
            dve(lambda e: e.scalar_tensor_tensor(out=xn, in0=src_ap, scalar=st[:, S_RS + scol:S_RS + scol + 1], in1=gb,
                                                 op0=ALU.mult, op1=ALU.mult),
                list(it["src_atoms"]) + [("st", S_RS + scol), it["gb_atom"]], list(it["xn_atoms"]))

        def norm_B(it):
            xn, dst_fn, dst_atoms = it["xn"], it["dst_fn"], it["dst_atoms"]
            for half in range(2):
                b = nbank()
                pb = ps[:, b, :].bitcast(BF16)
                pe([tp(pb[:, k * 128:(k + 1) * 128], xn[:, (half * 8 + k) * 128:(half * 8 + k + 1) * 128]) for k in range(8)],
                   list(it["xn_atoms"]) + [("ident",)], [("ps", b)])
                if half == 0:
                    act(lambda e, pb=pb, half=half: e.activation(out=dst_fn(half * 8), in_=pb.rearrange("p (k t) -> p k t", k=8), func=AF.Copy),
                        [("ps", b)], dst_atoms(half * 8))
                else:
                    dve(lambda e, pb=pb, half=half: e.tensor_copy(out=dst_fn(half * 8), in_=pb.rearrange("p (k t) -> p k t", k=8)),
                        [("ps", b)], dst_atoms(half * 8))

        def norm_phase(items, L=2, hook=None):
            n = len(items)
            for i in range(n + L):
                if i < n:
                    norm_A(items[i])
                if i - L >= 0:
                    norm_B(items[i - L])
                if hook is not None:
                    hook(i)

        tr.tensor("hq", ZA, ZA + 36864)
        tr.tensor("hc", ZB, ZB + 28672)
        hq = bf(ZA, KC * TQ).rearrange("p (k t) -> p k t", k=KC)
        hc = bf(ZB, KC * 896).rearrange("p (k t) -> p k t", k=KC)
        tr.tensor("p1tmp", ZD, ZD + 36864)
        tr.tensor("p1tmp2", ZC + 12288, ZC + 24576)
        xblk = [f32(ZD + i * 8192, D) for i in range(3)]
        xnb = [bf(ZD + 24576 + i * 4096, D) for i in range(3)]
        gb1 = f32(ZC + 12288, D)
        junk1 = bf(ZC + 20480, D)
        tr.dma("sp", "gb1", lambda e: e.dma_start(out=gb1, in_=gbc_d[0, :, :]), writes=[("p1tmp2", "gb")])

        def hwin(kc, wb):
            if wb < 7:
                return hc[:, kc, wb * 128:(wb + 1) * 128]
            return hq[:, kc, (wb - 7) * 128:(wb - 6) * 128]

        def hwin_atom(kc, wb):
            return ("hc", kc, wb) if wb < 7 else ("hq", kc, wb - 7)

        items = []
        for n_i, wb in enumerate(list(range(7, NB)) + list(range(0, 7))):
            xi = n_i % 3
            xb = xblk[xi]
            if wb < 7:
                dst = lambda k0, wb=wb: hc[:, k0:k0 + 8, wb * 128:(wb + 1) * 128]
            else:
                dst = lambda k0, wb=wb: hq[:, k0:k0 + 8, (wb - 7) * 128:(wb - 6) * 128]
            items.append(dict(
                pre=lambda xb=xb, wb=wb, xi=xi: tr.dma("sp", "xb%d" % xi, lambda e: e.dma_start(out=xb, in_=xw[wb * 128:(wb + 1) * 128, :]),
                                                      writes=[("p1tmp", "xb", xi)]),
                src=xb, src_atoms=[("p1tmp", "xb", xi)], gb=gb1, gb_atom=("p1tmp2", "gb"),
                xn=xnb[xi], xn_atoms=[("p1tmp", "xn", xi)], junk=junk1, junk_atoms=[("p1tmp2", "junk")],
                dst_fn=dst, dst_atoms=lambda k0, wb=wb: [hwin_atom(k0 + k, wb) for k in range(8)], scol=n_i))
        norm_phase(items)

        tr.tensor("gv", ZC, ZC + 36864)
        tr.tensor("vn", ZD, ZD + 18432)
        tr.tensor("lngb", ZE, ZE + 8192)
        gv = f32(ZC, QB * 1024).rearrange("p (b n) -> p b n", b=QB)
        vn = bf(ZD, QB * 1024).rearrange("p (b n) -> p b n", b=QB)
        lnG = f32(ZE, 1024)
        lnB = f32(ZE + 4096, 1024)
        tr.dma("sp", "lng", lambda e: e.dma_start(out=lnG, in_=lngb_d[0, :, :]), writes=[("lngb", 0)])
        tr.dma("sp", "lnb", lambda e: e.dma_start(out=lnB, in_=lngb_d[1, :, :]), writes=[("lngb", 1)])
        S_M2 = 144

        def ln_block(qb):
            gva = [("gv", qb, s) for s in range(4)]
            vm, vq, m2 = st[:, S_VM + qb:S_VM + qb + 1], st[:, S_VQ + qb:S_VQ + qb + 1], st[:, S_M2 + qb:S_M2 + qb + 1]
            dve(lambda e: e.tensor_reduce(out=vm, in_=st[:, S_V1 + qb * 4:S_V1 + qb * 4 + 4], axis=AX.X, op=ALU.add),
                [("st", S_V1 + qb * 4 + s) for s in range(4)], [("st", S_VM + qb)])
            dve(lambda e: e.tensor_scalar(out=vm, in0=vm, scalar1=-1.0 / 1024, scalar2=None, op0=ALU.mult),
                [("st", S_VM + qb)], [("st", S_VM + qb)])
            act(lambda e: e.activation(out=vn[:, qb, :], in_=gv[:, qb, :], func=AF.Square, accum_out=vq),
                gva, [("vn", qb), ("st", S_VQ + qb)])
            dve(lambda e: e.tensor_tensor(out=m2, in0=vm, in1=vm, op=ALU.mult), [("st", S_VM + qb)], [("st", S_M2 + qb)])
            dve(lambda e: e.tensor_scalar(out=vq, in0=vq, scalar1=1.0 / 1024, scalar2=m2, op0=ALU.mult, op1=ALU.subtract),
                [("st", S_VQ + qb), ("st", S_M2 + qb)], [("st", S_VQ + qb)])
            act(lambda e: e.activation(out=vq, in_=vq, func=AF.Sqrt, scale=1.0, bias=EPS), [("st", S_VQ + qb)], [("st", S_VQ + qb)])
            dve(lambda e: e.reciprocal(out=vq, in_=vq), [("st", S_VQ + qb)], [("st", S_VQ + qb)])
            dve(lambda e: e.scalar_tensor_tensor(out=gv[:, qb, :], in0=gv[:, qb, :], scalar=vm, in1=lnG, op0=ALU.add, op1=ALU.mult),
                gva + [("st", S_VM + qb), ("lngb", 0)], gva)
            dve(lambda e: e.scalar_tensor_tensor(out=vn[:, qb, :], in0=gv[:, qb, :], scalar=vq, in1=lnB, op0=ALU.mult, op1=ALU.add),
                gva + [("st", S_VQ + qb), ("lngb", 1)], [("vn", qb)])

        for s in range(4):
            r = load_w("v")
            wv = ring[:, r, :].rearrange("p (k n) -> p k n", k=KC)
            for qb in range(QB):
                b = nbank()
                pe([mm(ps[:, b, 0:256], hq[:, kc, qb * 128:(qb + 1) * 128], wv[:, kc, :], kc == 0, kc == KC - 1) for kc in range(KC)],
                   [("ring", r)] + [("hq", kc, qb) for kc in range(KC)], [("ps", b)])
                act(lambda e, b=b, qb=qb, s=s: e.activation(out=gv[:, qb, s * 256:(s + 1) * 256], in_=ps[:, b, 0:256], func=AF.Gelu,
                                                            accum_out=st[:, S_V1 + qb * 4 + s:S_V1 + qb * 4 + s + 1]),
                    [("ps", b)], [("gv", qb, s), ("st", S_V1 + qb * 4 + s)])
                if s == 3:
                    ln_block(qb)
        tr.tensor("oa", ZE, ZE + 18432)
        oa = bf(ZE, 8 * TQ).rearrange("p (g t) -> p g t", g=8)
        for s in range(4):
            r = load_w("u")
            wu = ring[:, r, :].rearrange("p (k n) -> p k n", k=KC)
            for gg in range(2):
                g = 2 * s + gg
                for ti, (t0, tn) in enumerate(Q_TILES):
                    b = nbank()
                    pe([mm(ps[:, b, 0:tn], wu[:, kc, gg * 128:(gg + 1) * 128], hq[:, kc, t0:t0 + tn], kc == 0, kc == KC - 1) for kc in range(KC)],
                       [("ring", r)] + [("hq", kc, qb) for kc in range(KC) for qb in range(t0 // 128, (t0 + tn) // 128)], [("ps", b)])
                    act(lambda e, b=b, g=g, t0=t0, tn=tn: e.activation(out=oa[:, g, t0:t0 + tn], in_=ps[:, b, 0:tn], func=AF.Gelu),
                        [("ps", b)], [("oa", g, qb) for qb in range(t0 // 128, (t0 + tn) // 128)])

        for qb in range(QB):
            for g0 in (0, 4):
                b = nbank()
                fns = []
                for g in range(g0, g0 + 4):
                    o = ps[:, b, (g - g0) * 128:(g - g0 + 1) * 128]
                    fns.append(mm(o, vn[:, qb, g * 128:(g + 1) * 128], wsb[:, g, :], True, False))
                    fns.append(mm(o, cbf[:, 3, :], bsh[:, 0, g * 128:(g + 1) * 128], False, False))
                    fns.append(mm(o, cbf[:, 3, :], bsh[:, 1, g * 128:(g + 1) * 128], False, True))
                pe(fns, [("vn", qb), ("wsb",), ("bsh", 0), ("bsh", 1), ("e0",)], [("ps", b)])
                dve(lambda e, b=b, g0=g0, qb=qb: e.tensor_tensor(out=oa[:, g0:g0 + 4, qb * 128:(qb + 1) * 128],
                                                                  in0=ps[:, b, :].rearrange("p (g t) -> p g t", g=4),
                                                                  in1=oa[:, g0:g0 + 4, qb * 128:(qb + 1) * 128], op=ALU.mult),
                    [("ps", b)] + [("oa", g, qb) for g in range(g0, g0 + 4)], [("oa", g, qb) for g in range(g0, g0 + 4)])

        tr.tensor("kT", ZC, ZC + 32768)
        tr.tensor("va", ZD, ZD + 33280)
        tr.tensor("rtm", ZC + 32768, ZC + 36864)
        tr.tensor("atmp", ZD + 33280, ZD + 36864)
        tr.tensor("fx", 10 ** 6, 10 ** 6 + 4096)
        kT = bf(ZC, H * TW).rearrange("p (h t) -> p h t", h=H)
        va = bf(ZD, NB * H * 130).rearrange("p (b h d) -> p b h d", b=NB, h=H)
        sqt = [bf(ZD + 33280 + i * 1024, 512) for i in range(2)]
        rtm = [f32(ZC + 32768 + i * 2048, 512) for i in range(2)]
        ptb = [bf(ZD + 35328 + i * 256, 128) for i in range(6)]
        W_TILES = [(0, 512), (512, 384)]

        qk_pend = [None]
        qk_cnt = [0]

        def qk_norm_tile(b, n, gcol, out_ap, out_atoms, cnt=None):
            i2 = qk_cnt[0] % 2
            qk_cnt[0] += 1
            act(lambda e: e.activation(out=sqt[i2][:, 0:n], in_=ps[:, b, 0:n], func=AF.Square), [("ps", b)], [("atmp", "sq", i2)])

            def tail():
                b2 = nbank()
                pe([mm(ps[:, b2, 0:n], ones_bf, sqt[i2][:, 0:n], True, True)], [("atmp", "sq", i2), ("ones_bf",)], [("ps", b2)])
                act(lambda e: e.activation(out=rtm[i2][:, 0:n], in_=ps[:, b2, 0:n], func=AF.Ln, scale=1.0 / 128, bias=EPS),
                    [("ps", b2)], [("rtm", i2)])
                act(lambda e: e.activation(out=rtm[i2][:, 0:n], in_=rtm[i2][:, 0:n], func=AF.Exp, scale=-0.5), [("rtm", i2)], [("rtm", i2)])
                dve(lambda e: e.scalar_tensor_tensor(out=out_ap, in0=ps[:, b, 0:n], scalar=gq2[:, gcol:gcol + 1], in1=rtm[i2][:, 0:n],
                                                     op0=ALU.mult, op1=ALU.mult),
                    [("ps", b), ("rtm", i2), ("gq2",)], out_atoms)

            if qk_pend[0] is not None:
                qk_pend[0]()
            qk_pend[0] = tail

        def qk_flush():
            if qk_pend[0] is not None:
                qk_pend[0]()
                qk_pend[0] = None

        cnt = 0
        for s in range(4):
            r = load_w("k")
            wk = ring[:, r, :].rearrange("p (k n) -> p k n", k=KC)
            for hh in range(2):
                h = 2 * s + hh
                tiles = [("c", t0, tn) for (t0, tn) in W_TILES] + [("q", t0, tn) for (t0, tn) in Q_TILES]
                for (src, t0, tn) in tiles:
                    b = nbank()
                    if src == "c":
                        rhs = lambda kc: hc[:, kc, t0:t0 + tn]
                        ratoms = [("hc", kc, wb) for kc in range(KC) for wb in range(t0 // 128, (t0 + tn) // 128)]
                        w0 = t0
                    else:
                        rhs = lambda kc: hq[:, kc, t0:t0 + tn]
                        ratoms = [("hq", kc, qb) for kc in range(KC) for qb in range(t0 // 128, (t0 + tn) // 128)]
                        w0 = 896 + t0
                    pe([mm(ps[:, b, 0:tn], wk[:, kc, hh * 128:(hh + 1) * 128], rhs(kc), kc == 0, kc == KC - 1) for kc in range(KC)],
                       [("ring", r)] + ratoms, [("ps", b)])
                    qk_norm_tile(b, tn, 1, kT[:, h, w0:w0 + tn], [("kT", h, wb) for wb in range(w0 // 128, (w0 + tn) // 128)], cnt)
                    cnt += 1
        qk_flush()
        dve(lambda e: e.memset(va[:, :, :, 128:130], 1.0), [], [("va", "ones")])
        for s in range(4):
            r = load_w("vv")
            wvv = ring[:, r, :].rearrange("p (k n) -> p k n", k=KC)
            for wb in range(NB):
                b = nbank()
                pe([mm(ps[:, b, 0:256], hwin(kc, wb), wvv[:, kc, :], kc == 0, kc == KC - 1) for kc in range(KC)],
                   [("ring", r)] + [hwin_atom(kc, wb) for kc in range(KC)], [("ps", b)])
                act(lambda e, b=b, wb=wb, s=s: e.activation(out=va[:, wb, 2 * s:2 * s + 2, 0:128],
                                                            in_=ps[:, b, 0:256].rearrange("p (h d) -> p h d", h=2), func=AF.Copy),
                    [("ps", b)], [("va", wb, 2 * s), ("va", wb, 2 * s + 1)])
        r = load_w("f")
        wf = ring[:, r, 0:128].rearrange("p (k n) -> p k n", k=KC)
        bF = nbank()
        fns = []
        for wb in range(NB):
            for kc in range(KC):
                fns.append(mm(ps[:, bF, wb * 8:(wb + 1) * 8], hwin(kc, wb), wf[:, kc, :], kc == 0, kc == KC - 1))
        pe(fns, [("ring", r)] + [hwin_atom(kc, wb) for kc in range(KC) for wb in range(NB)], [("ps", bF)])
        nl, cin, tot, mid, pre, cumN, refN = (fx[:, i, :] for i in range(7))
        dve(lambda e: e.tensor_tensor(out=nl, in0=ps[:, bF, 0:128], in1=cst[:, C_BF:C_BF + 128], op=ALU.add), [("ps", bF), ("cst",)], [("fx", 0)])
        act(lambda e: e.activation(out=nl, in_=nl, func=AF.Exp, scale=-1.0), [("fx", 0)], [("fx", 0)])
        act(lambda e: e.activation(out=nl, in_=nl, func=AF.Ln, bias=1.0), [("fx", 0)], [("fx", 0)])
        for i, lhs in ((1, cst[:, C_TRI:C_TRI + 128]), (2, c32), (3, cst[:, C_E63:C_E63 + 128])):
            b = nbank()
            pe([mm(ps[:, b, 0:128], lhs, nl, True, True)], [("fx", 0), ("cst",)], [("ps", b)])
            act(lambda e, b=b, i=i: e.activation(out=fx[:, i, :], in_=ps[:, b, 0:128], func=AF.Copy), [("ps", b)], [("fx", i)])
        dve(lambda e: e.memset(pre[:, 0:8], 0.0), [], [("fx", 4)])
        for wb in range(1, NB):
            dve(lambda e, wb=wb: e.tensor_tensor(out=pre[:, wb * 8:(wb + 1) * 8], in0=pre[:, (wb - 1) * 8:wb * 8], in1=tot[:, (wb - 1) * 8:wb * 8], op=ALU.add),
                [("fx", 4), ("fx", 2)], [("fx", 4)])
        dve(lambda e: e.tensor_tensor(out=cumN, in0=cin, in1=pre, op=ALU.add), [("fx", 1), ("fx", 4)], [("fx", 5)])
        dve(lambda e: e.tensor_tensor(out=refN, in0=mid, in1=pre, op=ALU.add), [("fx", 3), ("fx", 4)], [("fx", 6)])
        WBMID = [7, 10, 14]
        cum3 = cumN.rearrange("p (i h) -> p i h", h=H)
        for T in range(3):
            dve(lambda e, T=T: e.tensor_tensor(out=biasK[:, :, :, T], in0=cum3,
                                               in1=refN[:, WBMID[T] * 8:(WBMID[T] + 1) * 8].unsqueeze(1).to_broadcast([P, NB, H]), op=ALU.subtract),
                [("fx", 5), ("fx", 6)], [("biasK", T)])
            if T >= 1:
                dve(lambda e, T=T: e.tensor_scalar(out=biasK[:, 0:8, :, T], in0=biasK[:, 0:8, :, T], scalar1=cst[:, C_FL:C_FL + 1], scalar2=None, op0=ALU.add),
                    [("biasK", T), ("cst",)], [("biasK", T)])

        tr.tensor("qT", ZB, ZB + 18432)
        qT = bf(ZB, H * TQ).rearrange("p (h t) -> p h t", h=H)
        for s in range(4):
            r = load_w("q")
            wq = ring[:, r, :].rearrange("p (k n) -> p k n", k=KC)
            for hh in range(2):
                h = 2 * s + hh
                for (t0, tn) in Q_TILES:
                    b = nbank()
                    pe([mm(ps[:, b, 0:tn], wq[:, kc, hh * 128:(hh + 1) * 128], hq[:, kc, t0:t0 + tn], kc == 0, kc == KC - 1) for kc in range(KC)],
                       [("ring", r)] + [("hq", kc, qb) for kc in range(KC) for qb in range(t0 // 128, (t0 + tn) // 128)], [("ps", b)])
                    qk_norm_tile(b, tn, 0, qT[:, h, t0:t0 + tn], [("qT", h, qb) for qb in range(t0 // 128, (t0 + tn) // 128)], cnt)
                    cnt += 1

        qk_flush()
        tr.tensor("obT", ZB + 18432, ZB + 36864)
        tr.tensor("crow", ZD + 33280, ZD + 35584)
        tr.tensor("obh", ZD + 35584, ZD + 36608)
        tr.tensor("pt", ZC + 32768, ZC + 35840)
        obT = bf(ZB + 18432, H * TQ).rearrange("p (h t) -> p h t", h=H)
        crow = bf(ZD + 33280, TQ)
        obh = [bf(ZD + 35584 + i * 256, 128) for i in range(4)]
        ptb = [bf(ZC + 32768 + i * 1024, 512) for i in range(3)]
        TILES = [(0, 1), (1, 5), (5, 9)]
        idf = cst[:, C_ID:C_ID + 128]

        def tp32(out, in_):
            return lambda e: e.transpose(out, in_, idf)

        dve(lambda e: e.memset(crow, 0.0), [], [("crow",)])
        ba, bb, bc = nbank(), nbank(), nbank()
        pe([tp32(ps[0:8, ba, k * 128:(k + 1) * 128], cumN[:, (7 + k) * 8:(8 + k) * 8]) for k in range(4)], [("fx", 5), ("cst",)], [("ps", ba)])
        pe([tp32(ps[0:8, bb, k * 128:(k + 1) * 128], cumN[:, (11 + k) * 8:(12 + k) * 8]) for k in range(4)], [("fx", 5), ("cst",)], [("ps", bb)])
        pe([tp32(ps[0:8, bc, 0:128], cumN[:, 15 * 8:16 * 8])]
           + [tp32(ps[0:8, bc, (1 + T) * 128:(2 + T) * 128], refN[:, WBMID[T] * 8:(WBMID[T] + 1) * 8]) for T in range(3)],
           [("fx", 5), ("fx", 6), ("cst",)], [("ps", bc)])
        dve(lambda e: e.tensor_copy(out=rsb[0:8, 0:3], in_=ps[0:8, bc, 128:512].rearrange("p (t n) -> p t n", t=3)[:, :, 0]), [("ps", bc)], [("rsb",)])
        for qb in range(QB):
            bank, k = (ba, qb) if qb < 4 else ((bb, qb - 4) if qb < 8 else (bc, 0))
            T = 0 if qb == 0 else (1 if qb < 5 else 2)
            dve(lambda e, bank=bank, k=k, T=T, qb=qb: e.tensor_scalar(out=crow[0:8, qb * 128:(qb + 1) * 128], in0=ps[0:8, bank, k * 128:(k + 1) * 128],
                                                                       scalar1=-1.0, scalar2=rsb[0:8, T:T + 1], op0=ALU.mult, op1=ALU.add),
                [("ps", bank), ("rsb",), ("crow",)], [("crow",)])

        units = []
        for T, (q0, q1) in enumerate(TILES):
            for h in range(H):
                for i in range(7 + q1):
                    units.append((T, h, i, max(q0, i - 7), q1, q0))
        sbank = {}
        s_rr = [0]
        p_rr = [0]
        o_rr = [0]
        t_rr = [0]
        S_BANKS = (0, 1, 7)
        pbt6 = ps[:, 6, :].bitcast(BF16)
        for k in range(8):
            tr.merge(("pst", k), [("ps", 6)])

        def emit_S(u):
            T, h, i, qlo, q1, q0 = u
            n = (q1 - qlo) * 128
            b = S_BANKS[s_rr[0] % 3]
            s_rr[0] += 1
            sbank[u] = b
            diag = (i - 7) >= q0
            fns = [mm(ps[:, b, 0:n], kT[:, h, i * 128:(i + 1) * 128], qT[:, h, qlo * 128:q1 * 128], True, False),
                   mm(ps[:, b, 0:n], sel[:, h, :], crow[:, qlo * 128:q1 * 128], False, not diag)]
            if diag:
                fns.append(mm(ps[:, b, 0:128], ident, tri_bf, False, True))
            pe(fns, [("kT", h, i), ("crow",), ("sel",), ("ident",), ("tri_bf",)] + [("qT", h, qb) for qb in range(qlo, q1)], [("ps", b)])

        def emit_PV(u):
            T, h, i, qlo, q1, q0 = u
            n = (q1 - qlo) * 128
            b = sbank[u]
            pi = p_rr[0] % 3
            p_rr[0] += 1
            act(lambda e: e.activation(out=ptb[pi][:, 0:n], in_=ps[:, b, 0:n], func=AF.Exp, bias=biasK[:, i, h, T:T + 1]),
                [("ps", b), ("biasK", T)], [("pt", pi)])
            for qb in range(qlo, q1):
                j = 7 + qb
                bo = 2 + (qb - q0)
                pe([mm(ps[:, bo, 0:129], ptb[pi][:, (qb - qlo) * 128:(qb - qlo + 1) * 128], va[:, i, h, 0:129], i == 0, i == j)],
                   [("pt", pi), ("va", i, h), ("va", "ones")], [("ps", bo)])
                if i == j:
                    rc = S_REC + o_rr[0] % 8
                    oi = o_rr[0] % 4
                    o_rr[0] += 1
                    dve(lambda e, bo=bo, rc=rc: e.reciprocal(out=st[:, rc:rc + 1], in_=ps[:, bo, 128:129]), [("ps", bo)], [("st", rc)])
                    dve(lambda e, bo=bo, rc=rc, oi=oi: e.tensor_scalar(out=obh[oi], in0=ps[:, bo, 0:128], scalar1=st[:, rc:rc + 1], scalar2=None, op0=ALU.mult),
                        [("ps", bo), ("st", rc)], [("obh", oi)])
                    k = t_rr[0] % 8
                    t_rr[0] += 1

                    def fin(k=k, oi=oi, h=h, qb=qb):
                        pe([tp(pbt6[:, k * 128:(k + 1) * 128], obh[oi])], [("obh", oi), ("ident",)], [("pst", k)])
                        dve(lambda e: e.tensor_copy(out=obT[:, h, qb * 128:(qb + 1) * 128], in_=pbt6[:, k * 128:(k + 1) * 128]),
                            [("pst", k)], [("obT", h, qb)])
                    fin_pend.append(fin)

        fin_pend = []
        emit_S(units[0])
        for ui, u in enumerate(units):
            if ui + 1 < len(units):
                emit_S(units[ui + 1])
            ready = list(fin_pend)
            del fin_pend[:]
            emit_PV(u)
            for f_ in ready:
                f_()
        for f_ in fin_pend:
            f_()
        tr.merge(("ps", 6), [("pst", k) for k in range(8)])

        tr.tensor("yT", ZC, ZC + 36864)
        tr.tensor("gtmp", ZB, ZB + 8192)
        yT = bf(ZC, KC * TQ).rearrange("p (k t) -> p k t", k=KC)
        sgt = [f32(ZB + i * 2048, 512) for i in range(4)]
        gcnt = 0
        for c in range(16):
            ra = load_w("ga")
            rb = load_w("gb")
            wa = ring[:, ra, 0:3072].rearrange("p (k n) -> p k n", k=24)
            wb_ = ring[:, rb, 0:3072].rearrange("p (k n) -> p k n", k=24)
            for (t0, tn) in ((126, 342), (468, 342), (810, 342)):
                qbs = range(t0 // 128, (t0 + tn - 1) // 128 + 1)
                hqa = [("hq", kc, qb) for kc in range(KC) for qb in qbs]
                b1, b2, b3, b4 = nbank(), nbank(), nbank(), nbank()
                pe([mm(ps[:, b1, 0:tn], wa[:, kc, :], hq[:, kc, t0:t0 + tn], kc == 0, kc == KC - 1) for kc in range(KC)],
                   [("ring", ra)] + hqa, [("ps", b1)])
                pe([mm(ps[:, b2, 0:tn], wa[:, 16 + kc, :], oa[:, kc, t0:t0 + tn], kc == 0, kc == 7) for kc in range(8)],
                   [("ring", ra)] + [("oa", g, qb) for g in range(8) for qb in qbs], [("ps", b2)])
                pe([mm(ps[:, b3, 0:tn], wb_[:, kc, :], hq[:, kc, t0:t0 + tn], kc == 0, kc == KC - 1) for kc in range(KC)],
                   [("ring", rb)] + hqa, [("ps", b3)])
                pe([mm(ps[:, b4, 0:tn], wb_[:, 16 + kc, :], obT[:, kc, t0:t0 + tn], kc == 0, kc == 7) for kc in range(8)],
                   [("ring", rb)] + [("obT", hh, qb) for hh in range(8) for qb in qbs], [("ps", b4)])
                s1, s2 = sgt[(gcnt % 2) * 2], sgt[(gcnt % 2) * 2 + 1]
                a1, a2 = ("gtmp", (gcnt % 2) * 2), ("gtmp", (gcnt % 2) * 2 + 1)
                gcnt += 1
                act(lambda e, b1=b1, s1=s1, tn=tn: e.activation(out=s1[:, 0:tn], in_=ps[:, b1, 0:tn], func=AF.Sigmoid), [("ps", b1)], [a1])
                act(lambda e, b3=b3, s2=s2, tn=tn: e.activation(out=s2[:, 0:tn], in_=ps[:, b3, 0:tn], func=AF.Sigmoid), [("ps", b3)], [a2])
                dve(lambda e, b2=b2, s1=s1, tn=tn: e.tensor_tensor(out=s1[:, 0:tn], in0=ps[:, b2, 0:tn], in1=s1[:, 0:tn], op=ALU.mult), [("ps", b2), a1], [a1])
                dve(lambda e, b4=b4, s2=s2, tn=tn: e.tensor_tensor(out=s2[:, 0:tn], in0=ps[:, b4, 0:tn], in1=s2[:, 0:tn], op=ALU.mult), [("ps", b4), a2], [a2])
                dve(lambda e, s1=s1, s2=s2, c=c, t0=t0, tn=tn: e.tensor_tensor(out=yT[:, c, t0:t0 + tn], in0=s1[:, 0:tn], in1=s2[:, 0:tn], op=ALU.add),
                    [a1, a2], [("yT", c, qb) for qb in qbs])

        tr.tensor("x1", ZA, ZA + 65536)
        tr.tensor("xh", ZA + 65536, ZA + 73728)
        tr.tensor("xp", ZE, ZE + 8192)
        x1 = f32(ZA, 8 * D).rearrange("p (b n) -> p b n", b=8)
        xh = f32(ZA + 65536, D)
        xp = [f32(ZE + i * 2048, 512) for i in range(4)]
        xc = 0
        for cg in range(4):
            r0 = load_w("wo")
            r1 = load_w("wo")
            wo = [ring[:, r0, :].rearrange("p (k n) -> p k n", k=8), ring[:, r1, :].rearrange("p (k n) -> p k n", k=8)]
            for qb in range(QB):
                xi = xc % 4
                xc += 1
                tr.dma("sp", "xp%d" % xi,
                       lambda e, xi=xi, qb=qb, cg=cg: e.dma_start(out=xp[xi], in_=xw[(7 + qb) * 128:(8 + qb) * 128, cg * 512:(cg + 1) * 512]),
                       writes=[("xp", xi)])
                b = nbank()
                pe([mm(ps[:, b, :], yT[:, kc, qb * 128:(qb + 1) * 128], wo[kc // 8][:, kc % 8, :], kc == 0, kc == KC - 1) for kc in range(KC)],
                   [("ring", r0), ("ring", r1)] + [("yT", kc, qb) for kc in range(KC)], [("ps", b)])
                if qb == 0:
                    o, oat = xh[:, cg * 512:(cg + 1) * 512], ("xh", cg)
                else:
                    o, oat = x1[:, qb - 1, cg * 512:(cg + 1) * 512], ("x1", qb - 1, cg)
                dve(lambda e, b=b, o=o, xi=xi: e.tensor_tensor(out=o, in0=ps[:, b, :], in1=xp[xi], op=ALU.add), [("ps", b), ("xp", xi)], [oat])

        tr.tensor("h2", ZD, ZD + 36864)
        tr.tensor("ctmp", ZC, ZC + 36864)
        h2 = bf(ZD, KC * TQ).rearrange("p (k t) -> p k t", k=KC)
        xn2 = [bf(ZC, D), bf(ZC + 4096, D), bf(ZC + 24608 + 4096, D)]
        xn2_atoms = [[("ctmp", "xn", 0)], [("ctmp", "xn", 1)], [("ctmp", "tA", 1)]]
        junk2 = bf(ZC + 24608, D)
        gb2 = f32(ZC + 8192, D)
        tr.dma("sp", "gb2", lambda e: e.dma_start(out=gb2, in_=gbc_d[1, :, :]), writes=[("ctmp", "gb")])
        items = []
        for blk in range(QB):
            src = xh if blk == 0 else x1[:, blk - 1, :]
            satoms = [("xh", cg) for cg in range(4)] if blk == 0 else [("x1", blk - 1, cg) for cg in range(4)]
            items.append(dict(src=src, src_atoms=satoms, gb=gb2, gb_atom=("ctmp", "gb"), xn=xn2[blk % 3], xn_atoms=xn2_atoms[blk % 3],
                              junk=junk2, junk_atoms=[("ctmp", "tA", 0)],
                              dst_fn=lambda k0, blk=blk: h2[:, k0:k0 + 8, blk * 128:(blk + 1) * 128],
                              dst_atoms=lambda k0, blk=blk: [("h2", k0 + k, blk) for k in range(8)], scol=blk))
        norm_phase(items)

        a_sb = [f32(ZC + 16384 + i * 4112, 1028) for i in range(2)]
        tA = [f32(ZC + 24608 + i * 4096, 1024) for i in range(2)]
        tr.tensor("hid", ZE + 8192, ZE + 16384)
        hid = [bf(ZE + 8192 + i * 4096, 2048).rearrange("p (c t) -> p c t", c=2) for i in range(2)]
        tB = [f32(ZE + i * 4096, 1024) for i in range(2)]
        tr.tensor("tB", ZE, ZE + 8192)
        cw = cst[:, C_CW:C_CW + 132].rearrange("p (c j) -> p c j", j=3)
        cb = cst[:, C_CB:C_CB + 44]
        fcnt = [0]

        BA0, BA1, BAH, BB0, BB1 = 0, 1, 2, 3, 4
        d_rr = [0]

        def U_chunk_steps(g, cc):
            c = 2 * g + cc
            hb = hid[g % 2]
            i2 = c % 2
            asb, ta, tb_ = a_sb[i2], tA[i2], tB[i2]
            aat, tat, tbt = ("ctmp", "a", i2), ("ctmp", "tA", i2), ("tB", i2)
            box = {}

            def agroup(b, t0, tn):
                w = box["w"]
                pe([mm(ps[:, b, 0:tn], w[:, kc, :], h2[:, kc, t0:t0 + tn], kc == 0, kc == KC - 1) for kc in range(KC)],
                   [("ring", box["r"])] + [("h2", kc, blk) for kc in range(KC) for blk in range(t0 // 128, (t0 + tn - 1) // 128 + 1)], [("ps", b)])

            def bgroup(b, t0, tn):
                w = box["w"]
                pe([mm(ps[:, b, 0:tn], w[:, 16 + kc, :], h2[:, kc, t0:t0 + tn], kc == 0, kc == KC - 1) for kc in range(KC)],
                   [("ring", box["r"])] + [("h2", kc, blk) for kc in range(KC) for blk in range(t0 // 128, (t0 + tn) // 128)], [("ps", b)])

            def s0():
                box["r"] = load_w("uc")
                box["w"] = ring[:, box["r"], :].rearrange("p (k n) -> p k n", k=32)
                agroup(BA0, 126, 342)
                agroup(BA1, 468, 342)
                act(lambda e: e.activation(out=asb[:, 0:342], in_=ps[:, BA0, 0:342], func=AF.Copy), [("ps", BA0)], [aat])
                act(lambda e: e.activation(out=asb[:, 0:2], in_=asb[:, 0:2], func=AF.Copy, scale=cst[:, C_FL + 1:C_FL + 2]), [aat, ("cst",)], [aat])
                act(lambda e: e.activation(out=asb[:, 342:684], in_=ps[:, BA1, 0:342], func=AF.Copy), [("ps", BA1)], [aat])

            def s1():
                agroup(BAH, 810, 342)
                act(lambda e: e.activation(out=asb[:, 684:1026], in_=ps[:, BAH, 0:342], func=AF.Copy), [("ps", BAH)], [aat])

            def s1post():
                dve(lambda e: e.tensor_scalar(out=ta, in0=asb[:, 2:1026], scalar1=cw[:, c, 2:3], scalar2=cb[:, c:c + 1], op0=ALU.mult, op1=ALU.add),
                    [aat, ("cst",)], [tat])
                dve(lambda e: e.scalar_tensor_tensor(out=tb_, in0=asb[:, 1:1025], scalar=cw[:, c, 1:2], in1=ta, op0=ALU.mult, op1=ALU.add),
                    [aat, tat, ("cst",)], [tbt])
                dve(lambda e: e.scalar_tensor_tensor(out=ta, in0=asb[:, 0:1024], scalar=cw[:, c, 0:1], in1=tb_, op0=ALU.mult, op1=ALU.add),
                    [aat, tbt, ("cst",)], [tat])
                act(lambda e: e.activation(out=tb_, in_=ta, func=AF.Gelu_apprx_tanh), [tat], [tbt])

            def s2():
                bgroup(BB0, 128, 512)

            def s2post():
                dve(lambda e: e.tensor_tensor(out=hb[:, cc, 0:512], in0=ps[:, BB0, :], in1=tb_[:, 0:512], op=ALU.mult),
                    [("ps", BB0), tbt], [("hid", g % 2, cc, 0)])

            def s3():
                bgroup(BB1, 640, 512)

            def s3post():
                dve(lambda e: e.tensor_tensor(out=hb[:, cc, 512:1024], in0=ps[:, BB1, :], in1=tb_[:, 512:1024], op=ALU.mult),
                    [("ps", BB1), tbt], [("hid", g % 2, cc, 1)])

            return [(s0, None), (s1, s1post), (s2, s2post), (s3, s3post)]

        def D_steps(g):
            box = {}
            hb = hid[g % 2]

            def piece(tb, cg):
                def f():
                    if "r" not in box:
                        box["r"] = load_w("dn")
                        box["w"] = ring[:, box["r"], :].rearrange("p (c n) -> p c n", c=2)
                    wd = box["w"]
                    b = 5 + d_rr[0] % 3
                    d_rr[0] += 1
                    pe([mm(ps[:, b, :], hb[:, cc, tb * 128:(tb + 1) * 128], wd[:, cc, cg * 512:(cg + 1) * 512], cc == 0, cc == 1) for cc in range(2)],
                       [("ring", box["r"])] + [("hid", g % 2, cc, tb // 4) for cc in range(2)], [("ps", b)])
                    dve(lambda e: e.tensor_tensor(out=x1[:, tb, cg * 512:(cg + 1) * 512], in0=ps[:, b, :],
                                                  in1=x1[:, tb, cg * 512:(cg + 1) * 512], op=ALU.add),
                        [("ps", b), ("x1", tb, cg)], [("x1", tb, cg)])
                return f
            return [piece(tb, cg) for tb in range(8) for cg in range(4)]

        def p12_prelude():
            tr.tensor("h3", ZD, ZD + 32768)
            tr.tensor("c12", ZC, ZC + 36864)
            tr.tensor("pp", ZE, ZE + 8192)
            tr.dma("sp", "gb3", lambda e: e.dma_start(out=gb3, in_=gbc_d[2, :, :]), writes=[("c12", "gb")])
            tr.dma("pool", "pp", lambda e: e.dma_start(out=bf(ZE, 2 * D), in_=pproj_d[:, :]), writes=[("pp",)])
            tr.dma("pool", "pT", lambda e: e.dma_start(out=bf(ZC + 16384, 2 * TO), in_=pw[:, :]), writes=[("c12", "pT", tb) for tb in range(8)])

        h3 = bf(ZD, KC * TO).rearrange("p (k t) -> p k t", k=KC)
        xn3 = [bf(ZC, D), bf(ZC + 4096, D), bf(ZC + 27648, D)]
        xn3_atoms = [[("c12", "xn", 0)], [("c12", "xn", 1)], [("c12", "te", 0), ("c12", "te", 1)]]
        junk3 = bf(ZC + 23552, D)
        gb3 = f32(ZC + 8192, D)
        pT = bf(ZC + 16384, 2 * TO).rearrange("p (k t) -> p k t", k=2)
        pf = [f32(ZC + 20480 + i * 1024, 256) for i in range(2)]
        pb_ = [bf(ZC + 22528 + i * 512, 256) for i in range(2)]
        sg = [f32(ZC + 23552 + i * 2048, 512) for i in range(2)]
        te = [f32(ZC + 27648 + i * 2048, 512) for i in range(2)]
        gpl = [f32(ZC + 31744 + i * 2048, 512) for i in range(2)]
        ppj = bf(ZE, 2 * D).rearrange("p (k n) -> p k n", k=2)

        for g in range(NG + 1):
            usteps = (U_chunk_steps(g, 0) + U_chunk_steps(g, 1)) if g < NG else []
            dsteps = D_steps(g - 1) if g >= 1 else []
            if g == NG:
                p12_prelude()
            if usteps:
                per = len(dsteps) // len(usteps)
                for i, (u, post) in enumerate(usteps):
                    u()
                    for d in dsteps[i * per:(i + 1) * per]:
                        d()
                    if post is not None:
                        post()
            else:
                for d in dsteps:
                    d()

        for tb in range(8):
            for cg in range(4):
                b = nbank()
                pe([mm(ps[:, b, :], pT[:, k, tb * 128:(tb + 1) * 128], ppj[:, k, cg * 512:(cg + 1) * 512], k == 0, k == 1) for k in range(2)],
                   [("c12", "pT", tb), ("pp",)], [("ps", b)])
                i2 = (tb * 4 + cg) % 2
                act(lambda e, b=b, tb=tb, cg=cg, i2=i2: e.activation(out=sg[i2], in_=ps[:, b, :], func=AF.Square,
                                                                      accum_out=st[:, S_ES + tb * 4 + cg:S_ES + tb * 4 + cg + 1]),
                    [("ps", b)], [("c12", "sg", i2), ("st", S_ES + tb * 4 + cg)])
            dve(lambda e, tb=tb: e.tensor_reduce(out=st[:, S_ER + tb:S_ER + tb + 1], in_=st[:, S_ES + tb * 4:S_ES + tb * 4 + 4], axis=AX.X, op=ALU.add),
                [("st", S_ES + tb * 4 + cg) for cg in range(4)], [("st", S_ER + tb)])
            act(lambda e, tb=tb: e.activation(out=st[:, S_ER + tb:S_ER + tb + 1], in_=st[:, S_ER + tb:S_ER + tb + 1], func=AF.Sqrt, scale=1.0 / D, bias=EPS),
                [("st", S_ER + tb)], [("st", S_ER + tb)])
            dve(lambda e, tb=tb: e.reciprocal(out=st[:, S_ER + tb:S_ER + tb + 1], in_=st[:, S_ER + tb:S_ER + tb + 1]), [("st", S_ER + tb)], [("st", S_ER + tb)])
        items = []
        for tb in range(8):
            items.append(dict(src=x1[:, tb, :], src_atoms=[("x1", tb, cg) for cg in range(4)], gb=gb3, gb_atom=("c12", "gb"),
                              xn=xn3[tb % 3], xn_atoms=xn3_atoms[tb % 3], junk=junk3, junk_atoms=[("c12", "sg", 0), ("c12", "sg", 1)],
                              dst_fn=lambda k0, tb=tb: h3[:, k0:k0 + 8, tb * 128:(tb + 1) * 128],
                              dst_atoms=lambda k0, tb=tb: [("h3", k0 + k, tb) for k in range(8)], scol=tb))
        norm_phase(items)
        for cg in range(4):
            r0 = load_w("wg")
            r1 = load_w("wg")
            wg = [ring[:, r0, :].rearrange("p (k n) -> p k n", k=8), ring[:, r1, :].rearrange("p (k n) -> p k n", k=8)]
            gi = cg % 2
            tr.dma("sp", "gpl%d" % gi, lambda e, gi=gi, cg=cg: e.dma_start(out=gpl[gi], in_=gbc_d[3, :, cg * 512:(cg + 1) * 512]), writes=[("c12", "gpl", gi)])
            for tb in range(8):
                b1, b2 = nbank(), nbank()
                pe([mm(ps[:, b1, :], h3[:, kc, tb * 128:(tb + 1) * 128], wg[kc // 8][:, kc % 8, :], kc == 0, kc == KC - 1) for kc in range(KC)],
                   [("ring", r0), ("ring", r1)] + [("h3", kc, tb) for kc in range(KC)], [("ps", b1)])
                pe([mm(ps[:, b2, :], pT[:, k, tb * 128:(tb + 1) * 128], ppj[:, k, cg * 512:(cg + 1) * 512], k == 0, k == 1) for k in range(2)],
                   [("c12", "pT", tb), ("pp",)], [("ps", b2)])
                i2 = tb % 2
                act(lambda e, b1=b1, i2=i2: e.activation(out=sg[i2], in_=ps[:, b1, :], func=AF.Sigmoid), [("ps", b1)], [("c12", "sg", i2)])
                dve(lambda e, b2=b2, i2=i2, tb=tb, gi=gi: e.scalar_tensor_tensor(out=te[i2], in0=ps[:, b2, :], scalar=st[:, S_ER + tb:S_ER + tb + 1], in1=gpl[gi],
                                                                                 op0=ALU.mult, op1=ALU.mult),
                    [("ps", b2), ("st", S_ER + tb), ("c12", "gpl", gi)], [("c12", "te", i2)])
                dve(lambda e, i2=i2: e.tensor_tensor(out=te[i2], in0=te[i2], in1=sg[i2], op=ALU.mult), [("c12", "te", i2), ("c12", "sg", i2)], [("c12", "te", i2)])
                dve(lambda e, i2=i2, tb=tb, cg=cg: e.tensor_tensor(out=x1[:, tb, cg * 512:(cg + 1) * 512], in0=x1[:, tb, cg * 512:(cg + 1) * 512], in1=te[i2], op=ALU.add),
                    [("c12", "te", i2), ("x1", tb, cg)], [("x1", tb, cg)])
                if cg == 3:
                    tr.dma("sp", "out%d" % tb, lambda e, tb=tb: e.dma_start(out=y[tb * 128:(tb + 1) * 128, :], in_=x1[:, tb, :]),
                           reads=[("x1", tb, c4) for c4 in range(4)], writes=[("yout", tb)])
        tr.wait_all("sp", [("yout", tb) for tb in range(8)])
        assert ring_pos[0] == NL

        with nc.Block() as block:
            @block.sync
            def _(e):
                tr.replay("sp", e)

            @block.gpsimd
            def _(e):
                tr.replay("pool", e)

            @block.tensor
            def _(e):
                tr.replay("pe", e)

            @block.scalar
            def _(e):
                tr.replay("act", e)

            @block.vector
            def _(e):
                tr.replay("dve", e)
    return nc


_CACHE = {}


def kernel(x, p, norm_mix_g, w_in, gmlp_ln_g, gmlp_ln_b, gmlp_w_s, gmlp_b_s, fox_b_f,
           q_norm_g, k_norm_g, w_branch_a, w_branch_b, w_out, norm_ffn_g, w_up,
           conv_w, conv_b, w_down, ple_proj, ple_norm_g, ple_gate_norm_g, w_ple_gate):
    f = lambda a: np.ascontiguousarray(np.asarray(a, dtype=np.float32))
    x, p = f(x), f(p)
    w_in, w_branch_a, w_branch_b, w_out = f(w_in)[0], f(w_branch_a)[0], f(w_branch_b)[0], f(w_out)[0]
    w_up, w_down, w_ple_gate, ple_proj = f(w_up)[0], f(w_down)[0], f(w_ple_gate)[0], f(ple_proj)[0]

    ws = _build_wstream(w_in, w_branch_a, w_branch_b, w_out, w_up, w_down, w_ple_gate)
    pproj = np.ascontiguousarray(ple_proj.reshape(2, P, D).transpose(1, 0, 2).reshape(P, 2 * D))
    rep = lambda v: np.ascontiguousarray(np.broadcast_to(np.asarray(v, np.float32).reshape(1, -1), (P, np.asarray(v).size)))
    gbc = np.stack([rep(f(norm_mix_g)[0]), rep(f(norm_ffn_g)[0]), rep(f(ple_gate_norm_g)[0]), rep(f(ple_norm_g)[0])])
    lngb = np.stack([rep(f(gmlp_ln_g)[0]), rep(f(gmlp_ln_b)[0])])
    wsT = np.ascontiguousarray(f(gmlp_w_s)[0].transpose(2, 0, 1).reshape(P, 1024))
    cst = np.zeros((P, CST_W), np.float32)
    ii = np.arange(P)
    cst[:, C_TRI:C_TRI + 128] = (ii[None, :] >= ii[:, None]).astype(np.float32)
    cst[:, C_E63:C_E63 + 128] = (ii[:, None] <= 63).astype(np.float32)
    cst[:, C_ID:C_ID + 128] = np.eye(P, dtype=np.float32)
    cst[:, C_ONE:C_ONE + 128] = 1.0
    bsrep = rep(f(gmlp_b_s)[0].reshape(-1))
    cst[:, C_BF:C_BF + 128] = rep(np.tile(f(fox_b_f)[0], NB))
    cst[:, C_GQ] = f(q_norm_g)[0]
    cst[:, C_GK] = f(k_norm_g)[0]
    cst[:, C_CW:C_CW + 132] = f(conv_w)[0].reshape(3, NFC, P).transpose(2, 1, 0).reshape(P, 132)
    cst[:, C_CB:C_CB + 44] = f(conv_b)[0].reshape(NFC, P).T

    in_maps = []
    for c in range(NCORES):
        b, hf = c // 2, c % 2
        if hf == 1:
            xwc = x[b]
        else:
            xwc = np.concatenate([x[b, :1024], x[b, :1024]], axis=0)
        cc = cst.copy()
        cc[:, C_FL] = 0.0 if hf == 1 else NEG
        cc[:, C_FL + 1] = 1.0 if hf == 1 else 0.0
        in_maps.append({
            "xw": np.ascontiguousarray(xwc), "pw": np.ascontiguousarray(p[0, b, hf * 1024:(hf + 1) * 1024].T.reshape(2, P, TO).transpose(1, 0, 2).reshape(P, 2 * TO)),
            "wstream": ws, "pproj": pproj, "gbc": gbc, "lngb": lngb, "wsT": wsT, "cst": cc, "bsrep": bsrep,
        })
    if "nc" not in _CACHE:
        _CACHE["nc"] = build_program()
    res = run_bass_kernel_spmd(_CACHE["nc"], in_maps, core_ids=list(range(NCORES)))
    out = np.empty((4, 2048, D), np.float32)
    for c in range(NCORES):
        b, hf = c // 2, c % 2
        out[b, hf * 1024:(hf + 1) * 1024] = res.results[c]["y"]
    return out
```

```python
import numpy as np
from contextlib import ExitStack

import concourse.bass as bass
import concourse.mybir as mybir
from concourse.bass_utils import run_bass_kernel_spmd

F32 = mybir.dt.float32
BF16 = mybir.dt.bfloat16
AF = mybir.ActivationFunctionType
ALU = mybir.AluOpType
AX = mybir.AxisListType

NCORES = 8
P = 128
D = 2048
KC = 16
TW = 2048
NB = 16
QB = 9
TQ = QB * P
TO = 1024
DFF = 5632
NFC = 44
NG = 22
H = 8
EPS = 1e-6
SLOT = 4096
NS = 4
NEG = -30000.0

C_TRI = 0
C_E63 = 128
C_ID = 256
C_BF = 384
C_GQ = 512
C_GK = 513
C_CW = 514
C_CB = 646
C_FL = 690
C_ONE = 692
CST_W = 820

Q_TILES = [(0, 512), (512, 512), (1024, 128)]


def _weight_stream_plan():
    plan = []
    for s in range(4):
        plan.append(("v", s))
    for s in range(4):
        plan.append(("u", s))
    for s in range(4):
        plan.append(("k", s))
    for s in range(4):
        plan.append(("vv", s))
    plan.append(("f", 0))
    for s in range(4):
        plan.append(("q", s))
    for c in range(16):
        plan.append(("ga", c))
        plan.append(("gb", c))
    for cg in range(4):
        plan.append(("wo", 2 * cg))
        plan.append(("wo", 2 * cg + 1))
    plan.append(("uc", 0))
    plan.append(("uc", 1))
    for g in range(1, NG):
        plan.append(("uc", 2 * g))
        plan.append(("dn", g - 1))
        plan.append(("uc", 2 * g + 1))
    plan.append(("dn", NG - 1))
    for cg in range(4):
        plan.append(("wg", 2 * cg))
        plan.append(("wg", 2 * cg + 1))
    return plan


PLAN = _weight_stream_plan()
NL = len(PLAN)


def _slot_elems(kind):
    if kind == "f":
        return 16 * 8
    if kind in ("ga", "gb"):
        return 24 * 128
    return 4096


def _build_wstream(w_in, w_branch_a, w_branch_b, w_out, w_up, w_down, w_ple_gate):
    ws = np.zeros((NL, P, SLOT), dtype=np.float32)

    def kcols(w, c0, n):
        K = w.shape[0]
        return w[:, c0:c0 + n].reshape(K // P, P, n).transpose(1, 0, 2)

    for i, (kind, j) in enumerate(PLAN):
        if kind == "v":
            t = kcols(w_in, 1024 + 256 * j, 256)
        elif kind == "u":
            t = kcols(w_in, 256 * j, 256)
        elif kind == "k":
            t = kcols(w_in, 3072 + 256 * j, 256)
        elif kind == "vv":
            t = kcols(w_in, 4096 + 256 * j, 256)
        elif kind == "f":
            t = kcols(w_in, 5120, 8)
        elif kind == "q":
            t = kcols(w_in, 2048 + 256 * j, 256)
        elif kind == "ga":
            t = np.concatenate([kcols(w_in, 5128 + 128 * j, 128), kcols(w_branch_a, 128 * j, 128)], axis=1)
        elif kind == "gb":
            t = np.concatenate([kcols(w_in, 7176 + 128 * j, 128), kcols(w_branch_b, 128 * j, 128)], axis=1)
        elif kind == "wo":
            cg, hf = j // 2, j % 2
            t = kcols(w_out, 512 * cg, 512)[:, 8 * hf:8 * hf + 8, :]
        elif kind == "uc":
            t = np.concatenate([kcols(w_up, 128 * j, 128), kcols(w_up, DFF + 128 * j, 128)], axis=1)
        elif kind == "dn":
            t = w_down[256 * j:256 * j + 256, :].reshape(2, P, D).transpose(1, 0, 2)
        elif kind == "wg":
            cg, hf = j // 2, j % 2
            t = kcols(w_ple_gate, 512 * cg, 512)[:, 8 * hf:8 * hf + 8, :]
        else:
            raise AssertionError(kind)
        t = t.reshape(P, -1)
        ws[i, :, :t.shape[1]] = t
    return ws


class Tracker:
    ENG = ("pe", "act", "dve", "pool", "sp")

    def __init__(self, nc, es):
        self.nc = nc
        self.es = es
        self.sem = {}
        self.cnt = {}
        self.streams = {e: [] for e in self.ENG}
        self.known = {e: {} for e in self.ENG}
        self.lw = {}
        self.rd = {}
        self.tensors = []
        self.atoms_of = {}
        self._inherit = {}
        for e in self.ENG[:4]:
            self._mksem(e)

    def _mksem(self, name):
        if name not in self.sem:
            self.sem[name] = self.es.enter_context(self.nc.semaphore("s_" + name))
            self.cnt[name] = 0
        return self.sem[name]

    def tensor(self, name, lo, hi):
        inherited = {}
        self.ghosts = getattr(self, "ghosts", [])
        for (glo, ghi, gd) in self.ghosts:
            if not (hi <= glo or lo >= ghi):
                for s, v in gd.items():
                    inherited[s] = max(inherited.get(s, 0), v)
        for t in self.tensors:
            if t[3] and not (hi <= t[1] or lo >= t[2]):
                t[3] = False
                gd = dict(self._inherit.get(t[0], {}))
                for a in self.atoms_of.get(t[0], ()):
                    w = self.lw.pop(a, None)
                    if w is not None:
                        gd[w[0]] = max(gd.get(w[0], 0), w[1])
                    for s, v in self.rd.pop(a, {}).items():
                        gd[s] = max(gd.get(s, 0), v)
                self.atoms_of.pop(t[0], None)
                self.ghosts.append((t[1], t[2], gd))
                for s, v in gd.items():
                    inherited[s] = max(inherited.get(s, 0), v)
        self.tensors.append([name, lo, hi, True])
        self.atoms_of[name] = set()
        self._inherit[name] = inherited

    def _touch(self, a):
        name = a[0]
        s = self.atoms_of.get(name)
        if s is not None and a not in s:
            s.add(a)
            inh = self._inherit.get(name)
            if inh:
                self.rd[a] = dict(inh)

    def _deps(self, eng, reads, writes):
        deps = {}

        def add(s, v, kind):
            if s == eng and (eng == "pe" or kind == "war"):
                return
            if v > deps.get(s, 0):
                deps[s] = v

        for a in reads:
            self._touch(a)
            w = self.lw.get(a)
            if w is not None:
                add(w[0], w[1], "raw")
        for a in writes:
            self._touch(a)
            w = self.lw.get(a)
            if w is not None:
                add(w[0], w[1], "waw")
            for s, v in self.rd.get(a, {}).items():
                add(s, v, "war")
        kn = self.known[eng]
        out = []
        for s, v in deps.items():
            if kn.get(s, 0) < v:
                kn[s] = v
                out.append((s, v))
        return out

    def _record(self, reads, writes, tag):
        for a in reads:
            r = self.rd.setdefault(a, {})
            if r.get(tag[0], 0) < tag[1]:
                r[tag[0]] = tag[1]
        for a in writes:
            self.lw[a] = tag
            self.rd[a] = {}

    def op(self, eng, fns, reads=(), writes=()):
        if not isinstance(fns, (list, tuple)):
            fns = [fns]
        waits = self._deps(eng, reads, writes)
        self.cnt[eng] += 1
        val = self.cnt[eng]
        sem = self.sem[eng]
        st = self.streams[eng]
        for s, v in waits:
            st.append(("w", self.sem[s], v))
        for f in fns[:-1]:
            st.append(("i", f, None, 0))
        st.append(("i", fns[-1], sem, 1))
        self._record(reads, writes, (eng, val))

    def dma(self, queue, slot, fn, reads=(), writes=()):
        sem = self._mksem("d_" + slot)
        name = "d_" + slot
        waits = self._deps(queue, reads, writes)
        self.cnt[name] += 16
        val = self.cnt[name]
        st = self.streams[queue]
        for s, v in waits:
            st.append(("w", self.sem[s], v))
        st.append(("i", fn, sem, 16))
        self._record(reads, writes, (name, val))

    def merge(self, dst, srcs):
        r = self.rd.setdefault(dst, {})
        for a in srcs:
            w = self.lw.get(a)
            if w is not None and r.get(w[0], 0) < w[1]:
                r[w[0]] = w[1]
            for s_, v in self.rd.get(a, {}).items():
                if r.get(s_, 0) < v:
                    r[s_] = v

    def wait_all(self, eng, atoms):
        waits = self._deps(eng, atoms, ())
        for s, v in waits:
            self.streams[eng].append(("w", self.sem[s], v))

    def replay(self, eng, e):
        for it in self.streams[eng]:
            if it[0] == "w":
                e.wait_ge(it[1], it[2])
            else:
                ins = it[1](e)
                if it[2] is not None:
                    ins.then_inc(it[2], it[3])


def build_program():
    nc = bass.Bass("TRN2", target_bir_lowering=False)
    xw = nc.dram_tensor("xw", [TW, D], F32, kind="ExternalInput").ap()
    pw = nc.dram_tensor("pw", [P, 2 * TO], F32, kind="ExternalInput").ap()
    wstream = nc.dram_tensor("wstream", [NL, P, SLOT], F32, kind="ExternalInput").ap()
    pproj_d = nc.dram_tensor("pproj", [P, 2 * D], F32, kind="ExternalInput").ap()
    gbc_d = nc.dram_tensor("gbc", [4, P, D], F32, kind="ExternalInput").ap()
    lngb_d = nc.dram_tensor("lngb", [2, P, 1024], F32, kind="ExternalInput").ap()
    wsT_d = nc.dram_tensor("wsT", [P, 1024], F32, kind="ExternalInput").ap()
    cst_d = nc.dram_tensor("cst", [P, CST_W], F32, kind="ExternalInput").ap()
    bsrep_d = nc.dram_tensor("bsrep", [P, 1024], F32, kind="ExternalInput").ap()
    y = nc.dram_tensor("y", [TO, D], F32, kind="ExternalOutput").ap()

    es = ExitStack()
    with es:
        ARENA_B = 165888
        arena = es.enter_context(nc.sbuf_tensor("arena", [P, ARENA_B // 2], BF16))
        ring = es.enter_context(nc.sbuf_tensor("ring", [P, NS, SLOT], BF16))
        cst = es.enter_context(nc.sbuf_tensor("cst_sb", [P, CST_W], F32))
        cbf = es.enter_context(nc.sbuf_tensor("cbf", [P, 4, 128], BF16))
        st = es.enter_context(nc.sbuf_tensor("stats", [P, 160], F32))
        fxob = es.enter_context(nc.sbuf_tensor("fxob", [P, 1024], F32))
        fx = fxob[:, :].rearrange("p (i n) -> p i n", i=8)
        biasK = es.enter_context(nc.sbuf_tensor("biasK", [P, NB, H, 3], F32))
        sel = es.enter_context(nc.sbuf_tensor("sel", [P, H, 128], BF16))
        rsb = es.enter_context(nc.sbuf_tensor("rsb", [P, 4], F32))
        gq2 = es.enter_context(nc.sbuf_tensor("gq2", [P, 2], F32))
        ps = es.enter_context(nc.psum_tensor("ps", [P, 8, 512], F32))

        tr = Tracker(nc, es)

        def bf(lo, n):
            return arena[:, lo // 2: lo // 2 + n]

        def f32(lo, n):
            return arena[:, lo // 2: lo // 2 + 2 * n].bitcast(F32)

        ZA, ZB, ZC, ZD, ZE = 0, 36864, 73728, 110592, 147456
        c32 = cst[:, C_ONE:C_ONE + 128]
        wsb = bf(ZB + 32768, 1024).rearrange("p (g t) -> p g t", g=8)
        bsh = bf(ZB + 28672, 2048).rearrange("p (i n) -> p i n", i=2)

        ident = cbf[:, 0, :]
        ones_bf = cbf[:, 1, :]
        tri_bf = cbf[:, 2, :]

        bank_rr = [0]

        def nbank():
            b = bank_rr[0]
            bank_rr[0] = (b + 1) % 8
            return b

        ring_pos = [0]

        def load_w(expect_kind):
            i = ring_pos[0]
            kind, j = PLAN[i]
            assert kind == expect_kind, (kind, expect_kind)
            r = i % NS
            n = _slot_elems(kind)
            ring_pos[0] += 1
            tr.dma("pool", "ring%d" % r,
                   lambda e, r=r, i=i, n=n: e.dma_start(out=ring[:, r, 0:n], in_=wstream[i, :, 0:n]),
                   reads=(), writes=[("ring", r)])
            return r

        def act(fn, reads, writes):
            tr.op("act", fn, reads, writes)

        def dve(fn, reads, writes):
            tr.op("dve", fn, reads, writes)

        def pe(fns, reads, writes):
            tr.op("pe", fns, reads, writes)

        def mm(out, lhsT, rhs, start, stop):
            return lambda e: e.matmul(out, lhsT, rhs, start=start, stop=stop)

        def tp(out, in_):
            return lambda e: e.transpose(out, in_, ident)

        tr.dma("sp", "cst", lambda e: e.dma_start(out=cst[:], in_=cst_d[:, :]), writes=[("cst",)])
        tr.tensor("wsf", ZC, ZC + 4096)
        tr.tensor("bstmp", ZC + 4096, ZC + 8192)
        tr.tensor("bsf", ZC + 8192, ZC + 12288)
        tr.tensor("bsh", ZB + 28672, ZB + 32768)
        tr.tensor("wsb", ZB + 32768, ZB + 34816)
        wsf = f32(ZC, 1024).rearrange("p (g t) -> p g t", g=8)
        bst = f32(ZC + 4096, 1024)
        bsf = f32(ZC + 8192, 1024)
        tr.dma("sp", "wsf", lambda e: e.dma_start(out=f32(ZC, 1024), in_=wsT_d[:, :]), writes=[("wsf",)])
        tr.dma("sp", "bsf", lambda e: e.dma_start(out=bsf, in_=bsrep_d[:, :]), writes=[("bsf",)])
        dve(lambda e: e.tensor_copy(out=ident, in_=cst[:, C_ID:C_ID + 128]), [("cst",)], [("ident",)])
        dve(lambda e: e.memset(ones_bf, 1.0), [], [("ones_bf",)])
        dve(lambda e: e.tensor_scalar(out=tri_bf, in0=cst[:, C_TRI:C_TRI + 128], scalar1=-1.0, scalar2=30000.0, op0=ALU.add, op1=ALU.mult),
            [("cst",)], [("tri_bf",)])
        for h in range(H):
            dve(lambda e, h=h: e.tensor_scalar(out=sel[:, h, :], in0=c32, scalar1=cst[:, C_ID + h:C_ID + h + 1], scalar2=None, op0=ALU.mult),
                [("cst",)], [("sel",)])
        for g in range(8):
            dve(lambda e, g=g: e.tensor_tensor(out=wsb[:, g, :], in0=wsf[:, g, :], in1=cst[:, C_TRI:C_TRI + 128], op=ALU.mult),
                [("wsf",), ("cst",)], [("wsb",)])
        dve(lambda e: e.tensor_copy(out=bsh[:, 0, :], in_=bsf), [("bsf",)], [("bsh", 0)])
        dve(lambda e: e.tensor_tensor(out=bst, in0=bsf, in1=bsh[:, 0, :], op=ALU.subtract), [("bsf",), ("bsh", 0)], [("bstmp",)])
        dve(lambda e: e.tensor_copy(out=bsh[:, 1, :], in_=bst), [("bstmp",)], [("bsh", 1)])
        dve(lambda e: e.tensor_scalar(out=cbf[:, 3, :], in0=c32, scalar1=cst[:, C_ID:C_ID + 1], scalar2=None, op0=ALU.mult),
            [("cst",)], [("e0",)])
        dve(lambda e: e.tensor_scalar(out=gq2[:, 0:1], in0=cst[:, C_GQ:C_GQ + 1], scalar1=float(128 ** -0.5), scalar2=None, op0=ALU.mult),
            [("cst",)], [("gq2",)])
        dve(lambda e: e.tensor_copy(out=gq2[:, 1:2], in_=cst[:, C_GK:C_GK + 1]), [("cst",)], [("gq2",)])

        S_SS, S_RS = 0, 16
        S_V1, S_VM, S_VQ = 32, 68, 80
        S_REC = 96
        S_ES, S_ER = 104, 136

        def norm_A(it):
            if it.get("pre"):
                it["pre"]()
            scol = it["scol"]
            src_ap, xn, junk, gb = it["src"], it["xn"], it["junk"], it["gb"]
            act(lambda e: e.activation(out=junk, in_=src_ap, func=AF.Square, accum_out=st[:, S_SS + scol:S_SS + scol + 1]),
                it["src_atoms"], list(it["junk_atoms"]) + [("st", S_SS + scol)])
            act(lambda e: e.activation(out=st[:, S_RS + scol:S_RS + scol + 1], in_=st[:, S_SS + scol:S_SS + scol + 1],
                                       func=AF.Sqrt, scale=1.0 / D, bias=EPS),
                [("st", S_SS + scol)], [("st", S_RS + scol)])
            dve(lambda e: e.reciprocal(out=st[:, S_RS + scol:S_RS + scol + 1], in_=st[:, S_RS + scol:S_RS + scol + 1]),
                [("st", S_RS + scol)], [("st", S_RS + scol)])
            dve(lambda e: e.scalar_tensor_tensor(out=xn, in0=src_ap, scalar=st[:, S_RS + scol:S_RS + scol + 1], in1=gb,
                                                 op0=ALU.mult, op1=ALU.mult),
                list(it["src_atoms"]) + [("st", S_RS + scol), it["gb_atom"]], list(it["xn_atoms"]))

        def norm_B(it):
            xn, dst_fn, dst_atoms = it["xn"], it["dst_fn"], it["dst_atoms"]
            for half in range(2):
                b = nbank()
                pb = ps[:, b, :].bitcast(BF16)
                pe([tp(pb[:, k * 128:(k + 1) * 128], xn[:, (half * 8 + k) * 128:(half * 8 + k + 1) * 128]) for k in range(8)],
                   list(it["xn_atoms"]) + [("ident",)], [("ps", b)])
                if half == 0:
                    act(lambda e, pb=pb, half=half: e.activation(out=dst_fn(half * 8), in_=pb.rearrange("p (k t) -> p k t", k=8), func=AF.Copy),
                        [("ps", b)], dst_atoms(half * 8))
                else:
                    dve(lambda e, pb=pb, half=half: e.tensor_copy(out=dst_fn(half * 8), in_=pb.rearrange("p (k t) -> p k t", k=8)),
                        [("ps", b)], dst_atoms(half * 8))

        def norm_phase(items, L=2, hook=None):
            n = len(items)
            for i in range(n + L):
                if i < n:
                    norm_A(items[i])
                if i - L >= 0:
                    norm_B(items[i - L])
                if hook is not None:
                    hook(i)

        tr.tensor("hq", ZA, ZA + 36864)
        tr.tensor("hc", ZB, ZB + 28672)
        hq = bf(ZA, KC * TQ).rearrange("p (k t) -> p k t", k=KC)
        hc = bf(ZB, KC * 896).rearrange("p (k t) -> p k t", k=KC)
        tr.tensor("p1tmp", ZD, ZD + 36864)
        tr.tensor("p1tmp2", ZC + 12288, ZC + 24576)
        xblk = [f32(ZD + i * 8192, D) for i in range(3)]
        xnb = [bf(ZD + 24576 + i * 4096, D) for i in range(3)]
        gb1 = f32(ZC + 12288, D)
        junk1 = bf(ZC + 20480, D)
        tr.dma("sp", "gb1", lambda e: e.dma_start(out=gb1, in_=gbc_d[0, :, :]), writes=[("p1tmp2", "gb")])

        def hwin(kc, wb):
            if wb < 7:
                return hc[:, kc, wb * 128:(wb + 1) * 128]
            return hq[:, kc, (wb - 7) * 128:(wb - 6) * 128]

        def hwin_atom(kc, wb):
            return ("hc", kc, wb) if wb < 7 else ("hq", kc, wb - 7)

        items = []
        for n_i, wb in enumerate(list(range(7, NB)) + list(range(0, 7))):
            xi = n_i % 3
            xb = xblk[xi]
            if wb < 7:
                dst = lambda k0, wb=wb: hc[:, k0:k0 + 8, wb * 128:(wb + 1) * 128]
            else:
                dst = lambda k0, wb=wb: hq[:, k0:k0 + 8, (wb - 7) * 128:(wb - 6) * 128]
            items.append(dict(
                pre=lambda xb=xb, wb=wb, xi=xi: tr.dma("sp", "xb%d" % xi, lambda e: e.dma_start(out=xb, in_=xw[wb * 128:(wb + 1) * 128, :]),
                                                      writes=[("p1tmp", "xb", xi)]),
                src=xb, src_atoms=[("p1tmp", "xb", xi)], gb=gb1, gb_atom=("p1tmp2", "gb"),
                xn=xnb[xi], xn_atoms=[("p1tmp", "xn", xi)], junk=junk1, junk_atoms=[("p1tmp2", "junk")],
                dst_fn=dst, dst_atoms=lambda k0, wb=wb: [hwin_atom(k0 + k, wb) for k in range(8)], scol=n_i))
        norm_phase(items)

        tr.tensor("gv", ZC, ZC + 36864)
        tr.tensor("vn", ZD, ZD + 18432)
        tr.tensor("lngb", ZE, ZE + 8192)
        gv = f32(ZC, QB * 1024).rearrange("p (b n) -> p b n", b=QB)
        vn = bf(ZD, QB * 1024).rearrange("p (b n) -> p b n", b=QB)
        lnG = f32(ZE, 1024)
        lnB = f32(ZE + 4096, 1024)
        tr.dma("sp", "lng", lambda e: e.dma_start(out=lnG, in_=lngb_d[0, :, :]), writes=[("lngb", 0)])
        tr.dma("sp", "lnb", lambda e: e.dma_start(out=lnB, in_=lngb_d[1, :, :]), writes=[("lngb", 1)])
        S_M2 = 144

        def ln_block(qb):
            gva = [("gv", qb, s) for s in range(4)]
            vm, vq, m2 = st[:, S_VM + qb:S_VM + qb + 1], st[:, S_VQ + qb:S_VQ + qb + 1], st[:, S_M2 + qb:S_M2 + qb + 1]
            dve(lambda e: e.tensor_reduce(out=vm, in_=st[:, S_V1 + qb * 4:S_V1 + qb * 4 + 4], axis=AX.X, op=ALU.add),
                [("st", S_V1 + qb * 4 + s) for s in range(4)], [("st", S_VM + qb)])
            dve(lambda e: e.tensor_scalar(out=vm, in0=vm, scalar1=-1.0 / 1024, scalar2=None, op0=ALU.mult),
                [("st", S_VM + qb)], [("st", S_VM + qb)])
            act(lambda e: e.activation(out=vn[:, qb, :], in_=gv[:, qb, :], func=AF.Square, accum_out=vq),
                gva, [("vn", qb), ("st", S_VQ + qb)])
            dve(lambda e: e.tensor_tensor(out=m2, in0=vm, in1=vm, op=ALU.mult), [("st", S_VM + qb)], [("st", S_M2 + qb)])
            dve(lambda e: e.tensor_scalar(out=vq, in0=vq, scalar1=1.0 / 1024, scalar2=m2, op0=ALU.mult, op1=ALU.subtract),
                [("st", S_VQ + qb), ("st", S_M2 + qb)], [("st", S_VQ + qb)])
            act(lambda e: e.activation(out=vq, in_=vq, func=AF.Sqrt, scale=1.0, bias=EPS), [("st", S_VQ + qb)], [("st", S_VQ + qb)])
            dve(lambda e: e.reciprocal(out=vq, in_=vq), [("st", S_VQ + qb)], [("st", S_VQ + qb)])
            dve(lambda e: e.scalar_tensor_tensor(out=gv[:, qb, :], in0=gv[:, qb, :], scalar=vm, in1=lnG, op0=ALU.add, op1=ALU.mult),
                gva + [("st", S_VM + qb), ("lngb", 0)], gva)
            dve(lambda e: e.scalar_tensor_tensor(out=vn[:, qb, :], in0=gv[:, qb, :], scalar=vq, in1=lnB, op0=ALU.mult, op1=ALU.add),
                gva + [("st", S_VQ + qb), ("lngb", 1)], [("vn", qb)])

        for s in range(4):
            r = load_w("v")
            wv = ring[:, r, :].rearrange("p (k n) -> p k n", k=KC)
            for qb in range(QB):
                b = nbank()
                pe([mm(ps[:, b, 0:256], hq[:, kc, qb * 128:(qb + 1) * 128], wv[:, kc, :], kc == 0, kc == KC - 1) for kc in range(KC)],
                   [("ring", r)] + [("hq", kc, qb) for kc in range(KC)], [("ps", b)])
                act(lambda e, b=b, qb=qb, s=s: e.activation(out=gv[:, qb, s * 256:(s + 1) * 256], in_=ps[:, b, 0:256], func=AF.Gelu,
                                                            accum_out=st[:, S_V1 + qb * 4 + s:S_V1 + qb * 4 + s + 1]),
                    [("ps", b)], [("gv", qb, s), ("st", S_V1 + qb * 4 + s)])
                if s == 3:
                    ln_block(qb)
        tr.tensor("oa", ZE, ZE + 18432)
        oa = bf(ZE, 8 * TQ).rearrange("p (g t) -> p g t", g=8)
        for s in range(4):
            r = load_w("u")
            wu = ring[:, r, :].rearrange("p (k n) -> p k n", k=KC)
            for gg in range(2):
                g = 2 * s + gg
                for ti, (t0, tn) in enumerate(Q_TILES):
                    b = nbank()
                    pe([mm(ps[:, b, 0:tn], wu[:, kc, gg * 128:(gg + 1) * 128], hq[:, kc, t0:t0 + tn], kc == 0, kc == KC - 1) for kc in range(KC)],
                       [("ring", r)] + [("hq", kc, qb) for kc in range(KC) for qb in range(t0 // 128, (t0 + tn) // 128)], [("ps", b)])
                    act(lambda e, b=b, g=g, t0=t0, tn=tn: e.activation(out=oa[:, g, t0:t0 + tn], in_=ps[:, b, 0:tn], func=AF.Gelu),
                        [("ps", b)], [("oa", g, qb) for qb in range(t0 // 128, (t0 + tn) // 128)])

        for qb in range(QB):
            for g0 in (0, 4):
                b = nbank()
                fns = []
                for g in range(g0, g0 + 4):
                    o = ps[:, b, (g - g0) * 128:(g - g0 + 1) * 128]
                    fns.append(mm(o, vn[:, qb, g * 128:(g + 1) * 128], wsb[:, g, :], True, False))
                    fns.append(mm(o, cbf[:, 3, :], bsh[:, 0, g * 128:(g + 1) * 128], False, False))
                    fns.append(mm(o, cbf[:, 3, :], bsh[:, 1, g * 128:(g + 1) * 128], False, True))
                pe(fns, [("vn", qb), ("wsb",), ("bsh", 0), ("bsh", 1), ("e0",)], [("ps", b)])
                dve(lambda e, b=b, g0=g0, qb=qb: e.tensor_tensor(out=oa[:, g0:g0 + 4, qb * 128:(qb + 1) * 128],
                                                                  in0=ps[:, b, :].rearrange("p (g t) -> p g t", g=4),
                                                                  in1=oa[:, g0:g0 + 4, qb * 128:(qb + 1) * 128], op=ALU.mult),
                    [("ps", b)] + [("oa", g, qb) for g in range(g0, g0 + 4)], [("oa", g, qb) for g in range(g0, g0 + 4)])

        tr.tensor("kT", ZC, ZC + 32768)
        tr.tensor("va", ZD, ZD + 33280)
        tr.tensor("rtm", ZC + 32768, ZC + 36864)
        tr.tensor("atmp", ZD + 33280, ZD + 36864)
        tr.tensor("fx", 10 ** 6, 10 ** 6 + 4096)
        kT = bf(ZC, H * TW).rearrange("p (h t) -> p h t", h=H)
        va = bf(ZD, NB * H * 130).rearrange("p (b h d) -> p b h d", b=NB, h=H)
        sqt = [bf(ZD + 33280 + i * 1024, 512) for i in range(2)]
        rtm = [f32(ZC + 32768 + i * 2048, 512) for i in range(2)]
        ptb = [bf(ZD + 35328 + i * 256, 128) for i in range(6)]
        W_TILES = [(0, 512), (512, 384)]

        qk_pend = [None]
        qk_cnt = [0]

        def qk_norm_tile(b, n, gcol, out_ap, out_atoms, cnt=None):
            i2 = qk_cnt[0] % 2
            qk_cnt[0] += 1
            act(lambda e: e.activation(out=sqt[i2][:, 0:n], in_=ps[:, b, 0:n], func=AF.Square), [("ps", b)], [("atmp", "sq", i2)])

            def tail():
                b2 = nbank()
                pe([mm(ps[:, b2, 0:n], ones_bf, sqt[i2][:, 0:n], True, True)], [("atmp", "sq", i2), ("ones_bf",)], [("ps", b2)])
                act(lambda e: e.activation(out=rtm[i2][:, 0:n], in_=ps[:, b2, 0:n], func=AF.Ln, scale=1.0 / 128, bias=EPS),
                    [("ps", b2)], [("rtm", i2)])
                act(lambda e: e.activation(out=rtm[i2][:, 0:n], in_=rtm[i2][:, 0:n], func=AF.Exp, scale=-0.5), [("rtm", i2)], [("rtm", i2)])
                dve(lambda e: e.scalar_tensor_tensor(out=out_ap, in0=ps[:, b, 0:n], scalar=gq2[:, gcol:gcol + 1], in1=rtm[i2][:, 0:n],
                                                     op0=ALU.mult, op1=ALU.mult),
                    [("ps", b), ("rtm", i2), ("gq2",)], out_atoms)

            if qk_pend[0] is not None:
                qk_pend[0]()
            qk_pend[0] = tail

        def qk_flush():
            if qk_pend[0] is not None:
                qk_pend[0]()
                qk_pend[0] = None

        cnt = 0
        for s in range(4):
            r = load_w("k")
            wk = ring[:, r, :].rearrange("p (k n) -> p k n", k=KC)
            for hh in range(2):
                h = 2 * s + hh
                tiles = [("c", t0, tn) for (t0, tn) in W_TILES] + [("q", t0, tn) for (t0, tn) in Q_TILES]
                for (src, t0, tn) in tiles:
                    b = nbank()
                    if src == "c":
                        rhs = lambda kc: hc[:, kc, t0:t0 + tn]
                        ratoms = [("hc", kc, wb) for kc in range(KC) for wb in range(t0 // 128, (t0 + tn) // 128)]
                        w0 = t0
                    else:
                        rhs = lambda kc: hq[:, kc, t0:t0 + tn]
                        ratoms = [("hq", kc, qb) for kc in range(KC) for qb in range(t0 // 128, (t0 + tn) // 128)]
                        w0 = 896 + t0
                    pe([mm(ps[:, b, 0:tn], wk[:, kc, hh * 128:(hh + 1) * 128], rhs(kc), kc == 0, kc == KC - 1) for kc in range(KC)],
                       [("ring", r)] + ratoms, [("ps", b)])
                    qk_norm_tile(b, tn, 1, kT[:, h, w0:w0 + tn], [("kT", h, wb) for wb in range(w0 // 128, (w0 + tn) // 128)], cnt)
                    cnt += 1
        qk_flush()
        dve(lambda e: e.memset(va[:, :, :, 128:130], 1.0), [], [("va", "ones")])
        for s in range(4):
            r = load_w("vv")
            wvv = ring[:, r, :].rearrange("p (k n) -> p k n", k=KC)
            for wb in range(NB):
                b = nbank()
                pe([mm(ps[:, b, 0:256], hwin(kc, wb), wvv[:, kc, :], kc == 0, kc == KC - 1) for kc in range(KC)],
                   [("ring", r)] + [hwin_atom(kc, wb) for kc in range(KC)], [("ps", b)])
                act(lambda e, b=b, wb=wb, s=s: e.activation(out=va[:, wb, 2 * s:2 * s + 2, 0:128],
                                                            in_=ps[:, b, 0:256].rearrange("p (h d) -> p h d", h=2), func=AF.Copy),
                    [("ps", b)], [("va", wb, 2 * s), ("va", wb, 2 * s + 1)])
        r = load_w("f")
        wf = ring[:, r, 0:128].rearrange("p (k n) -> p k n", k=KC)
        bF = nbank()
        fns = []
        for wb in range(NB):
            for kc in range(KC):
                fns.append(mm(ps[:, bF, wb * 8:(wb + 1) * 8], hwin(kc, wb), wf[:, kc, :], kc == 0, kc == KC - 1))
        pe(fns, [("ring", r)] + [hwin_atom(kc, wb) for kc in range(KC) for wb in range(NB)], [("ps", bF)])
        nl, cin, tot, mid, pre, cumN, refN = (fx[:, i, :] for i in range(7))
        dve(lambda e: e.tensor_tensor(out=nl, in0=ps[:, bF, 0:128], in1=cst[:, C_BF:C_BF + 128], op=ALU.add), [("ps", bF), ("cst",)], [("fx", 0)])
        act(lambda e: e.activation(out=nl, in_=nl, func=AF.Exp, scale=-1.0), [("fx", 0)], [("fx", 0)])
        act(lambda e: e.activation(out=nl, in_=nl, func=AF.Ln, bias=1.0), [("fx", 0)], [("fx", 0)])
        for i, lhs in ((1, cst[:, C_TRI:C_TRI + 128]), (2, c32), (3, cst[:, C_E63:C_E63 + 128])):
            b = nbank()
            pe([mm(ps[:, b, 0:128], lhs, nl, True, True)], [("fx", 0), ("cst",)], [("ps", b)])
            act(lambda e, b=b, i=i: e.activation(out=fx[:, i, :], in_=ps[:, b, 0:128], func=AF.Copy), [("ps", b)], [("fx", i)])
        dve(lambda e: e.memset(pre[:, 0:8], 0.0), [], [("fx", 4)])
        for wb in range(1, NB):
            dve(lambda e, wb=wb: e.tensor_tensor(out=pre[:, wb * 8:(wb + 1) * 8], in0=pre[:, (wb - 1) * 8:wb * 8], in1=tot[:, (wb - 1) * 8:wb * 8], op=ALU.add),
                [("fx", 4), ("fx", 2)], [("fx", 4)])
        dve(lambda e: e.tensor_tensor(out=cumN, in0=cin, in1=pre, op=ALU.add), [("fx", 1), ("fx", 4)], [("fx", 5)])
        dve(lambda e: e.tensor_tensor(out=refN, in0=mid, in1=pre, op=ALU.add), [("fx", 3), ("fx", 4)], [("fx", 6)])
        WBMID = [7, 10, 14]
        cum3 = cumN.rearrange("p (i h) -> p i h", h=H)
        for T in range(3):
            dve(lambda e, T=T: e.tensor_tensor(out=biasK[:, :, :, T], in0=cum3,
                                               in1=refN[:, WBMID[T] * 8:(WBMID[T] + 1) * 8].unsqueeze(1).to_broadcast([P, NB, H]), op=ALU.subtract),
                [("fx", 5), ("fx", 6)], [("biasK", T)])
            if T >= 1:
                dve(lambda e, T=T: e.tensor_scalar(out=biasK[:, 0:8, :, T], in0=biasK[:, 0:8, :, T], scalar1=cst[:, C_FL:C_FL + 1], scalar2=None, op0=ALU.add),
                    [("biasK", T), ("cst",)], [("biasK", T)])

        tr.tensor("qT", ZB, ZB + 18432)
        qT = bf(ZB, H * TQ).rearrange("p (h t) -> p h t", h=H)
        for s in range(4):
            r = load_w("q")
            wq = ring[:, r, :].rearrange("p (k n) -> p k n", k=KC)
            for hh in range(2):
                h = 2 * s + hh
                for (t0, tn) in Q_TILES:
                    b = nbank()
                    pe([mm(ps[:, b, 0:tn], wq[:, kc, hh * 128:(hh + 1) * 128], hq[:, kc, t0:t0 + tn], kc == 0, kc == KC - 1) for kc in range(KC)],
                       [("ring", r)] + [("hq", kc, qb) for kc in range(KC) for qb in range(t0 // 128, (t0 + tn) // 128)], [("ps", b)])
                    qk_norm_tile(b, tn, 0, qT[:, h, t0:t0 + tn], [("qT", h, qb) for qb in range(t0 // 128, (t0 + tn) // 128)], cnt)
                    cnt += 1

        qk_flush()
        tr.tensor("obT", ZB + 18432, ZB + 36864)
        tr.tensor("crow", ZD + 33280, ZD + 35584)
        tr.tensor("obh", ZD + 35584, ZD + 36608)
        tr.tensor("pt", ZC + 32768, ZC + 35840)
        obT = bf(ZB + 18432, H * TQ).rearrange("p (h t) -> p h t", h=H)
        crow = bf(ZD + 33280, TQ)
        obh = [bf(ZD + 35584 + i * 256, 128) for i in range(4)]
        ptb = [bf(ZC + 32768 + i * 1024, 512) for i in range(3)]
        TILES = [(0, 1), (1, 5), (5, 9)]
        idf = cst[:, C_ID:C_ID + 128]

        def tp32(out, in_):
            return lambda e: e.transpose(out, in_, idf)

        dve(lambda e: e.memset(crow, 0.0), [], [("crow",)])
        ba, bb, bc = nbank(), nbank(), nbank()
        pe([tp32(ps[0:8, ba, k * 128:(k + 1) * 128], cumN[:, (7 + k) * 8:(8 + k) * 8]) for k in range(4)], [("fx", 5), ("cst",)], [("ps", ba)])
        pe([tp32(ps[0:8, bb, k * 128:(k + 1) * 128], cumN[:, (11 + k) * 8:(12 + k) * 8]) for k in range(4)], [("fx", 5), ("cst",)], [("ps", bb)])
        pe([tp32(ps[0:8, bc, 0:128], cumN[:, 15 * 8:16 * 8])]
           + [tp32(ps[0:8, bc, (1 + T) * 128:(2 + T) * 128], refN[:, WBMID[T] * 8:(WBMID[T] + 1) * 8]) for T in range(3)],
           [("fx", 5), ("fx", 6), ("cst",)], [("ps", bc)])
        dve(lambda e: e.tensor_copy(out=rsb[0:8, 0:3], in_=ps[0:8, bc, 128:512].rearrange("p (t n) -> p t n", t=3)[:, :, 0]), [("ps", bc)], [("rsb",)])
        for qb in range(QB):
            bank, k = (ba, qb) if qb < 4 else ((bb, qb - 4) if qb < 8 else (bc, 0))
            T = 0 if qb == 0 else (1 if qb < 5 else 2)
            dve(lambda e, bank=bank, k=k, T=T, qb=qb: e.tensor_scalar(out=crow[0:8, qb * 128:(qb + 1) * 128], in0=ps[0:8, bank, k * 128:(k + 1) * 128],
                                                                       scalar1=-1.0, scalar2=rsb[0:8, T:T + 1], op0=ALU.mult, op1=ALU.add),
                [("ps", bank), ("rsb",), ("crow",)], [("crow",)])

        units = []
        for T, (q0, q1) in enumerate(TILES):
            for h in range(H):
                for i in range(7 + q1):
                    units.append((T, h, i, max(q0, i - 7), q1, q0))
        sbank = {}
        s_rr = [0]
        p_rr = [0]
        o_rr = [0]
        t_rr = [0]
        S_BANKS = (0, 1, 7)
        pbt6 = ps[:, 6, :].bitcast(BF16)
        for k in range(8):
            tr.merge(("pst", k), [("ps", 6)])

        def emit_S(u):
            T, h, i, qlo, q1, q0 = u
            n = (q1 - qlo) * 128
            b = S_BANKS[s_rr[0] % 3]
            s_rr[0] += 1
            sbank[u] = b
            diag = (i - 7) >= q0
            fns = [mm(ps[:, b, 0:n], kT[:, h, i * 128:(i + 1) * 128], qT[:, h, qlo * 128:q1 * 128], True, False),
                   mm(ps[:, b, 0:n], sel[:, h, :], crow[:, qlo * 128:q1 * 128], False, not diag)]
            if diag:
                fns.append(mm(ps[:, b, 0:128], ident, tri_bf, False, True))
            pe(fns, [("kT", h, i), ("crow",), ("sel",), ("ident",), ("tri_bf",)] + [("qT", h, qb) for qb in range(qlo, q1)], [("ps", b)])

        def emit_PV(u):
            T, h, i, qlo, q1, q0 = u
            n = (q1 - qlo) * 128
            b = sbank[u]
            pi = p_rr[0] % 3
            p_rr[0] += 1
            act(lambda e: e.activation(out=ptb[pi][:, 0:n], in_=ps[:, b, 0:n], func=AF.Exp, bias=biasK[:, i, h, T:T + 1]),
                [("ps", b), ("biasK", T)], [("pt", pi)])
            for qb in range(qlo, q1):
                j = 7 + qb
                bo = 2 + (qb - q0)
                pe([mm(ps[:, bo, 0:129], ptb[pi][:, (qb - qlo) * 128:(qb - qlo + 1) * 128], va[:, i, h, 0:129], i == 0, i == j)],
                   [("pt", pi), ("va", i, h), ("va", "ones")], [("ps", bo)])
                if i == j:
                    rc = S_REC + o_rr[0] % 8
                    oi = o_rr[0] % 4
                    o_rr[0] += 1
                    dve(lambda e, bo=bo, rc=rc: e.reciprocal(out=st[:, rc:rc + 1], in_=ps[:, bo, 128:129]), [("ps", bo)], [("st", rc)])
                    dve(lambda e, bo=bo, rc=rc, oi=oi: e.tensor_scalar(out=obh[oi], in0=ps[:, bo, 0:128], scalar1=st[:, rc:rc + 1], scalar2=None, op0=ALU.mult),
                        [("ps", bo), ("st", rc)], [("obh", oi)])
                    k = t_rr[0] % 8
                    t_rr[0] += 1

                    def fin(k=k, oi=oi, h=h, qb=qb):
                        pe([tp(pbt6[:, k * 128:(k + 1) * 128], obh[oi])], [("obh", oi), ("ident",)], [("pst", k)])
                        dve(lambda e: e.tensor_copy(out=obT[:, h, qb * 128:(qb + 1) * 128], in_=pbt6[:, k * 128:(k + 1) * 128]),
                            [("pst", k)], [("obT", h, qb)])
                    fin_pend.append(fin)

        fin_pend = []
        emit_S(units[0])
        emit_S(units[1])
        for ui, u in enumerate(units):
            if ui + 2 < len(units):
                emit_S(units[ui + 2])
            ready = list(fin_pend)
            del fin_pend[:]
            emit_PV(u)
            for f_ in ready:
                f_()
        for f_ in fin_pend:
            f_()
        tr.merge(("ps", 6), [("pst", k) for k in range(8)])

        tr.tensor("yT", ZC, ZC + 36864)
        tr.tensor("gtmp", ZB, ZB + 8192)
        yT = bf(ZC, KC * TQ).rearrange("p (k t) -> p k t", k=KC)
        sgt = [f32(ZB + i * 2048, 512) for i in range(4)]
        gcnt = 0
        for c in range(16):
            ra = load_w("ga")
            rb = load_w("gb")
            wa = ring[:, ra, 0:3072].rearrange("p (k n) -> p k n", k=24)
            wb_ = ring[:, rb, 0:3072].rearrange("p (k n) -> p k n", k=24)
            for (t0, tn) in ((126, 342), (468, 342), (810, 342)):
                qbs = range(t0 // 128, (t0 + tn - 1) // 128 + 1)
                hqa = [("hq", kc, qb) for kc in range(KC) for qb in qbs]
                b1, b2, b3, b4 = nbank(), nbank(), nbank(), nbank()
                pe([mm(ps[:, b1, 0:tn], wa[:, kc, :], hq[:, kc, t0:t0 + tn], kc == 0, kc == KC - 1) for kc in range(KC)],
                   [("ring", ra)] + hqa, [("ps", b1)])
                pe([mm(ps[:, b2, 0:tn], wa[:, 16 + kc, :], oa[:, kc, t0:t0 + tn], kc == 0, kc == 7) for kc in range(8)],
                   [("ring", ra)] + [("oa", g, qb) for g in range(8) for qb in qbs], [("ps", b2)])
                pe([mm(ps[:, b3, 0:tn], wb_[:, kc, :], hq[:, kc, t0:t0 + tn], kc == 0, kc == KC - 1) for kc in range(KC)],
                   [("ring", rb)] + hqa, [("ps", b3)])
                pe([mm(ps[:, b4, 0:tn], wb_[:, 16 + kc, :], obT[:, kc, t0:t0 + tn], kc == 0, kc == 7) for kc in range(8)],
                   [("ring", rb)] + [("obT", hh, qb) for hh in range(8) for qb in qbs], [("ps", b4)])
                s1, s2 = sgt[(gcnt % 2) * 2], sgt[(gcnt % 2) * 2 + 1]
                a1, a2 = ("gtmp", (gcnt % 2) * 2), ("gtmp", (gcnt % 2) * 2 + 1)
                gcnt += 1
                act(lambda e, b1=b1, s1=s1, tn=tn: e.activation(out=s1[:, 0:tn], in_=ps[:, b1, 0:tn], func=AF.Sigmoid), [("ps", b1)], [a1])
                act(lambda e, b3=b3, s2=s2, tn=tn: e.activation(out=s2[:, 0:tn], in_=ps[:, b3, 0:tn], func=AF.Sigmoid), [("ps", b3)], [a2])
                dve(lambda e, b2=b2, s1=s1, tn=tn: e.tensor_tensor(out=s1[:, 0:tn], in0=ps[:, b2, 0:tn], in1=s1[:, 0:tn], op=ALU.mult), [("ps", b2), a1], [a1])
                dve(lambda e, b4=b4, s2=s2, tn=tn: e.tensor_tensor(out=s2[:, 0:tn], in0=ps[:, b4, 0:tn], in1=s2[:, 0:tn], op=ALU.mult), [("ps", b4), a2], [a2])
                dve(lambda e, s1=s1, s2=s2, c=c, t0=t0, tn=tn: e.tensor_tensor(out=yT[:, c, t0:t0 + tn], in0=s1[:, 0:tn], in1=s2[:, 0:tn], op=ALU.add),
                    [a1, a2], [("yT", c, qb) for qb in qbs])

        tr.tensor("x1", ZA, ZA + 65536)
        tr.tensor("xh", ZA + 65536, ZA + 73728)
        tr.tensor("xp", ZE, ZE + 8192)
        x1 = f32(ZA, 8 * D).rearrange("p (b n) -> p b n", b=8)
        xh = f32(ZA + 65536, D)
        xp = [f32(ZE + i * 2048, 512) for i in range(4)]
        xc = 0
        for cg in range(4):
            r0 = load_w("wo")
            r1 = load_w("wo")
            wo = [ring[:, r0, :].rearrange("p (k n) -> p k n", k=8), ring[:, r1, :].rearrange("p (k n) -> p k n", k=8)]
            for qb in range(QB):
                xi = xc % 4
                xc += 1
                tr.dma("sp", "xp%d" % xi,
                       lambda e, xi=xi, qb=qb, cg=cg: e.dma_start(out=xp[xi], in_=xw[(7 + qb) * 128:(8 + qb) * 128, cg * 512:(cg + 1) * 512]),
                       writes=[("xp", xi)])
                b = nbank()
                pe([mm(ps[:, b, :], yT[:, kc, qb * 128:(qb + 1) * 128], wo[kc // 8][:, kc % 8, :], kc == 0, kc == KC - 1) for kc in range(KC)],
                   [("ring", r0), ("ring", r1)] + [("yT", kc, qb) for kc in range(KC)], [("ps", b)])
                if qb == 0:
                    o, oat = xh[:, cg * 512:(cg + 1) * 512], ("xh", cg)
                else:
                    o, oat = x1[:, qb - 1, cg * 512:(cg + 1) * 512], ("x1", qb - 1, cg)
                dve(lambda e, b=b, o=o, xi=xi: e.tensor_tensor(out=o, in0=ps[:, b, :], in1=xp[xi], op=ALU.add), [("ps", b), ("xp", xi)], [oat])

        tr.tensor("h2", ZD, ZD + 36864)
        tr.tensor("ctmp", ZC, ZC + 36864)
        h2 = bf(ZD, KC * TQ).rearrange("p (k t) -> p k t", k=KC)
        xn2 = [bf(ZC, D), bf(ZC + 4096, D), bf(ZC + 24608 + 4096, D)]
        xn2_atoms = [[("ctmp", "xn", 0)], [("ctmp", "xn", 1)], [("ctmp", "tA", 1)]]
        junk2 = bf(ZC + 24608, D)
        gb2 = f32(ZC + 8192, D)
        tr.dma("sp", "gb2", lambda e: e.dma_start(out=gb2, in_=gbc_d[1, :, :]), writes=[("ctmp", "gb")])
        items = []
        for blk in range(QB):
            src = xh if blk == 0 else x1[:, blk - 1, :]
            satoms = [("xh", cg) for cg in range(4)] if blk == 0 else [("x1", blk - 1, cg) for cg in range(4)]
            items.append(dict(src=src, src_atoms=satoms, gb=gb2, gb_atom=("ctmp", "gb"), xn=xn2[blk % 3], xn_atoms=xn2_atoms[blk % 3],
                              junk=junk2, junk_atoms=[("ctmp", "tA", 0)],
                              dst_fn=lambda k0, blk=blk: h2[:, k0:k0 + 8, blk * 128:(blk + 1) * 128],
                              dst_atoms=lambda k0, blk=blk: [("h2", k0 + k, blk) for k in range(8)], scol=blk))
        norm_phase(items)

        a_sb = [f32(ZC + 16384 + i * 4112, 1028) for i in range(2)]
        tA = [f32(ZC + 24608 + i * 4096, 1024) for i in range(2)]
        tr.tensor("hid", ZE + 8192, ZE + 16384)
        hid = [bf(ZE + 8192 + i * 4096, 2048).rearrange("p (c t) -> p c t", c=2) for i in range(2)]
        tB = [f32(ZE + i * 4096, 1024) for i in range(2)]
        tr.tensor("tB", ZE, ZE + 8192)
        cw = cst[:, C_CW:C_CW + 132].rearrange("p (c j) -> p c j", j=3)
        cb = cst[:, C_CB:C_CB + 44]
        fcnt = [0]

        BA0, BA1, BAH, BB0, BB1 = 0, 1, 2, 3, 4
        d_rr = [0]

        def U_chunk_steps(g, cc):
            c = 2 * g + cc
            hb = hid[g % 2]
            i2 = c % 2
            asb, ta, tb_ = a_sb[i2], tA[i2], tB[i2]
            aat, tat, tbt = ("ctmp", "a", i2), ("ctmp", "tA", i2), ("tB", i2)
            box = {}

            def agroup(b, t0, tn):
                w = box["w"]
                pe([mm(ps[:, b, 0:tn], w[:, kc, :], h2[:, kc, t0:t0 + tn], kc == 0, kc == KC - 1) for kc in range(KC)],
                   [("ring", box["r"])] + [("h2", kc, blk) for kc in range(KC) for blk in range(t0 // 128, (t0 + tn - 1) // 128 + 1)], [("ps", b)])

            def bgroup(b, t0, tn):
                w = box["w"]
                pe([mm(ps[:, b, 0:tn], w[:, 16 + kc, :], h2[:, kc, t0:t0 + tn], kc == 0, kc == KC - 1) for kc in range(KC)],
                   [("ring", box["r"])] + [("h2", kc, blk) for kc in range(KC) for blk in range(t0 // 128, (t0 + tn) // 128)], [("ps", b)])

            def s0():
                box["r"] = load_w("uc")
                box["w"] = ring[:, box["r"], :].rearrange("p (k n) -> p k n", k=32)
                agroup(BA0, 126, 342)
                agroup(BA1, 468, 342)
                act(lambda e: e.activation(out=asb[:, 0:342], in_=ps[:, BA0, 0:342], func=AF.Copy), [("ps", BA0)], [aat])
                act(lambda e: e.activation(out=asb[:, 0:2], in_=asb[:, 0:2], func=AF.Copy, scale=cst[:, C_FL + 1:C_FL + 2]), [aat, ("cst",)], [aat])
                act(lambda e: e.activation(out=asb[:, 342:684], in_=ps[:, BA1, 0:342], func=AF.Copy), [("ps", BA1)], [aat])

            def s1():
                agroup(BAH, 810, 342)
                act(lambda e: e.activation(out=asb[:, 684:1026], in_=ps[:, BAH, 0:342], func=AF.Copy), [("ps", BAH)], [aat])

            def s1post():
                dve(lambda e: e.tensor_scalar(out=ta, in0=asb[:, 2:1026], scalar1=cw[:, c, 2:3], scalar2=cb[:, c:c + 1], op0=ALU.mult, op1=ALU.add),
                    [aat, ("cst",)], [tat])
                dve(lambda e: e.scalar_tensor_tensor(out=tb_, in0=asb[:, 1:1025], scalar=cw[:, c, 1:2], in1=ta, op0=ALU.mult, op1=ALU.add),
                    [aat, tat, ("cst",)], [tbt])
                dve(lambda e: e.scalar_tensor_tensor(out=ta, in0=asb[:, 0:1024], scalar=cw[:, c, 0:1], in1=tb_, op0=ALU.mult, op1=ALU.add),
                    [aat, tbt, ("cst",)], [tat])
                act(lambda e: e.activation(out=tb_, in_=ta, func=AF.Gelu_apprx_tanh), [tat], [tbt])

            def s2():
                bgroup(BB0, 128, 512)

            def s2post():
                dve(lambda e: e.tensor_tensor(out=hb[:, cc, 0:512], in0=ps[:, BB0, :], in1=tb_[:, 0:512], op=ALU.mult),
                    [("ps", BB0), tbt], [("hid", g % 2, cc, 0)])

            def s3():
                bgroup(BB1, 640, 512)

            def s3post():
                dve(lambda e: e.tensor_tensor(out=hb[:, cc, 512:1024], in0=ps[:, BB1, :], in1=tb_[:, 512:1024], op=ALU.mult),
                    [("ps", BB1), tbt], [("hid", g % 2, cc, 1)])

            return [(s0, None), (s1, s1post), (s2, s2post), (s3, s3post)]

        def D_steps(g):
            box = {}
            hb = hid[g % 2]

            def preload():
                if "r" not in box:
                    box["r"] = load_w("dn")
                    box["w"] = ring[:, box["r"], :].rearrange("p (c n) -> p c n", c=2)

            def piece(tb, cg):
                def f():
                    preload()
                    wd = box["w"]
                    b = 5 + d_rr[0] % 3
                    d_rr[0] += 1
                    pe([mm(ps[:, b, :], hb[:, cc, tb * 128:(tb + 1) * 128], wd[:, cc, cg * 512:(cg + 1) * 512], cc == 0, cc == 1) for cc in range(2)],
                       [("ring", box["r"])] + [("hid", g % 2, cc, tb // 4) for cc in range(2)], [("ps", b)])
                    dve(lambda e: e.tensor_tensor(out=x1[:, tb, cg * 512:(cg + 1) * 512], in0=ps[:, b, :],
                                                  in1=x1[:, tb, cg * 512:(cg + 1) * 512], op=ALU.add),
                        [("ps", b), ("x1", tb, cg)], [("x1", tb, cg)])
                return f
            return [preload] + [piece(tb, cg) for tb in range(8) for cg in range(4)]

        def p12_prelude():
            tr.tensor("h3", ZD, ZD + 32768)
            tr.tensor("c12", ZC, ZC + 36864)
            tr.tensor("pp", ZE, ZE + 8192)
            tr.dma("sp", "gb3", lambda e: e.dma_start(out=gb3, in_=gbc_d[2, :, :]), writes=[("c12", "gb")])
            tr.dma("pool", "pp", lambda e: e.dma_start(out=bf(ZE, 2 * D), in_=pproj_d[:, :]), writes=[("pp",)])
            tr.dma("pool", "pT", lambda e: e.dma_start(out=bf(ZC + 16384, 2 * TO), in_=pw[:, :]), writes=[("c12", "pT", tb) for tb in range(8)])

        h3 = bf(ZD, KC * TO).rearrange("p (k t) -> p k t", k=KC)
        xn3 = [bf(ZC, D), bf(ZC + 4096, D), bf(ZC + 27648, D)]
        xn3_atoms = [[("c12", "xn", 0)], [("c12", "xn", 1)], [("c12", "te", 0), ("c12", "te", 1)]]
        junk3 = bf(ZC + 23552, D)
        gb3 = f32(ZC + 8192, D)
        pT = bf(ZC + 16384, 2 * TO).rearrange("p (k t) -> p k t", k=2)
        pf = [f32(ZC + 20480 + i * 1024, 256) for i in range(2)]
        pb_ = [bf(ZC + 22528 + i * 512, 256) for i in range(2)]
        sg = [f32(ZC + 23552 + i * 2048, 512) for i in range(2)]
        te = [f32(ZC + 27648 + i * 2048, 512) for i in range(2)]
        gpl = [f32(ZC + 31744 + i * 2048, 512) for i in range(2)]
        ppj = bf(ZE, 2 * D).rearrange("p (k n) -> p k n", k=2)

        for g in range(NG + 1):
            usteps = (U_chunk_steps(g, 0) + U_chunk_steps(g, 1)) if g < NG else []
            dsteps = D_steps(g - 1) if g >= 1 else []
            if dsteps:
                dpre, dsteps = dsteps[0], dsteps[1:]
            if g == NG:
                dpre()
                p12_prelude()
            if usteps:
                per = len(dsteps) // len(usteps)
                for i, (u, post) in enumerate(usteps):
                    u()
                    for d in dsteps[i * per:(i + 1) * per]:
                        d()
                    if post is not None:
                        post()
            else:
                for d in dsteps:
                    d()

        def e_stats(tb):
            for cg in range(4):
                b = nbank()
                pe([mm(ps[:, b, :], pT[:, k, tb * 128:(tb + 1) * 128], ppj[:, k, cg * 512:(cg + 1) * 512], k == 0, k == 1) for k in range(2)],
                   [("c12", "pT", tb), ("pp",)], [("ps", b)])
                i2 = (tb * 4 + cg) % 2
                act(lambda e, b=b, tb=tb, cg=cg, i2=i2: e.activation(out=gpl[i2], in_=ps[:, b, :], func=AF.Square,
                                                                      accum_out=st[:, S_ES + tb * 4 + cg:S_ES + tb * 4 + cg + 1]),
                    [("ps", b)], [("c12", "gpl", i2), ("st", S_ES + tb * 4 + cg)])
            dve(lambda e, tb=tb: e.tensor_reduce(out=st[:, S_ER + tb:S_ER + tb + 1], in_=st[:, S_ES + tb * 4:S_ES + tb * 4 + 4], axis=AX.X, op=ALU.add),
                [("st", S_ES + tb * 4 + cg) for cg in range(4)], [("st", S_ER + tb)])
            act(lambda e, tb=tb: e.activation(out=st[:, S_ER + tb:S_ER + tb + 1], in_=st[:, S_ER + tb:S_ER + tb + 1], func=AF.Sqrt, scale=1.0 / D, bias=EPS),
                [("st", S_ER + tb)], [("st", S_ER + tb)])
            dve(lambda e, tb=tb: e.reciprocal(out=st[:, S_ER + tb:S_ER + tb + 1], in_=st[:, S_ER + tb:S_ER + tb + 1]), [("st", S_ER + tb)], [("st", S_ER + tb)])
        items = []
        for tb in range(8):
            items.append(dict(src=x1[:, tb, :], src_atoms=[("x1", tb, cg) for cg in range(4)], gb=gb3, gb_atom=("c12", "gb"),
                              xn=xn3[tb % 3], xn_atoms=xn3_atoms[tb % 3], junk=junk3, junk_atoms=[("c12", "sg", 0), ("c12", "sg", 1)],
                              dst_fn=lambda k0, tb=tb: h3[:, k0:k0 + 8, tb * 128:(tb + 1) * 128],
                              dst_atoms=lambda k0, tb=tb: [("h3", k0 + k, tb) for k in range(8)], scol=tb))
        norm_phase(items, hook=lambda i: e_stats(i) if i < 8 else None)
        for cg in range(4):
            r0 = load_w("wg")
            r1 = load_w("wg")
            wg = [ring[:, r0, :].rearrange("p (k n) -> p k n", k=8), ring[:, r1, :].rearrange("p (k n) -> p k n", k=8)]
            gi = cg % 2
            tr.dma("sp", "gpl%d" % gi, lambda e, gi=gi, cg=cg: e.dma_start(out=gpl[gi], in_=gbc_d[3, :, cg * 512:(cg + 1) * 512]), writes=[("c12", "gpl", gi)])
            for tb in range(8):
                b1, b2 = nbank(), nbank()
                pe([mm(ps[:, b1, :], h3[:, kc, tb * 128:(tb + 1) * 128], wg[kc // 8][:, kc % 8, :], kc == 0, kc == KC - 1) for kc in range(KC)],
                   [("ring", r0), ("ring", r1)] + [("h3", kc, tb) for kc in range(KC)], [("ps", b1)])
                pe([mm(ps[:, b2, :], pT[:, k, tb * 128:(tb + 1) * 128], ppj[:, k, cg * 512:(cg + 1) * 512], k == 0, k == 1) for k in range(2)],
                   [("c12", "pT", tb), ("pp",)], [("ps", b2)])
                i2 = tb % 2
                act(lambda e, b1=b1, i2=i2: e.activation(out=sg[i2], in_=ps[:, b1, :], func=AF.Sigmoid), [("ps", b1)], [("c12", "sg", i2)])
                dve(lambda e, b2=b2, i2=i2, tb=tb, gi=gi: e.scalar_tensor_tensor(out=te[i2], in0=ps[:, b2, :], scalar=st[:, S_ER + tb:S_ER + tb + 1], in1=gpl[gi],
                                                                                 op0=ALU.mult, op1=ALU.mult),
                    [("ps", b2), ("st", S_ER + tb), ("c12", "gpl", gi)], [("c12", "te", i2)])
                dve(lambda e, i2=i2: e.tensor_tensor(out=te[i2], in0=te[i2], in1=sg[i2], op=ALU.mult), [("c12", "te", i2), ("c12", "sg", i2)], [("c12", "te", i2)])
                dve(lambda e, i2=i2, tb=tb, cg=cg: e.tensor_tensor(out=x1[:, tb, cg * 512:(cg + 1) * 512], in0=x1[:, tb, cg * 512:(cg + 1) * 512], in1=te[i2], op=ALU.add),
                    [("c12", "te", i2), ("x1", tb, cg)], [("x1", tb, cg)])
                if cg == 3:
                    tr.dma("sp", "out%d" % tb, lambda e, tb=tb: e.dma_start(out=y[tb * 128:(tb + 1) * 128, :], in_=x1[:, tb, :]),
                           reads=[("x1", tb, c4) for c4 in range(4)], writes=[("yout", tb)])
        tr.wait_all("sp", [("yout", tb) for tb in range(8)])
        assert ring_pos[0] == NL

        with nc.Block() as block:
            @block.sync
            def _(e):
                tr.replay("sp", e)

            @block.gpsimd
            def _(e):
                tr.replay("pool", e)

            @block.tensor
            def _(e):
                tr.replay("pe", e)

            @block.scalar
            def _(e):
                tr.replay("act", e)

            @block.vector
            def _(e):
                tr.replay("dve", e)
    return nc


_CACHE = {}


def kernel(x, p, norm_mix_g, w_in, gmlp_ln_g, gmlp_ln_b, gmlp_w_s, gmlp_b_s, fox_b_f,
           q_norm_g, k_norm_g, w_branch_a, w_branch_b, w_out, norm_ffn_g, w_up,
           conv_w, conv_b, w_down, ple_proj, ple_norm_g, ple_gate_norm_g, w_ple_gate):
    f = lambda a: np.ascontiguousarray(np.asarray(a, dtype=np.float32))
    x, p = f(x), f(p)
    w_in, w_branch_a, w_branch_b, w_out = f(w_in)[0], f(w_branch_a)[0], f(w_branch_b)[0], f(w_out)[0]
    w_up, w_down, w_ple_gate, ple_proj = f(w_up)[0], f(w_down)[0], f(w_ple_gate)[0], f(ple_proj)[0]

    ws = _build_wstream(w_in, w_branch_a, w_branch_b, w_out, w_up, w_down, w_ple_gate)
    pproj = np.ascontiguousarray(ple_proj.reshape(2, P, D).transpose(1, 0, 2).reshape(P, 2 * D))
    rep = lambda v: np.ascontiguousarray(np.broadcast_to(np.asarray(v, np.float32).reshape(1, -1), (P, np.asarray(v).size)))
    gbc = np.stack([rep(f(norm_mix_g)[0]), rep(f(norm_ffn_g)[0]), rep(f(ple_gate_norm_g)[0]), rep(f(ple_norm_g)[0])])
    lngb = np.stack([rep(f(gmlp_ln_g)[0]), rep(f(gmlp_ln_b)[0])])
    wsT = np.ascontiguousarray(f(gmlp_w_s)[0].transpose(2, 0, 1).reshape(P, 1024))
    cst = np.zeros((P, CST_W), np.float32)
    ii = np.arange(P)
    cst[:, C_TRI:C_TRI + 128] = (ii[None, :] >= ii[:, None]).astype(np.float32)
    cst[:, C_E63:C_E63 + 128] = (ii[:, None] <= 63).astype(np.float32)
    cst[:, C_ID:C_ID + 128] = np.eye(P, dtype=np.float32)
    cst[:, C_ONE:C_ONE + 128] = 1.0
    bsrep = rep(f(gmlp_b_s)[0].reshape(-1))
    cst[:, C_BF:C_BF + 128] = rep(np.tile(f(fox_b_f)[0], NB))
    cst[:, C_GQ] = f(q_norm_g)[0]
    cst[:, C_GK] = f(k_norm_g)[0]
    cst[:, C_CW:C_CW + 132] = f(conv_w)[0].reshape(3, NFC, P).transpose(2, 1, 0).reshape(P, 132)
    cst[:, C_CB:C_CB + 44] = f(conv_b)[0].reshape(NFC, P).T

    in_maps = []
    for c in range(NCORES):
        b, hf = c // 2, c % 2
        if hf == 1:
            xwc = x[b]
        else:
            xwc = np.concatenate([x[b, :1024], x[b, :1024]], axis=0)
        cc = cst.copy()
        cc[:, C_FL] = 0.0 if hf == 1 else NEG
        cc[:, C_FL + 1] = 1.0 if hf == 1 else 0.0
        in_maps.append({
            "xw": np.ascontiguousarray(xwc), "pw": np.ascontiguousarray(p[0, b, hf * 1024:(hf + 1) * 1024].T.reshape(2, P, TO).transpose(1, 0, 2).reshape(P, 2 * TO)),
            "wstream": ws, "pproj": pproj, "gbc": gbc, "lngb": lngb, "wsT": wsT, "cst": cc, "bsrep": bsrep,
        })
    if "nc" not in _CACHE:
        _CACHE["nc"] = build_program()
    res = run_bass_kernel_spmd(_CACHE["nc"], in_maps, core_ids=list(range(NCORES)))
    out = np.empty((4, 2048, D), np.float32)
    for c in range(NCORES):
        b, hf = c // 2, c % 2
        out[b, hf * 1024:(hf + 1) * 1024] = res.results[c]["y"]
    return out
```

```python
import numpy as np
from contextlib import ExitStack

import concourse.bass as bass
import concourse.mybir as mybir
from concourse.bass_utils import run_bass_kernel_spmd

F32 = mybir.dt.float32
BF16 = mybir.dt.bfloat16
AF = mybir.ActivationFunctionType
ALU = mybir.AluOpType
AX = mybir.AxisListType

NCORES = 8
P = 128
D = 2048
KC = 16
TW = 2048
NB = 16
QB = 9
TQ = QB * P
TO = 1024
DFF = 5632
NFC = 44
NG = 22
H = 8
EPS = 1e-6
SLOT = 4096
NS = 4
NEG = -30000.0

C_TRI = 0
C_E63 = 128
C_ID = 256
C_BF = 384
C_GQ = 512
C_GK = 513
C_CW = 514
C_CB = 646
C_FL = 690
C_ONE = 692
CST_W = 820

Q_TILES = [(0, 512), (512, 512), (1024, 128)]


def _weight_stream_plan():
    plan = []
    for s in range(4):
        plan.append(("v", s))
    for s in range(4):
        plan.append(("u", s))
    for s in range(4):
        plan.append(("k", s))
    for s in range(4):
        plan.append(("vv", s))
    plan.append(("f", 0))
    for s in range(4):
        plan.append(("q", s))
    for c in range(16):
        plan.append(("ga", c))
        plan.append(("gb", c))
    for cg in range(4):
        plan.append(("wo", 2 * cg))
        plan.append(("wo", 2 * cg + 1))
    plan.append(("uc", 0))
    plan.append(("uc", 1))
    for g in range(1, NG):
        plan.append(("uc", 2 * g))
        plan.append(("dn", g - 1))
        plan.append(("uc", 2 * g + 1))
    plan.append(("dn", NG - 1))
    for cg in range(4):
        plan.append(("wg", 2 * cg))
        plan.append(("wg", 2 * cg + 1))
    return plan


PLAN = _weight_stream_plan()
NL = len(PLAN)


def _slot_elems(kind):
    if kind == "f":
        return 16 * 8
    if kind in ("ga", "gb"):
        return 24 * 128
    return 4096


def _build_wstream(w_in, w_branch_a, w_branch_b, w_out, w_up, w_down, w_ple_gate):
    ws = np.zeros((NL, P, SLOT), dtype=np.float32)

    def kcols(w, c0, n):
        K = w.shape[0]
        return w[:, c0:c0 + n].reshape(K // P, P, n).transpose(1, 0, 2)

    for i, (kind, j) in enumerate(PLAN):
        if kind == "v":
            t = kcols(w_in, 1024 + 256 * j, 256)
        elif kind == "u":
            t = kcols(w_in, 256 * j, 256)
        elif kind == "k":
            t = kcols(w_in, 3072 + 256 * j, 256)
        elif kind == "vv":
            t = kcols(w_in, 4096 + 256 * j, 256)
        elif kind == "f":
            t = kcols(w_in, 5120, 8)
        elif kind == "q":
            t = kcols(w_in, 2048 + 256 * j, 256)
        elif kind == "ga":
            t = np.concatenate([kcols(w_in, 5128 + 128 * j, 128), kcols(w_branch_a, 128 * j, 128)], axis=1)
        elif kind == "gb":
            t = np.concatenate([kcols(w_in, 7176 + 128 * j, 128), kcols(w_branch_b, 128 * j, 128)], axis=1)
        elif kind == "wo":
            cg, hf = j // 2, j % 2
            t = kcols(w_out, 512 * cg, 512)[:, 8 * hf:8 * hf + 8, :]
        elif kind == "uc":
            t = np.concatenate([kcols(w_up, 128 * j, 128), kcols(w_up, DFF + 128 * j, 128)], axis=1)
        elif kind == "dn":
            t = w_down[256 * j:256 * j + 256, :].reshape(2, P, D).transpose(1, 0, 2)
        elif kind == "wg":
            cg, hf = j // 2, j % 2
            t = kcols(w_ple_gate, 512 * cg, 512)[:, 8 * hf:8 * hf + 8, :]
        else:
            raise AssertionError(kind)
        t = t.reshape(P, -1)
        ws[i, :, :t.shape[1]] = t
    return ws


class Tracker:
    ENG = ("pe", "act", "dve", "pool", "sp")

    def __init__(self, nc, es):
        self.nc = nc
        self.es = es
        self.sem = {}
        self.cnt = {}
        self.streams = {e: [] for e in self.ENG}
        self.known = {e: {} for e in self.ENG}
        self.lw = {}
        self.rd = {}
        self.tensors = []
        self.atoms_of = {}
        self._inherit = {}
        for e in self.ENG[:4]:
            self._mksem(e)

    def _mksem(self, name):
        if name not in self.sem:
            self.sem[name] = self.es.enter_context(self.nc.semaphore("s_" + name))
            self.cnt[name] = 0
        return self.sem[name]

    def tensor(self, name, lo, hi):
        inherited = {}
        self.ghosts = getattr(self, "ghosts", [])
        for (glo, ghi, gd) in self.ghosts:
            if not (hi <= glo or lo >= ghi):
                for s, v in gd.items():
                    inherited[s] = max(inherited.get(s, 0), v)
        for t in self.tensors:
            if t[3] and not (hi <= t[1] or lo >= t[2]):
                t[3] = False
                gd = dict(self._inherit.get(t[0], {}))
                for a in self.atoms_of.get(t[0], ()):
                    w = self.lw.pop(a, None)
                    if w is not None:
                        gd[w[0]] = max(gd.get(w[0], 0), w[1])
                    for s, v in self.rd.pop(a, {}).items():
                        gd[s] = max(gd.get(s, 0), v)
                self.atoms_of.pop(t[0], None)
                self.ghosts.append((t[1], t[2], gd))
                for s, v in gd.items():
                    inherited[s] = max(inherited.get(s, 0), v)
        self.tensors.append([name, lo, hi, True])
        self.atoms_of[name] = set()
        self._inherit[name] = inherited

    def _touch(self, a):
        name = a[0]
        s = self.atoms_of.get(name)
        if s is not None and a not in s:
            s.add(a)
            inh = self._inherit.get(name)
            if inh:
                self.rd[a] = dict(inh)

    def _deps(self, eng, reads, writes):
        deps = {}

        def add(s, v, kind):
            if s == eng and (eng == "pe" or kind == "war"):
                return
            if v > deps.get(s, 0):
                deps[s] = v

        for a in reads:
            self._touch(a)
            w = self.lw.get(a)
            if w is not None:
                add(w[0], w[1], "raw")
        for a in writes:
            self._touch(a)
            w = self.lw.get(a)
            if w is not None:
                add(w[0], w[1], "waw")
            for s, v in self.rd.get(a, {}).items():
                add(s, v, "war")
        kn = self.known[eng]
        out = []
        for s, v in deps.items():
            if kn.get(s, 0) < v:
                kn[s] = v
                out.append((s, v))
        return out

    def _record(self, reads, writes, tag):
        for a in reads:
            r = self.rd.setdefault(a, {})
            if r.get(tag[0], 0) < tag[1]:
                r[tag[0]] = tag[1]
        for a in writes:
            self.lw[a] = tag
            self.rd[a] = {}

    def op(self, eng, fns, reads=(), writes=()):
        if not isinstance(fns, (list, tuple)):
            fns = [fns]
        waits = self._deps(eng, reads, writes)
        self.cnt[eng] += 1
        val = self.cnt[eng]
        sem = self.sem[eng]
        st = self.streams[eng]
        for s, v in waits:
            st.append(("w", self.sem[s], v))
        for f in fns[:-1]:
            st.append(("i", f, None, 0))
        st.append(("i", fns[-1], sem, 1))
        self._record(reads, writes, (eng, val))

    def dma(self, queue, slot, fn, reads=(), writes=(), extra=()):
        sem = self._mksem("d_" + slot)
        name = "d_" + slot
        waits = self._deps(queue, reads, writes)
        for s_, v in extra:
            if self.known[queue].get(s_, 0) < v:
                self.known[queue][s_] = v
                waits.append((s_, v))
        self.cnt[name] += 16
        val = self.cnt[name]
        st = self.streams[queue]
        for s, v in waits:
            st.append(("w", self.sem[s], v))
        st.append(("i", fn, sem, 16))
        self._record(reads, writes, (name, val))

    def merge(self, dst, srcs):
        r = self.rd.setdefault(dst, {})
        for a in srcs:
            w = self.lw.get(a)
            if w is not None and r.get(w[0], 0) < w[1]:
                r[w[0]] = w[1]
            for s_, v in self.rd.get(a, {}).items():
                if r.get(s_, 0) < v:
                    r[s_] = v

    def wait_all(self, eng, atoms):
        waits = self._deps(eng, atoms, ())
        for s, v in waits:
            self.streams[eng].append(("w", self.sem[s], v))

    def replay(self, eng, e):
        for it in self.streams[eng]:
            if it[0] == "w":
                e.wait_ge(it[1], it[2])
            else:
                ins = it[1](e)
                if it[2] is not None:
                    ins.then_inc(it[2], it[3])


def build_program():
    nc = bass.Bass("TRN2", target_bir_lowering=False)
    xw = nc.dram_tensor("xw", [TW, D], F32, kind="ExternalInput").ap()
    pw = nc.dram_tensor("pw", [P, 2 * TO], F32, kind="ExternalInput").ap()
    wstream = nc.dram_tensor("wstream", [NL, P, SLOT], F32, kind="ExternalInput").ap()
    pproj_d = nc.dram_tensor("pproj", [P, 2 * D], F32, kind="ExternalInput").ap()
    gbc_d = nc.dram_tensor("gbc", [4, P, D], F32, kind="ExternalInput").ap()
    lngb_d = nc.dram_tensor("lngb", [2, P, 1024], F32, kind="ExternalInput").ap()
    wsT_d = nc.dram_tensor("wsT", [P, 1024], F32, kind="ExternalInput").ap()
    cst_d = nc.dram_tensor("cst", [P, CST_W], F32, kind="ExternalInput").ap()
    bsrep_d = nc.dram_tensor("bsrep", [P, 1024], F32, kind="ExternalInput").ap()
    y = nc.dram_tensor("y", [TO, D], F32, kind="ExternalOutput").ap()

    es = ExitStack()
    with es:
        ARENA_B = 165888
        arena = es.enter_context(nc.sbuf_tensor("arena", [P, ARENA_B // 2], BF16))
        ring = es.enter_context(nc.sbuf_tensor("ring", [P, NS, SLOT], BF16))
        cst = es.enter_context(nc.sbuf_tensor("cst_sb", [P, CST_W], F32))
        cbf = es.enter_context(nc.sbuf_tensor("cbf", [P, 4, 128], BF16))
        st = es.enter_context(nc.sbuf_tensor("stats", [P, 160], F32))
        fxob = es.enter_context(nc.sbuf_tensor("fxob", [P, 1024], F32))
        fx = fxob[:, :].rearrange("p (i n) -> p i n", i=8)
        biasK = es.enter_context(nc.sbuf_tensor("biasK", [P, NB, H, 3], F32))
        sel = es.enter_context(nc.sbuf_tensor("sel", [P, H, 128], BF16))
        rsb = es.enter_context(nc.sbuf_tensor("rsb", [P, 4], F32))
        gq2 = es.enter_context(nc.sbuf_tensor("gq2", [P, 2], F32))
        ps = es.enter_context(nc.psum_tensor("ps", [P, 8, 512], F32))

        tr = Tracker(nc, es)

        def bf(lo, n):
            return arena[:, lo // 2: lo // 2 + n]

        def f32(lo, n):
            return arena[:, lo // 2: lo // 2 + 2 * n].bitcast(F32)

        ZA, ZB, ZC, ZD, ZE = 0, 36864, 73728, 110592, 147456
        c32 = cst[:, C_ONE:C_ONE + 128]
        wsb = bf(ZB + 32768, 1024).rearrange("p (g t) -> p g t", g=8)
        bsh = bf(ZB + 28672, 2048).rearrange("p (i n) -> p i n", i=2)

        ident = cbf[:, 0, :]
        ones_bf = cbf[:, 1, :]
        tri_bf = cbf[:, 2, :]

        bank_rr = [0]

        def nbank():
            b = bank_rr[0]
            bank_rr[0] = (b + 1) % 8
            return b

        ring_pos = [0]
        xb_tags = []

        def load_w(expect_kind):
            i = ring_pos[0]
            kind, j = PLAN[i]
            assert kind == expect_kind, (kind, expect_kind)
            r = i % NS
            n = _slot_elems(kind)
            ring_pos[0] += 1
            extra = []
            if i < 4 and len(xb_tags) > 3 * i + 1:
                extra = [xb_tags[3 * i + 1]]
            tr.dma("pool", "ring%d" % r,
                   lambda e, r=r, i=i, n=n: e.dma_start(out=ring[:, r, 0:n], in_=wstream[i, :, 0:n]),
                   reads=(), writes=[("ring", r)], extra=extra)
            return r

        def act(fn, reads, writes):
            tr.op("act", fn, reads, writes)

        def dve(fn, reads, writes):
            tr.op("dve", fn, reads, writes)

        def pe(fns, reads, writes):
            tr.op("pe", fns, reads, writes)

        def mm(out, lhsT, rhs, start, stop):
            return lambda e: e.matmul(out, lhsT, rhs, start=start, stop=stop)

        def tp(out, in_):
            return lambda e: e.transpose(out, in_, ident)

        tr.dma("sp", "cst", lambda e: e.dma_start(out=cst[:], in_=cst_d[:, :]), writes=[("cst",)])
        tr.tensor("wsf", ZC, ZC + 4096)
        tr.tensor("bstmp", ZC + 4096, ZC + 8192)
        tr.tensor("bsf", ZC + 8192, ZC + 12288)
        tr.tensor("bsh", ZB + 28672, ZB + 32768)
        tr.tensor("wsb", ZB + 32768, ZB + 34816)
        wsf = f32(ZC, 1024).rearrange("p (g t) -> p g t", g=8)
        bst = f32(ZC + 4096, 1024)
        bsf = f32(ZC + 8192, 1024)
        tr.dma("sp", "wsf", lambda e: e.dma_start(out=f32(ZC, 1024), in_=wsT_d[:, :]), writes=[("wsf",)])
        tr.dma("sp", "bsf", lambda e: e.dma_start(out=bsf, in_=bsrep_d[:, :]), writes=[("bsf",)])
        dve(lambda e: e.tensor_copy(out=ident, in_=cst[:, C_ID:C_ID + 128]), [("cst",)], [("ident",)])
        dve(lambda e: e.memset(ones_bf, 1.0), [], [("ones_bf",)])
        dve(lambda e: e.tensor_scalar(out=tri_bf, in0=cst[:, C_TRI:C_TRI + 128], scalar1=-1.0, scalar2=30000.0, op0=ALU.add, op1=ALU.mult),
            [("cst",)], [("tri_bf",)])
        for h in range(H):
            dve(lambda e, h=h: e.tensor_scalar(out=sel[:, h, :], in0=c32, scalar1=cst[:, C_ID + h:C_ID + h + 1], scalar2=None, op0=ALU.mult),
                [("cst",)], [("sel",)])
        for g in range(8):
            dve(lambda e, g=g: e.tensor_tensor(out=wsb[:, g, :], in0=wsf[:, g, :], in1=cst[:, C_TRI:C_TRI + 128], op=ALU.mult),
                [("wsf",), ("cst",)], [("wsb",)])
        dve(lambda e: e.tensor_copy(out=bsh[:, 0, :], in_=bsf), [("bsf",)], [("bsh", 0)])
        dve(lambda e: e.tensor_tensor(out=bst, in0=bsf, in1=bsh[:, 0, :], op=ALU.subtract), [("bsf",), ("bsh", 0)], [("bstmp",)])
        dve(lambda e: e.tensor_copy(out=bsh[:, 1, :], in_=bst), [("bstmp",)], [("bsh", 1)])
        dve(lambda e: e.tensor_scalar(out=cbf[:, 3, :], in0=c32, scalar1=cst[:, C_ID:C_ID + 1], scalar2=None, op0=ALU.mult),
            [("cst",)], [("e0",)])
        dve(lambda e: e.tensor_scalar(out=gq2[:, 0:1], in0=cst[:, C_GQ:C_GQ + 1], scalar1=float(128 ** -0.5), scalar2=None, op0=ALU.mult),
            [("cst",)], [("gq2",)])
        dve(lambda e: e.tensor_copy(out=gq2[:, 1:2], in_=cst[:, C_GK:C_GK + 1]), [("cst",)], [("gq2",)])

        S_SS, S_RS = 0, 16
        S_V1, S_VM, S_VQ = 32, 68, 80
        S_REC = 96
        S_ES, S_ER = 104, 136

        def norm_A(it):
            if it.get("pre"):
                it["pre"]()
            scol = it["scol"]
            src_ap, xn, junk, gb = it["src"], it["xn"], it["junk"], it["gb"]
            act(lambda e: e.activation(out=junk, in_=src_ap, func=AF.Square, accum_out=st[:, S_SS + scol:S_SS + scol + 1]),
                it["src_atoms"], list(it["junk_atoms"]) + [("st", S_SS + scol)])
            act(lambda e: e.activation(out=st[:, S_RS + scol:S_RS + scol + 1], in_=st[:, S_SS + scol:S_SS + scol + 1],
                                       func=AF.Sqrt, scale=1.0 / D, bias=EPS),
                [("st", S_SS + scol)], [("st", S_RS + scol)])
            dve(lambda e: e.reciprocal(out=st[:, S_RS + scol:S_RS + scol + 1], in_=st[:, S_RS + scol:S_RS + scol + 1]),
                [("st", S_RS + scol)], [("st", S_RS + scol)])
            dve(lambda e: e.scalar_tensor_tensor(out=xn, in0=src_ap, scalar=st[:, S_RS + scol:S_RS + scol + 1], in1=gb,
                                                 op0=ALU.mult, op1=ALU.mult),
                list(it["src_atoms"]) + [("st", S_RS + scol), it["gb_atom"]], list(it["xn_atoms"]))

        def norm_B(it):
            xn, dst_fn, dst_atoms = it["xn"], it["dst_fn"], it["dst_atoms"]
            for half in range(2):
                b = nbank()
                pb = ps[:, b, :].bitcast(BF16)
                pe([tp(pb[:, k * 128:(k + 1) * 128], xn[:, (half * 8 + k) * 128:(half * 8 + k + 1) * 128]) for k in range(8)],
                   list(it["xn_atoms"]) + [("ident",)], [("ps", b)])
                if half == 0:
                    act(lambda e, pb=pb, half=half: e.activation(out=dst_fn(half * 8), in_=pb.rearrange("p (k t) -> p k t", k=8), func=AF.Copy),
                        [("ps", b)], dst_atoms(half * 8))
                else:
                    dve(lambda e, pb=pb, half=half: e.tensor_copy(out=dst_fn(half * 8), in_=pb.rearrange("p (k t) -> p k t", k=8)),
                        [("ps", b)], dst_atoms(half * 8))

        def norm_phase(items, L=2, hook=None):
            n = len(items)
            for i in range(n + L):
                if i < n:
                    norm_A(items[i])
                if i - L >= 0:
                    norm_B(items[i - L])
                if hook is not None:
                    hook(i)

        tr.tensor("hq", ZA, ZA + 36864)
        tr.tensor("hc", ZB, ZB + 28672)
        hq = bf(ZA, KC * TQ).rearrange("p (k t) -> p k t", k=KC)
        hc = bf(ZB, KC * 896).rearrange("p (k t) -> p k t", k=KC)
        tr.tensor("p1tmp", ZD, ZD + 36864)
        tr.tensor("p1tmp2", ZC + 12288, ZC + 24576)
        xblk = [f32(ZD + i * 8192, D) for i in range(3)]
        xnb = [bf(ZD + 24576 + i * 4096, D) for i in range(3)]
        gb1 = f32(ZC + 12288, D)
        junk1 = bf(ZC + 20480, D)
        tr.dma("sp", "gb1", lambda e: e.dma_start(out=gb1, in_=gbc_d[0, :, :]), writes=[("p1tmp2", "gb")])

        def hwin(kc, wb):
            if wb < 7:
                return hc[:, kc, wb * 128:(wb + 1) * 128]
            return hq[:, kc, (wb - 7) * 128:(wb - 6) * 128]

        def hwin_atom(kc, wb):
            return ("hc", kc, wb) if wb < 7 else ("hq", kc, wb - 7)

        items = []
        for n_i, wb in enumerate(list(range(7, NB)) + list(range(0, 7))):
            xi = n_i % 3
            xb = xblk[xi]
            if wb < 7:
                dst = lambda k0, wb=wb: hc[:, k0:k0 + 8, wb * 128:(wb + 1) * 128]
            else:
                dst = lambda k0, wb=wb: hq[:, k0:k0 + 8, (wb - 7) * 128:(wb - 6) * 128]
            items.append(dict(
                pre=lambda xb=xb, wb=wb, xi=xi: (tr.dma("sp", "xb%d" % xi, lambda e: e.dma_start(out=xb, in_=xw[wb * 128:(wb + 1) * 128, :]),
                                                       writes=[("p1tmp", "xb", xi)]), xb_tags.append(tr.lw[("p1tmp", "xb", xi)])),
                src=xb, src_atoms=[("p1tmp", "xb", xi)], gb=gb1, gb_atom=("p1tmp2", "gb"),
                xn=xnb[xi], xn_atoms=[("p1tmp", "xn", xi)], junk=junk1, junk_atoms=[("p1tmp2", "junk")],
                dst_fn=dst, dst_atoms=lambda k0, wb=wb: [hwin_atom(k0 + k, wb) for k in range(8)], scol=n_i))
        norm_phase(items)

        tr.tensor("gv", ZC, ZC + 36864)
        tr.tensor("vn", ZD, ZD + 18432)
        tr.tensor("lngb", ZE, ZE + 8192)
        gv = f32(ZC, QB * 1024).rearrange("p (b n) -> p b n", b=QB)
        vn = bf(ZD, QB * 1024).rearrange("p (b n) -> p b n", b=QB)
        lnG = f32(ZE, 1024)
        lnB = f32(ZE + 4096, 1024)
        tr.dma("sp", "lng", lambda e: e.dma_start(out=lnG, in_=lngb_d[0, :, :]), writes=[("lngb", 0)])
        tr.dma("sp", "lnb", lambda e: e.dma_start(out=lnB, in_=lngb_d[1, :, :]), writes=[("lngb", 1)])
        S_M2 = 144

        def ln_block(qb):
            gva = [("gv", qb, s) for s in range(4)]
            vm, vq, m2 = st[:, S_VM + qb:S_VM + qb + 1], st[:, S_VQ + qb:S_VQ + qb + 1], st[:, S_M2 + qb:S_M2 + qb + 1]
            dve(lambda e: e.tensor_reduce(out=vm, in_=st[:, S_V1 + qb * 4:S_V1 + qb * 4 + 4], axis=AX.X, op=ALU.add),
                [("st", S_V1 + qb * 4 + s) for s in range(4)], [("st", S_VM + qb)])
            dve(lambda e: e.tensor_scalar(out=vm, in0=vm, scalar1=-1.0 / 1024, scalar2=None, op0=ALU.mult),
                [("st", S_VM + qb)], [("st", S_VM + qb)])
            act(lambda e: e.activation(out=vn[:, qb, :], in_=gv[:, qb, :], func=AF.Square, accum_out=vq),
                gva, [("vn", qb), ("st", S_VQ + qb)])
            dve(lambda e: e.tensor_tensor(out=m2, in0=vm, in1=vm, op=ALU.mult), [("st", S_VM + qb)], [("st", S_M2 + qb)])
            dve(lambda e: e.tensor_scalar(out=vq, in0=vq, scalar1=1.0 / 1024, scalar2=m2, op0=ALU.mult, op1=ALU.subtract),
                [("st", S_VQ + qb), ("st", S_M2 + qb)], [("st", S_VQ + qb)])
            act(lambda e: e.activation(out=vq, in_=vq, func=AF.Sqrt, scale=1.0, bias=EPS), [("st", S_VQ + qb)], [("st", S_VQ + qb)])
            dve(lambda e: e.reciprocal(out=vq, in_=vq), [("st", S_VQ + qb)], [("st", S_VQ + qb)])
            dve(lambda e: e.scalar_tensor_tensor(out=gv[:, qb, :], in0=gv[:, qb, :], scalar=vm, in1=lnG, op0=ALU.add, op1=ALU.mult),
                gva + [("st", S_VM + qb), ("lngb", 0)], gva)
            dve(lambda e: e.scalar_tensor_tensor(out=vn[:, qb, :], in0=gv[:, qb, :], scalar=vq, in1=lnB, op0=ALU.mult, op1=ALU.add),
                gva + [("st", S_VQ + qb), ("lngb", 1)], [("vn", qb)])

        for s in range(4):
            r = load_w("v")
            wv = ring[:, r, :].rearrange("p (k n) -> p k n", k=KC)
            for qb in range(QB):
                b = nbank()
                pe([mm(ps[:, b, 0:256], hq[:, kc, qb * 128:(qb + 1) * 128], wv[:, kc, :], kc == 0, kc == KC - 1) for kc in range(KC)],
                   [("ring", r)] + [("hq", kc, qb) for kc in range(KC)], [("ps", b)])
                act(lambda e, b=b, qb=qb, s=s: e.activation(out=gv[:, qb, s * 256:(s + 1) * 256], in_=ps[:, b, 0:256], func=AF.Gelu,
                                                            accum_out=st[:, S_V1 + qb * 4 + s:S_V1 + qb * 4 + s + 1]),
                    [("ps", b)], [("gv", qb, s), ("st", S_V1 + qb * 4 + s)])
                if s == 3:
                    ln_block(qb)
        tr.tensor("oa", ZE, ZE + 18432)
        oa = bf(ZE, 8 * TQ).rearrange("p (g t) -> p g t", g=8)
        for s in range(4):
            r = load_w("u")
            wu = ring[:, r, :].rearrange("p (k n) -> p k n", k=KC)
            for gg in range(2):
                g = 2 * s + gg
                for ti, (t0, tn) in enumerate(Q_TILES):
                    b = nbank()
                    pe([mm(ps[:, b, 0:tn], wu[:, kc, gg * 128:(gg + 1) * 128], hq[:, kc, t0:t0 + tn], kc == 0, kc == KC - 1) for kc in range(KC)],
                       [("ring", r)] + [("hq", kc, qb) for kc in range(KC) for qb in range(t0 // 128, (t0 + tn) // 128)], [("ps", b)])
                    act(lambda e, b=b, g=g, t0=t0, tn=tn: e.activation(out=oa[:, g, t0:t0 + tn], in_=ps[:, b, 0:tn], func=AF.Gelu),
                        [("ps", b)], [("oa", g, qb) for qb in range(t0 // 128, (t0 + tn) // 128)])

        for qb in range(QB):
            for g0 in (0, 4):
                b = nbank()
                fns = []
                for g in range(g0, g0 + 4):
                    o = ps[:, b, (g - g0) * 128:(g - g0 + 1) * 128]
                    fns.append(mm(o, vn[:, qb, g * 128:(g + 1) * 128], wsb[:, g, :], True, False))
                    fns.append(mm(o, cbf[:, 3, :], bsh[:, 0, g * 128:(g + 1) * 128], False, False))
                    fns.append(mm(o, cbf[:, 3, :], bsh[:, 1, g * 128:(g + 1) * 128], False, True))
                pe(fns, [("vn", qb), ("wsb",), ("bsh", 0), ("bsh", 1), ("e0",)], [("ps", b)])
                dve(lambda e, b=b, g0=g0, qb=qb: e.tensor_tensor(out=oa[:, g0:g0 + 4, qb * 128:(qb + 1) * 128],
                                                                  in0=ps[:, b, :].rearrange("p (g t) -> p g t", g=4),
                                                                  in1=oa[:, g0:g0 + 4, qb * 128:(qb + 1) * 128], op=ALU.mult),
                    [("ps", b)] + [("oa", g, qb) for g in range(g0, g0 + 4)], [("oa", g, qb) for g in range(g0, g0 + 4)])

        tr.tensor("kT", ZC, ZC + 32768)
        tr.tensor("va", ZD, ZD + 33280)
        tr.tensor("rtm", ZC + 32768, ZC + 36864)
        tr.tensor("atmp", ZD + 33280, ZD + 36864)
        tr.tensor("fx", 10 ** 6, 10 ** 6 + 4096)
        kT = bf(ZC, H * TW).rearrange("p (h t) -> p h t", h=H)
        va = bf(ZD, NB * H * 130).rearrange("p (b h d) -> p b h d", b=NB, h=H)
        sqt = [bf(ZD + 33280 + i * 1024, 512) for i in range(2)]
        rtm = [f32(ZC + 32768 + i * 2048, 512) for i in range(2)]
        ptb = [bf(ZD + 35328 + i * 256, 128) for i in range(6)]
        W_TILES = [(0, 512), (512, 384)]

        qk_pend = [None]
        qk_cnt = [0]

        def qk_norm_tile(b, n, gcol, out_ap, out_atoms, cnt=None):
            i2 = qk_cnt[0] % 2
            qk_cnt[0] += 1
            act(lambda e: e.activation(out=sqt[i2][:, 0:n], in_=ps[:, b, 0:n], func=AF.Square), [("ps", b)], [("atmp", "sq", i2)])

            def tail():
                b2 = nbank()
                pe([mm(ps[:, b2, 0:n], ones_bf, sqt[i2][:, 0:n], True, True)], [("atmp", "sq", i2), ("ones_bf",)], [("ps", b2)])
                act(lambda e: e.activation(out=rtm[i2][:, 0:n], in_=ps[:, b2, 0:n], func=AF.Ln, scale=1.0 / 128, bias=EPS),
                    [("ps", b2)], [("rtm", i2)])
                act(lambda e: e.activation(out=rtm[i2][:, 0:n], in_=rtm[i2][:, 0:n], func=AF.Exp, scale=-0.5), [("rtm", i2)], [("rtm", i2)])
                dve(lambda e: e.scalar_tensor_tensor(out=out_ap, in0=ps[:, b, 0:n], scalar=gq2[:, gcol:gcol + 1], in1=rtm[i2][:, 0:n],
                                                     op0=ALU.mult, op1=ALU.mult),
                    [("ps", b), ("rtm", i2), ("gq2",)], out_atoms)

            if qk_pend[0] is not None:
                qk_pend[0]()
            qk_pend[0] = tail

        def qk_flush():
            if qk_pend[0] is not None:
                qk_pend[0]()
                qk_pend[0] = None

        cnt = 0
        for s in range(4):
            r = load_w("k")
            wk = ring[:, r, :].rearrange("p (k n) -> p k n", k=KC)
            for hh in range(2):
                h = 2 * s + hh
                tiles = [("c", t0, tn) for (t0, tn) in W_TILES] + [("q", t0, tn) for (t0, tn) in Q_TILES]
                for (src, t0, tn) in tiles:
                    b = nbank()
                    if src == "c":
                        rhs = lambda kc: hc[:, kc, t0:t0 + tn]
                        ratoms = [("hc", kc, wb) for kc in range(KC) for wb in range(t0 // 128, (t0 + tn) // 128)]
                        w0 = t0
                    else:
                        rhs = lambda kc: hq[:, kc, t0:t0 + tn]
                        ratoms = [("hq", kc, qb) for kc in range(KC) for qb in range(t0 // 128, (t0 + tn) // 128)]
                        w0 = 896 + t0
                    pe([mm(ps[:, b, 0:tn], wk[:, kc, hh * 128:(hh + 1) * 128], rhs(kc), kc == 0, kc == KC - 1) for kc in range(KC)],
                       [("ring", r)] + ratoms, [("ps", b)])
                    qk_norm_tile(b, tn, 1, kT[:, h, w0:w0 + tn], [("kT", h, wb) for wb in range(w0 // 128, (w0 + tn) // 128)], cnt)
                    cnt += 1
        qk_flush()
        dve(lambda e: e.memset(va[:, :, :, 128:130], 1.0), [], [("va", "ones")])
        for s in range(4):
            r = load_w("vv")
            wvv = ring[:, r, :].rearrange("p (k n) -> p k n", k=KC)
            for wb in range(NB):
                b = nbank()
                pe([mm(ps[:, b, 0:256], hwin(kc, wb), wvv[:, kc, :], kc == 0, kc == KC - 1) for kc in range(KC)],
                   [("ring", r)] + [hwin_atom(kc, wb) for kc in range(KC)], [("ps", b)])
                act(lambda e, b=b, wb=wb, s=s: e.activation(out=va[:, wb, 2 * s:2 * s + 2, 0:128],
                                                            in_=ps[:, b, 0:256].rearrange("p (h d) -> p h d", h=2), func=AF.Copy),
                    [("ps", b)], [("va", wb, 2 * s), ("va", wb, 2 * s + 1)])
        r = load_w("f")
        wf = ring[:, r, 0:128].rearrange("p (k n) -> p k n", k=KC)
        bF = nbank()
        fns = []
        for wb in range(NB):
            for kc in range(KC):
                fns.append(mm(ps[:, bF, wb * 8:(wb + 1) * 8], hwin(kc, wb), wf[:, kc, :], kc == 0, kc == KC - 1))
        pe(fns, [("ring", r)] + [hwin_atom(kc, wb) for kc in range(KC) for wb in range(NB)], [("ps", bF)])
        nl, cin, tot, mid, pre, cumN, refN = (fx[:, i, :] for i in range(7))
        dve(lambda e: e.tensor_tensor(out=nl, in0=ps[:, bF, 0:128], in1=cst[:, C_BF:C_BF + 128], op=ALU.add), [("ps", bF), ("cst",)], [("fx", 0)])
        act(lambda e: e.activation(out=nl, in_=nl, func=AF.Exp, scale=-1.0), [("fx", 0)], [("fx", 0)])
        act(lambda e: e.activation(out=nl, in_=nl, func=AF.Ln, bias=1.0), [("fx", 0)], [("fx", 0)])
        for i, lhs in ((1, cst[:, C_TRI:C_TRI + 128]), (2, c32), (3, cst[:, C_E63:C_E63 + 128])):
            b = nbank()
            pe([mm(ps[:, b, 0:128], lhs, nl, True, True)], [("fx", 0), ("cst",)], [("ps", b)])
            act(lambda e, b=b, i=i: e.activation(out=fx[:, i, :], in_=ps[:, b, 0:128], func=AF.Copy), [("ps", b)], [("fx", i)])
        dve(lambda e: e.memset(pre[:, 0:8], 0.0), [], [("fx", 4)])
        for wb in range(1, NB):
            dve(lambda e, wb=wb: e.tensor_tensor(out=pre[:, wb * 8:(wb + 1) * 8], in0=pre[:, (wb - 1) * 8:wb * 8], in1=tot[:, (wb - 1) * 8:wb * 8], op=ALU.add),
                [("fx", 4), ("fx", 2)], [("fx", 4)])
        dve(lambda e: e.tensor_tensor(out=cumN, in0=cin, in1=pre, op=ALU.add), [("fx", 1), ("fx", 4)], [("fx", 5)])
        dve(lambda e: e.tensor_tensor(out=refN, in0=mid, in1=pre, op=ALU.add), [("fx", 3), ("fx", 4)], [("fx", 6)])
        WBMID = [7, 10, 14]
        cum3 = cumN.rearrange("p (i h) -> p i h", h=H)
        for T in range(3):
            dve(lambda e, T=T: e.tensor_tensor(out=biasK[:, :, :, T], in0=cum3,
                                               in1=refN[:, WBMID[T] * 8:(WBMID[T] + 1) * 8].unsqueeze(1).to_broadcast([P, NB, H]), op=ALU.subtract),
                [("fx", 5), ("fx", 6)], [("biasK", T)])
            if T >= 1:
                dve(lambda e, T=T: e.tensor_scalar(out=biasK[:, 0:8, :, T], in0=biasK[:, 0:8, :, T], scalar1=cst[:, C_FL:C_FL + 1], scalar2=None, op0=ALU.add),
                    [("biasK", T), ("cst",)], [("biasK", T)])

        tr.tensor("qT", ZB, ZB + 18432)
        qT = bf(ZB, H * TQ).rearrange("p (h t) -> p h t", h=H)
        for s in range(4):
            r = load_w("q")
            wq = ring[:, r, :].rearrange("p (k n) -> p k n", k=KC)
            for hh in range(2):
                h = 2 * s + hh
                for (t0, tn) in Q_TILES:
                    b = nbank()
                    pe([mm(ps[:, b, 0:tn], wq[:, kc, hh * 128:(hh + 1) * 128], hq[:, kc, t0:t0 + tn], kc == 0, kc == KC - 1) for kc in range(KC)],
                       [("ring", r)] + [("hq", kc, qb) for kc in range(KC) for qb in range(t0 // 128, (t0 + tn) // 128)], [("ps", b)])
                    qk_norm_tile(b, tn, 0, qT[:, h, t0:t0 + tn], [("qT", h, qb) for qb in range(t0 // 128, (t0 + tn) // 128)], cnt)
                    cnt += 1

        qk_flush()
        tr.tensor("obT", ZB + 18432, ZB + 36864)
        tr.tensor("crow", ZD + 33280, ZD + 35584)
        tr.tensor("obh", ZD + 35584, ZD + 36608)
        tr.tensor("pt", ZC + 32768, ZC + 35840)
        obT = bf(ZB + 18432, H * TQ).rearrange("p (h t) -> p h t", h=H)
        crow = bf(ZD + 33280, TQ)
        obh = [bf(ZD + 35584 + i * 256, 128) for i in range(4)]
        ptb = [bf(ZC + 32768 + i * 1024, 512) for i in range(3)]
        TILES = [(0, 1), (1, 5), (5, 9)]
        idf = cst[:, C_ID:C_ID + 128]

        def tp32(out, in_):
            return lambda e: e.transpose(out, in_, idf)

        dve(lambda e: e.memset(crow, 0.0), [], [("crow",)])
        ba, bb, bc = nbank(), nbank(), nbank()
        pe([tp32(ps[0:8, ba, k * 128:(k + 1) * 128], cumN[:, (7 + k) * 8:(8 + k) * 8]) for k in range(4)], [("fx", 5), ("cst",)], [("ps", ba)])
        pe([tp32(ps[0:8, bb, k * 128:(k + 1) * 128], cumN[:, (11 + k) * 8:(12 + k) * 8]) for k in range(4)], [("fx", 5), ("cst",)], [("ps", bb)])
        pe([tp32(ps[0:8, bc, 0:128], cumN[:, 15 * 8:16 * 8])]
           + [tp32(ps[0:8, bc, (1 + T) * 128:(2 + T) * 128], refN[:, WBMID[T] * 8:(WBMID[T] + 1) * 8]) for T in range(3)],
           [("fx", 5), ("fx", 6), ("cst",)], [("ps", bc)])
        dve(lambda e: e.tensor_copy(out=rsb[0:8, 0:3], in_=ps[0:8, bc, 128:512].rearrange("p (t n) -> p t n", t=3)[:, :, 0]), [("ps", bc)], [("rsb",)])
        for qb in range(QB):
            bank, k = (ba, qb) if qb < 4 else ((bb, qb - 4) if qb < 8 else (bc, 0))
            T = 0 if qb == 0 else (1 if qb < 5 else 2)
            dve(lambda e, bank=bank, k=k, T=T, qb=qb: e.tensor_scalar(out=crow[0:8, qb * 128:(qb + 1) * 128], in0=ps[0:8, bank, k * 128:(k + 1) * 128],
                                                                       scalar1=-1.0, scalar2=rsb[0:8, T:T + 1], op0=ALU.mult, op1=ALU.add),
                [("ps", bank), ("rsb",), ("crow",)], [("crow",)])

        units = []
        for T, (q0, q1) in enumerate(TILES):
            for h in range(H):
                for i in range(7 + q1):
                    units.append((T, h, i, max(q0, i - 7), q1, q0))
        sbank = {}
        s_rr = [0]
        p_rr = [0]
        o_rr = [0]
        t_rr = [0]
        S_BANKS = (0, 1, 7)
        pbt6 = ps[:, 6, :].bitcast(BF16)
        for k in range(8):
            tr.merge(("pst", k), [("ps", 6)])

        def emit_S(u):
            T, h, i, qlo, q1, q0 = u
            n = (q1 - qlo) * 128
            b = S_BANKS[s_rr[0] % 3]
            s_rr[0] += 1
            sbank[u] = b
            diag = (i - 7) >= q0
            fns = [mm(ps[:, b, 0:n], kT[:, h, i * 128:(i + 1) * 128], qT[:, h, qlo * 128:q1 * 128], True, False),
                   mm(ps[:, b, 0:n], sel[:, h, :], crow[:, qlo * 128:q1 * 128], False, not diag)]
            if diag:
                fns.append(mm(ps[:, b, 0:128], ident, tri_bf, False, True))
            pe(fns, [("kT", h, i), ("crow",), ("sel",), ("ident",), ("tri_bf",)] + [("qT", h, qb) for qb in range(qlo, q1)], [("ps", b)])

        def emit_PV(u):
            T, h, i, qlo, q1, q0 = u
            n = (q1 - qlo) * 128
            b = sbank[u]
            pi = p_rr[0] % 3
            p_rr[0] += 1
            act(lambda e: e.activation(out=ptb[pi][:, 0:n], in_=ps[:, b, 0:n], func=AF.Exp, bias=biasK[:, i, h, T:T + 1]),
                [("ps", b), ("biasK", T)], [("pt", pi)])
            for qb in range(qlo, q1):
                j = 7 + qb
                bo = 2 + (qb - q0)
                pe([mm(ps[:, bo, 0:129], ptb[pi][:, (qb - qlo) * 128:(qb - qlo + 1) * 128], va[:, i, h, 0:129], i == 0, i == j)],
                   [("pt", pi), ("va", i, h), ("va", "ones")], [("ps", bo)])
                if i == j:
                    rc = S_REC + o_rr[0] % 8
                    oi = o_rr[0] % 4
                    o_rr[0] += 1
                    dve(lambda e, bo=bo, rc=rc: e.reciprocal(out=st[:, rc:rc + 1], in_=ps[:, bo, 128:129]), [("ps", bo)], [("st", rc)])
                    dve(lambda e, bo=bo, rc=rc, oi=oi: e.tensor_scalar(out=obh[oi], in0=ps[:, bo, 0:128], scalar1=st[:, rc:rc + 1], scalar2=None, op0=ALU.mult),
                        [("ps", bo), ("st", rc)], [("obh", oi)])
                    k = t_rr[0] % 8
                    t_rr[0] += 1

                    def fin(k=k, oi=oi, h=h, qb=qb):
                        pe([tp(pbt6[:, k * 128:(k + 1) * 128], obh[oi])], [("obh", oi), ("ident",)], [("pst", k)])
                        dve(lambda e: e.tensor_copy(out=obT[:, h, qb * 128:(qb + 1) * 128], in_=pbt6[:, k * 128:(k + 1) * 128]),
                            [("pst", k)], [("obT", h, qb)])
                    fin_pend.append(fin)

        fin_pend = []
        emit_S(units[0])
        emit_S(units[1])
        for ui, u in enumerate(units):
            if ui + 2 < len(units):
                emit_S(units[ui + 2])
            ready = list(fin_pend)
            del fin_pend[:]
            emit_PV(u)
            for f_ in ready:
                f_()
        for f_ in fin_pend:
            f_()
        tr.merge(("ps", 6), [("pst", k) for k in range(8)])

        tr.tensor("yT", ZC, ZC + 36864)
        tr.tensor("gtmp", ZB, ZB + 8192)
        yT = bf(ZC, KC * TQ).rearrange("p (k t) -> p k t", k=KC)
        sgt = [f32(ZB + i * 2048, 512) for i in range(4)]
        gcnt = 0
        for c in range(16):
            ra = load_w("ga")
            rb = load_w("gb")
            wa = ring[:, ra, 0:3072].rearrange("p (k n) -> p k n", k=24)
            wb_ = ring[:, rb, 0:3072].rearrange("p (k n) -> p k n", k=24)
            for (t0, tn) in ((126, 342), (468, 342), (810, 342)):
                qbs = range(t0 // 128, (t0 + tn - 1) // 128 + 1)
                hqa = [("hq", kc, qb) for kc in range(KC) for qb in qbs]
                b1, b2, b3, b4 = nbank(), nbank(), nbank(), nbank()
                pe([mm(ps[:, b1, 0:tn], wa[:, kc, :], hq[:, kc, t0:t0 + tn], kc == 0, kc == KC - 1) for kc in range(KC)],
                   [("ring", ra)] + hqa, [("ps", b1)])
                pe([mm(ps[:, b2, 0:tn], wa[:, 16 + kc, :], oa[:, kc, t0:t0 + tn], kc == 0, kc == 7) for kc in range(8)],
                   [("ring", ra)] + [("oa", g, qb) for g in range(8) for qb in qbs], [("ps", b2)])
                pe([mm(ps[:, b3, 0:tn], wb_[:, kc, :], hq[:, kc, t0:t0 + tn], kc == 0, kc == KC - 1) for kc in range(KC)],
                   [("ring", rb)] + hqa, [("ps", b3)])
                pe([mm(ps[:, b4, 0:tn], wb_[:, 16 + kc, :], obT[:, kc, t0:t0 + tn], kc == 0, kc == 7) for kc in range(8)],
                   [("ring", rb)] + [("obT", hh, qb) for hh in range(8) for qb in qbs], [("ps", b4)])
                s1, s2 = sgt[(gcnt % 2) * 2], sgt[(gcnt % 2) * 2 + 1]
                a1, a2 = ("gtmp", (gcnt % 2) * 2), ("gtmp", (gcnt % 2) * 2 + 1)
                gcnt += 1
                act(lambda e, b1=b1, s1=s1, tn=tn: e.activation(out=s1[:, 0:tn], in_=ps[:, b1, 0:tn], func=AF.Sigmoid), [("ps", b1)], [a1])
                act(lambda e, b3=b3, s2=s2, tn=tn: e.activation(out=s2[:, 0:tn], in_=ps[:, b3, 0:tn], func=AF.Sigmoid), [("ps", b3)], [a2])
                dve(lambda e, b2=b2, s1=s1, tn=tn: e.tensor_tensor(out=s1[:, 0:tn], in0=ps[:, b2, 0:tn], in1=s1[:, 0:tn], op=ALU.mult), [("ps", b2), a1], [a1])
                dve(lambda e, b4=b4, s2=s2, tn=tn: e.tensor_tensor(out=s2[:, 0:tn], in0=ps[:, b4, 0:tn], in1=s2[:, 0:tn], op=ALU.mult), [("ps", b4), a2], [a2])
                dve(lambda e, s1=s1, s2=s2, c=c, t0=t0, tn=tn: e.tensor_tensor(out=yT[:, c, t0:t0 + tn], in0=s1[:, 0:tn], in1=s2[:, 0:tn], op=ALU.add),
                    [a1, a2], [("yT", c, qb) for qb in qbs])

        tr.tensor("x1", ZA, ZA + 65536)
        tr.tensor("xh", ZA + 65536, ZA + 73728)
        tr.tensor("xp", ZE, ZE + 8192)
        x1 = f32(ZA, 8 * D).rearrange("p (b n) -> p b n", b=8)
        xh = f32(ZA + 65536, D)
        xp = [f32(ZE + i * 2048, 512) for i in range(4)]
        xc = 0
        for cg in range(4):
            r0 = load_w("wo")
            r1 = load_w("wo")
            wo = [ring[:, r0, :].rearrange("p (k n) -> p k n", k=8), ring[:, r1, :].rearrange("p (k n) -> p k n", k=8)]
            for qb in range(QB):
                xi = xc % 4
                xc += 1
                tr.dma("sp", "xp%d" % xi,
                       lambda e, xi=xi, qb=qb, cg=cg: e.dma_start(out=xp[xi], in_=xw[(7 + qb) * 128:(8 + qb) * 128, cg * 512:(cg + 1) * 512]),
                       writes=[("xp", xi)])
                b = nbank()
                pe([mm(ps[:, b, :], yT[:, kc, qb * 128:(qb + 1) * 128], wo[kc // 8][:, kc % 8, :], kc == 0, kc == KC - 1) for kc in range(KC)],
                   [("ring", r0), ("ring", r1)] + [("yT", kc, qb) for kc in range(KC)], [("ps", b)])
                if qb == 0:
                    o, oat = xh[:, cg * 512:(cg + 1) * 512], ("xh", cg)
                else:
                    o, oat = x1[:, qb - 1, cg * 512:(cg + 1) * 512], ("x1", qb - 1, cg)
                dve(lambda e, b=b, o=o, xi=xi: e.tensor_tensor(out=o, in0=ps[:, b, :], in1=xp[xi], op=ALU.add), [("ps", b), ("xp", xi)], [oat])

        tr.tensor("h2", ZD, ZD + 36864)
        tr.tensor("ctmp", ZC, ZC + 36864)
        h2 = bf(ZD, KC * TQ).rearrange("p (k t) -> p k t", k=KC)
        xn2 = [bf(ZC, D), bf(ZC + 4096, D), bf(ZC + 24608 + 4096, D)]
        xn2_atoms = [[("ctmp", "xn", 0)], [("ctmp", "xn", 1)], [("ctmp", "tA", 1)]]
        junk2 = bf(ZC + 24608, D)
        gb2 = f32(ZC + 8192, D)
        tr.dma("sp", "gb2", lambda e: e.dma_start(out=gb2, in_=gbc_d[1, :, :]), writes=[("ctmp", "gb")])
        items = []
        for blk in range(QB):
            src = xh if blk == 0 else x1[:, blk - 1, :]
            satoms = [("xh", cg) for cg in range(4)] if blk == 0 else [("x1", blk - 1, cg) for cg in range(4)]
            items.append(dict(src=src, src_atoms=satoms, gb=gb2, gb_atom=("ctmp", "gb"), xn=xn2[blk % 3], xn_atoms=xn2_atoms[blk % 3],
                              junk=junk2, junk_atoms=[("ctmp", "tA", 0)],
                              dst_fn=lambda k0, blk=blk: h2[:, k0:k0 + 8, blk * 128:(blk + 1) * 128],
                              dst_atoms=lambda k0, blk=blk: [("h2", k0 + k, blk) for k in range(8)], scol=blk))
        norm_phase(items)

        a_sb = [f32(ZC + 16384 + i * 4112, 1028) for i in range(2)]
        tA = [f32(ZC + 24608 + i * 4096, 1024) for i in range(2)]
        tr.tensor("hid", ZE + 8192, ZE + 16384)
        hid = [bf(ZE + 8192 + i * 4096, 2048).rearrange("p (c t) -> p c t", c=2) for i in range(2)]
        tB = [f32(ZE + i * 4096, 1024) for i in range(2)]
        tr.tensor("tB", ZE, ZE + 8192)
        cw = cst[:, C_CW:C_CW + 132].rearrange("p (c j) -> p c j", j=3)
        cb = cst[:, C_CB:C_CB + 44]
        fcnt = [0]

        BA0, BA1, BAH, BB0, BB1 = 0, 1, 2, 3, 4
        d_rr = [0]

        def U_chunk_steps(g, cc):
            c = 2 * g + cc
            hb = hid[g % 2]
            i2 = c % 2
            asb, ta, tb_ = a_sb[i2], tA[i2], tB[i2]
            aat, tat, tbt = ("ctmp", "a", i2), ("ctmp", "tA", i2), ("tB", i2)
            box = {}

            def agroup(b, t0, tn):
                w = box["w"]
                pe([mm(ps[:, b, 0:tn], w[:, kc, :], h2[:, kc, t0:t0 + tn], kc == 0, kc == KC - 1) for kc in range(KC)],
                   [("ring", box["r"])] + [("h2", kc, blk) for kc in range(KC) for blk in range(t0 // 128, (t0 + tn - 1) // 128 + 1)], [("ps", b)])

            def bgroup(b, t0, tn):
                w = box["w"]
                pe([mm(ps[:, b, 0:tn], w[:, 16 + kc, :], h2[:, kc, t0:t0 + tn], kc == 0, kc == KC - 1) for kc in range(KC)],
                   [("ring", box["r"])] + [("h2", kc, blk) for kc in range(KC) for blk in range(t0 // 128, (t0 + tn) // 128)], [("ps", b)])

            def s0():
                box["r"] = load_w("uc")
                box["w"] = ring[:, box["r"], :].rearrange("p (k n) -> p k n", k=32)
                agroup(BA0, 126, 342)
                agroup(BA1, 468, 342)
                act(lambda e: e.activation(out=asb[:, 0:342], in_=ps[:, BA0, 0:342], func=AF.Copy), [("ps", BA0)], [aat])
                act(lambda e: e.activation(out=asb[:, 0:2], in_=asb[:, 0:2], func=AF.Copy, scale=cst[:, C_FL + 1:C_FL + 2]), [aat, ("cst",)], [aat])
                act(lambda e: e.activation(out=asb[:, 342:684], in_=ps[:, BA1, 0:342], func=AF.Copy), [("ps", BA1)], [aat])

            def s1():
                agroup(BAH, 810, 342)
                act(lambda e: e.activation(out=asb[:, 684:1026], in_=ps[:, BAH, 0:342], func=AF.Copy), [("ps", BAH)], [aat])

            def s1post():
                dve(lambda e: e.tensor_scalar(out=ta, in0=asb[:, 2:1026], scalar1=cw[:, c, 2:3], scalar2=cb[:, c:c + 1], op0=ALU.mult, op1=ALU.add),
                    [aat, ("cst",)], [tat])
                dve(lambda e: e.scalar_tensor_tensor(out=tb_, in0=asb[:, 1:1025], scalar=cw[:, c, 1:2], in1=ta, op0=ALU.mult, op1=ALU.add),
                    [aat, tat, ("cst",)], [tbt])
                dve(lambda e: e.scalar_tensor_tensor(out=ta, in0=asb[:, 0:1024], scalar=cw[:, c, 0:1], in1=tb_, op0=ALU.mult, op1=ALU.add),
                    [aat, tbt, ("cst",)], [tat])
                act(lambda e: e.activation(out=tb_, in_=ta, func=AF.Gelu_apprx_tanh), [tat], [tbt])

            def s2():
                bgroup(BB0, 128, 512)

            def s2post():
                dve(lambda e: e.tensor_tensor(out=hb[:, cc, 0:512], in0=ps[:, BB0, :], in1=tb_[:, 0:512], op=ALU.mult),
                    [("ps", BB0), tbt], [("hid", g % 2, cc, 0)])

            def s3():
                bgroup(BB1, 640, 512)

            def s3post():
                dve(lambda e: e.tensor_tensor(out=hb[:, cc, 512:1024], in0=ps[:, BB1, :], in1=tb_[:, 512:1024], op=ALU.mult),
                    [("ps", BB1), tbt], [("hid", g % 2, cc, 1)])

            return [(s0, None), (s1, s1post), (s2, s2post), (s3, s3post)]

        def D_steps(g):
            box = {}
            hb = hid[g % 2]

            def preload():
                if "r" not in box:
                    box["r"] = load_w("dn")
                    box["w"] = ring[:, box["r"], :].rearrange("p (c n) -> p c n", c=2)

            def piece(tb, cg):
                def f():
                    preload()
                    wd = box["w"]
                    b = 5 + d_rr[0] % 3
                    d_rr[0] += 1
                    pe([mm(ps[:, b, :], hb[:, cc, tb * 128:(tb + 1) * 128], wd[:, cc, cg * 512:(cg + 1) * 512], cc == 0, cc == 1) for cc in range(2)],
                       [("ring", box["r"])] + [("hid", g % 2, cc, tb // 4) for cc in range(2)], [("ps", b)])
                    dve(lambda e: e.tensor_tensor(out=x1[:, tb, cg * 512:(cg + 1) * 512], in0=ps[:, b, :],
                                                  in1=x1[:, tb, cg * 512:(cg + 1) * 512], op=ALU.add),
                        [("ps", b), ("x1", tb, cg)], [("x1", tb, cg)])
                return f
            return [preload] + [piece(tb, cg) for tb in range(8) for cg in range(4)]

        def p12_prelude():
            tr.tensor("h3", ZD, ZD + 32768)
            tr.tensor("c12", ZC, ZC + 36864)
            tr.tensor("pp", ZE, ZE + 8192)
            tr.dma("sp", "gb3", lambda e: e.dma_start(out=gb3, in_=gbc_d[2, :, :]), writes=[("c12", "gb")])
            tr.dma("pool", "pp", lambda e: e.dma_start(out=bf(ZE, 2 * D), in_=pproj_d[:, :]), writes=[("pp",)])
            tr.dma("pool", "pT", lambda e: e.dma_start(out=bf(ZC + 16384, 2 * TO), in_=pw[:, :]), writes=[("c12", "pT", tb) for tb in range(8)])

        h3 = bf(ZD, KC * TO).rearrange("p (k t) -> p k t", k=KC)
        xn3 = [bf(ZC, D), bf(ZC + 4096, D), bf(ZC + 27648, D)]
        xn3_atoms = [[("c12", "xn", 0)], [("c12", "xn", 1)], [("c12", "te", 0), ("c12", "te", 1)]]
        junk3 = bf(ZC + 23552, D)
        gb3 = f32(ZC + 8192, D)
        pT = bf(ZC + 16384, 2 * TO).rearrange("p (k t) -> p k t", k=2)
        pf = [f32(ZC + 20480 + i * 1024, 256) for i in range(2)]
        pb_ = [bf(ZC + 22528 + i * 512, 256) for i in range(2)]
        sg = [f32(ZC + 23552 + i * 2048, 512) for i in range(2)]
        te = [f32(ZC + 27648 + i * 2048, 512) for i in range(2)]
        gpl = [f32(ZC + 31744 + i * 2048, 512) for i in range(2)]
        ppj = bf(ZE, 2 * D).rearrange("p (k n) -> p k n", k=2)

        for g in range(NG + 1):
            usteps = (U_chunk_steps(g, 0) + U_chunk_steps(g, 1)) if g < NG else []
            dsteps = D_steps(g - 1) if g >= 1 else []
            if dsteps:
                dpre, dsteps = dsteps[0], dsteps[1:]
            if g == NG:
                dpre()
                p12_prelude()
            if usteps:
                per = len(dsteps) // len(usteps)
                for i, (u, post) in enumerate(usteps):
                    u()
                    for d in dsteps[i * per:(i + 1) * per]:
                        d()
                    if post is not None:
                        post()
            else:
                for d in dsteps:
                    d()

        def e_stats(tb):
            for cg in range(4):
                b = nbank()
                pe([mm(ps[:, b, :], pT[:, k, tb * 128:(tb + 1) * 128], ppj[:, k, cg * 512:(cg + 1) * 512], k == 0, k == 1) for k in range(2)],
                   [("c12", "pT", tb), ("pp",)], [("ps", b)])
                i2 = (tb * 4 + cg) % 2
                act(lambda e, b=b, tb=tb, cg=cg, i2=i2: e.activation(out=gpl[i2], in_=ps[:, b, :], func=AF.Square,
                                                                      accum_out=st[:, S_ES + tb * 4 + cg:S_ES + tb * 4 + cg + 1]),
                    [("ps", b)], [("c12", "gpl", i2), ("st", S_ES + tb * 4 + cg)])
            dve(lambda e, tb=tb: e.tensor_reduce(out=st[:, S_ER + tb:S_ER + tb + 1], in_=st[:, S_ES + tb * 4:S_ES + tb * 4 + 4], axis=AX.X, op=ALU.add),
                [("st", S_ES + tb * 4 + cg) for cg in range(4)], [("st", S_ER + tb)])
            act(lambda e, tb=tb: e.activation(out=st[:, S_ER + tb:S_ER + tb + 1], in_=st[:, S_ER + tb:S_ER + tb + 1], func=AF.Sqrt, scale=1.0 / D, bias=EPS),
                [("st", S_ER + tb)], [("st", S_ER + tb)])
            dve(lambda e, tb=tb: e.reciprocal(out=st[:, S_ER + tb:S_ER + tb + 1], in_=st[:, S_ER + tb:S_ER + tb + 1]), [("st", S_ER + tb)], [("st", S_ER + tb)])
        items = []
        for tb in range(8):
            items.append(dict(src=x1[:, tb, :], src_atoms=[("x1", tb, cg) for cg in range(4)], gb=gb3, gb_atom=("c12", "gb"),
                              xn=xn3[tb % 3], xn_atoms=xn3_atoms[tb % 3], junk=junk3, junk_atoms=[("c12", "sg", 0), ("c12", "sg", 1)],
                              dst_fn=lambda k0, tb=tb: h3[:, k0:k0 + 8, tb * 128:(tb + 1) * 128],
                              dst_atoms=lambda k0, tb=tb: [("h3", k0 + k, tb) for k in range(8)], scol=tb))
        norm_phase(items, hook=lambda i: e_stats(i) if i < 8 else None)
        for cg in range(4):
            r0 = load_w("wg")
            r1 = load_w("wg")
            wg = [ring[:, r0, :].rearrange("p (k n) -> p k n", k=8), ring[:, r1, :].rearrange("p (k n) -> p k n", k=8)]
            gi = cg % 2
            tr.dma("sp", "gpl%d" % gi, lambda e, gi=gi, cg=cg: e.dma_start(out=gpl[gi], in_=gbc_d[3, :, cg * 512:(cg + 1) * 512]), writes=[("c12", "gpl", gi)])
            for tb in range(8):
                b1, b2 = nbank(), nbank()
                pe([mm(ps[:, b1, :], h3[:, kc, tb * 128:(tb + 1) * 128], wg[kc // 8][:, kc % 8, :], kc == 0, kc == KC - 1) for kc in range(KC)],
                   [("ring", r0), ("ring", r1)] + [("h3", kc, tb) for kc in range(KC)], [("ps", b1)])
                pe([mm(ps[:, b2, :], pT[:, k, tb * 128:(tb + 1) * 128], ppj[:, k, cg * 512:(cg + 1) * 512], k == 0, k == 1) for k in range(2)],
                   [("c12", "pT", tb), ("pp",)], [("ps", b2)])
                i2 = tb % 2
                act(lambda e, b1=b1, i2=i2: e.activation(out=sg[i2], in_=ps[:, b1, :], func=AF.Sigmoid), [("ps", b1)], [("c12", "sg", i2)])
                dve(lambda e, b2=b2, i2=i2, tb=tb, gi=gi: e.scalar_tensor_tensor(out=te[i2], in0=ps[:, b2, :], scalar=st[:, S_ER + tb:S_ER + tb + 1], in1=gpl[gi],
                                                                                 op0=ALU.mult, op1=ALU.mult),
                    [("ps", b2), ("st", S_ER + tb), ("c12", "gpl", gi)], [("c12", "te", i2)])
                dve(lambda e, i2=i2: e.tensor_tensor(out=te[i2], in0=te[i2], in1=sg[i2], op=ALU.mult), [("c12", "te", i2), ("c12", "sg", i2)], [("c12", "te", i2)])
                dve(lambda e, i2=i2, tb=tb, cg=cg: e.tensor_tensor(out=x1[:, tb, cg * 512:(cg + 1) * 512], in0=x1[:, tb, cg * 512:(cg + 1) * 512], in1=te[i2], op=ALU.add),
                    [("c12", "te", i2), ("x1", tb, cg)], [("x1", tb, cg)])
                if cg == 3:
                    tr.dma("sp", "out%d" % tb, lambda e, tb=tb: e.dma_start(out=y[tb * 128:(tb + 1) * 128, :], in_=x1[:, tb, :]),
                           reads=[("x1", tb, c4) for c4 in range(4)], writes=[("yout", tb)])
        tr.wait_all("sp", [("yout", tb) for tb in range(8)])
        assert ring_pos[0] == NL

        with nc.Block() as block:
            @block.sync
            def _(e):
                tr.replay("sp", e)

            @block.gpsimd
            def _(e):
                tr.replay("pool", e)

            @block.tensor
            def _(e):
                tr.replay("pe", e)

            @block.scalar
            def _(e):
                tr.replay("act", e)

            @block.vector
            def _(e):
                tr.replay("dve", e)
    return nc


_CACHE = {}


def kernel(x, p, norm_mix_g, w_in, gmlp_ln_g, gmlp_ln_b, gmlp_w_s, gmlp_b_s, fox_b_f,
           q_norm_g, k_norm_g, w_branch_a, w_branch_b, w_out, norm_ffn_g, w_up,
           conv_w, conv_b, w_down, ple_proj, ple_norm_g, ple_gate_norm_g, w_ple_gate):
    f = lambda a: np.ascontiguousarray(np.asarray(a, dtype=np.float32))
    x, p = f(x), f(p)
    w_in, w_branch_a, w_branch_b, w_out = f(w_in)[0], f(w_branch_a)[0], f(w_branch_b)[0], f(w_out)[0]
    w_up, w_down, w_ple_gate, ple_proj = f(w_up)[0], f(w_down)[0], f(w_ple_gate)[0], f(ple_proj)[0]

    ws = _build_wstream(w_in, w_branch_a, w_branch_b, w_out, w_up, w_down, w_ple_gate)
    pproj = np.ascontiguousarray(ple_proj.reshape(2, P, D).transpose(1, 0, 2).reshape(P, 2 * D))
    rep = lambda v: np.ascontiguousarray(np.broadcast_to(np.asarray(v, np.float32).reshape(1, -1), (P, np.asarray(v).size)))
    gbc = np.stack([rep(f(norm_mix_g)[0]), rep(f(norm_ffn_g)[0]), rep(f(ple_gate_norm_g)[0]), rep(f(ple_norm_g)[0])])
    lngb = np.stack([rep(f(gmlp_ln_g)[0]), rep(f(gmlp_ln_b)[0])])
    wsT = np.ascontiguousarray(f(gmlp_w_s)[0].transpose(2, 0, 1).reshape(P, 1024))
    cst = np.zeros((P, CST_W), np.float32)
    ii = np.arange(P)
    cst[:, C_TRI:C_TRI + 128] = (ii[None, :] >= ii[:, None]).astype(np.float32)
    cst[:, C_E63:C_E63 + 128] = (ii[:, None] <= 63).astype(np.float32)
    cst[:, C_ID:C_ID + 128] = np.eye(P, dtype=np.float32)
    cst[:, C_ONE:C_ONE + 128] = 1.0
    bsrep = rep(f(gmlp_b_s)[0].reshape(-1))
    cst[:, C_BF:C_BF + 128] = rep(np.tile(f(fox_b_f)[0], NB))
    cst[:, C_GQ] = f(q_norm_g)[0]
    cst[:, C_GK] = f(k_norm_g)[0]
    cst[:, C_CW:C_CW + 132] = f(conv_w)[0].reshape(3, NFC, P).transpose(2, 1, 0).reshape(P, 132)
    cst[:, C_CB:C_CB + 44] = f(conv_b)[0].reshape(NFC, P).T

    in_maps = []
    for c in range(NCORES):
        b, hf = c // 2, c % 2
        if hf == 1:
            xwc = x[b]
        else:
            xwc = np.concatenate([x[b, :1024], x[b, :1024]], axis=0)
        cc = cst.copy()
        cc[:, C_FL] = 0.0 if hf == 1 else NEG
        cc[:, C_FL + 1] = 1.0 if hf == 1 else 0.0
        in_maps.append({
            "xw": np.ascontiguousarray(xwc), "pw": np.ascontiguousarray(p[0, b, hf * 1024:(hf + 1) * 1024].T.reshape(2, P, TO).transpose(1, 0, 2).reshape(P, 2 * TO)),
            "wstream": ws, "pproj": pproj, "gbc": gbc, "lngb": lngb, "wsT": wsT, "cst": cc, "bsrep": bsrep,
        })
    if "nc" not in _CACHE:
        _CACHE["nc"] = build_program()
    res = run_bass_kernel_spmd(_CACHE["nc"], in_maps, core_ids=list(range(NCORES)))
    out = np.empty((4, 2048, D), np.float32)
    for c in range(NCORES):
        b, hf = c // 2, c % 2
        out[b, hf * 1024:(hf + 1) * 1024] = res.results[c]["y"]
    return out
```

```python
import numpy as np
from contextlib import ExitStack

import concourse.bass as bass
import concourse.mybir as mybir
from concourse.bass_utils import run_bass_kernel_spmd

F32 = mybir.dt.float32
BF16 = mybir.dt.bfloat16
AF = mybir.ActivationFunctionType
ALU = mybir.AluOpType
AX = mybir.AxisListType

NCORES = 8
P = 128
D = 2048
KC = 16
TW = 2048
NB = 16
QB = 9
TQ = QB * P
TO = 1024
DFF = 5632
NFC = 44
NG = 22
H = 8
EPS = 1e-6
SLOT = 4096
NS = 4
NEG = -30000.0

C_TRI = 0
C_E63 = 128
C_ID = 256
C_BF = 384
C_GQ = 512
C_GK = 513
C_CW = 514
C_CB = 646
C_FL = 690
C_ONE = 692
CST_W = 820

Q_TILES = [(0, 512), (512, 512), (1024, 128)]


def _weight_stream_plan():
    plan = []
    for s in range(4):
        plan.append(("v", s))
    for s in range(4):
        plan.append(("u", s))
    for s in range(4):
        plan.append(("k", s))
    for s in range(4):
        plan.append(("vv", s))
    plan.append(("f", 0))
    for s in range(4):
        plan.append(("q", s))
    for c in range(16):
        plan.append(("ga", c))
        plan.append(("gb", c))
    for cg in range(4):
        plan.append(("wo", 2 * cg))
        plan.append(("wo", 2 * cg + 1))
    plan.append(("uc", 0))
    plan.append(("uc", 1))
    for g in range(1, NG):
        plan.append(("uc", 2 * g))
        plan.append(("dn", g - 1))
        plan.append(("uc", 2 * g + 1))
    plan.append(("dn", NG - 1))
    for cg in range(4):
        plan.append(("wg", 2 * cg))
        plan.append(("wg", 2 * cg + 1))
    return plan


PLAN = _weight_stream_plan()
NL = len(PLAN)


def _slot_elems(kind):
    if kind == "f":
        return 16 * 8
    if kind in ("ga", "gb"):
        return 24 * 128
    return 4096


def _build_wstream(w_in, w_branch_a, w_branch_b, w_out, w_up, w_down, w_ple_gate):
    ws = np.zeros((NL, P, SLOT), dtype=np.float32)

    def kcols(w, c0, n):
        K = w.shape[0]
        return w[:, c0:c0 + n].reshape(K // P, P, n).transpose(1, 0, 2)

    for i, (kind, j) in enumerate(PLAN):
        if kind == "v":
            t = kcols(w_in, 1024 + 256 * j, 256)
        elif kind == "u":
            t = kcols(w_in, 256 * j, 256)
        elif kind == "k":
            t = kcols(w_in, 3072 + 256 * j, 256)
        elif kind == "vv":
            t = kcols(w_in, 4096 + 256 * j, 256)
        elif kind == "f":
            t = kcols(w_in, 5120, 8)
        elif kind == "q":
            t = kcols(w_in, 2048 + 256 * j, 256)
        elif kind == "ga":
            t = np.concatenate([kcols(w_in, 5128 + 128 * j, 128), kcols(w_branch_a, 128 * j, 128)], axis=1)
        elif kind == "gb":
            t = np.concatenate([kcols(w_in, 7176 + 128 * j, 128), kcols(w_branch_b, 128 * j, 128)], axis=1)
        elif kind == "wo":
            cg, hf = j // 2, j % 2
            t = kcols(w_out, 512 * cg, 512)[:, 8 * hf:8 * hf + 8, :]
        elif kind == "uc":
            t = np.concatenate([kcols(w_up, 128 * j, 128), kcols(w_up, DFF + 128 * j, 128)], axis=1)
        elif kind == "dn":
            t = w_down[256 * j:256 * j + 256, :].reshape(2, P, D).transpose(1, 0, 2)
        elif kind == "wg":
            cg, hf = j // 2, j % 2
            t = kcols(w_ple_gate, 512 * cg, 512)[:, 8 * hf:8 * hf + 8, :]
        else:
            raise AssertionError(kind)
        t = t.reshape(P, -1)
        ws[i, :, :t.shape[1]] = t
    return ws


class Tracker:
    ENG = ("pe", "act", "dve", "pool", "sp")

    def __init__(self, nc, es):
        self.nc = nc
        self.es = es
        self.sem = {}
        self.cnt = {}
        self.streams = {e: [] for e in self.ENG}
        self.known = {e: {} for e in self.ENG}
        self.lw = {}
        self.rd = {}
        self.tensors = []
        self.atoms_of = {}
        self._inherit = {}
        for e in self.ENG[:4]:
            self._mksem(e)

    def _mksem(self, name):
        if name not in self.sem:
            self.sem[name] = self.es.enter_context(self.nc.semaphore("s_" + name))
            self.cnt[name] = 0
        return self.sem[name]

    def tensor(self, name, lo, hi):
        inherited = {}
        self.ghosts = getattr(self, "ghosts", [])
        for (glo, ghi, gd) in self.ghosts:
            if not (hi <= glo or lo >= ghi):
                for s, v in gd.items():
                    inherited[s] = max(inherited.get(s, 0), v)
        for t in self.tensors:
            if t[3] and not (hi <= t[1] or lo >= t[2]):
                t[3] = False
                gd = dict(self._inherit.get(t[0], {}))
                for a in self.atoms_of.get(t[0], ()):
                    w = self.lw.pop(a, None)
                    if w is not None:
                        gd[w[0]] = max(gd.get(w[0], 0), w[1])
                    for s, v in self.rd.pop(a, {}).items():
                        gd[s] = max(gd.get(s, 0), v)
                self.atoms_of.pop(t[0], None)
                self.ghosts.append((t[1], t[2], gd))
                for s, v in gd.items():
                    inherited[s] = max(inherited.get(s, 0), v)
        self.tensors.append([name, lo, hi, True])
        self.atoms_of[name] = set()
        self._inherit[name] = inherited

    def _touch(self, a):
        name = a[0]
        s = self.atoms_of.get(name)
        if s is not None and a not in s:
            s.add(a)
            inh = self._inherit.get(name)
            if inh:
                self.rd[a] = dict(inh)

    def _deps(self, eng, reads, writes):
        deps = {}

        def add(s, v, kind):
            if s == eng and (eng == "pe" or kind == "war"):
                return
            if v > deps.get(s, 0):
                deps[s] = v

        for a in reads:
            self._touch(a)
            w = self.lw.get(a)
            if w is not None:
                add(w[0], w[1], "raw")
        for a in writes:
            self._touch(a)
            w = self.lw.get(a)
            if w is not None:
                add(w[0], w[1], "waw")
            for s, v in self.rd.get(a, {}).items():
                add(s, v, "war")
        kn = self.known[eng]
        out = []
        for s, v in deps.items():
            if kn.get(s, 0) < v:
                kn[s] = v
                out.append((s, v))
        return out

    def _record(self, reads, writes, tag):
        for a in reads:
            r = self.rd.setdefault(a, {})
            if r.get(tag[0], 0) < tag[1]:
                r[tag[0]] = tag[1]
        for a in writes:
            self.lw[a] = tag
            self.rd[a] = {}

    def op(self, eng, fns, reads=(), writes=()):
        if not isinstance(fns, (list, tuple)):
            fns = [fns]
        waits = self._deps(eng, reads, writes)
        self.cnt[eng] += 1
        val = self.cnt[eng]
        sem = self.sem[eng]
        st = self.streams[eng]
        for s, v in waits:
            st.append(("w", self.sem[s], v))
        for f in fns[:-1]:
            st.append(("i", f, None, 0))
        st.append(("i", fns[-1], sem, 1))
        self._record(reads, writes, (eng, val))

    def dma(self, queue, slot, fn, reads=(), writes=(), extra=()):
        sem = self._mksem("d_" + slot)
        name = "d_" + slot
        waits = self._deps(queue, reads, writes)
        for s_, v in extra:
            if self.known[queue].get(s_, 0) < v:
                self.known[queue][s_] = v
                waits.append((s_, v))
        self.cnt[name] += 16
        val = self.cnt[name]
        st = self.streams[queue]
        for s, v in waits:
            st.append(("w", self.sem[s], v))
        st.append(("i", fn, sem, 16))
        self._record(reads, writes, (name, val))

    def merge(self, dst, srcs):
        r = self.rd.setdefault(dst, {})
        for a in srcs:
            w = self.lw.get(a)
            if w is not None and r.get(w[0], 0) < w[1]:
                r[w[0]] = w[1]
            for s_, v in self.rd.get(a, {}).items():
                if r.get(s_, 0) < v:
                    r[s_] = v

    def wait_all(self, eng, atoms):
        waits = self._deps(eng, atoms, ())
        for s, v in waits:
            self.streams[eng].append(("w", self.sem[s], v))

    def replay(self, eng, e):
        for it in self.streams[eng]:
            if it[0] == "w":
                e.wait_ge(it[1], it[2])
            else:
                ins = it[1](e)
                if it[2] is not None:
                    ins.then_inc(it[2], it[3])


def build_program():
    nc = bass.Bass("TRN2", target_bir_lowering=False)
    xw = nc.dram_tensor("xw", [TW, D], F32, kind="ExternalInput").ap()
    pw = nc.dram_tensor("pw", [P, 2 * TO], F32, kind="ExternalInput").ap()
    wstream = nc.dram_tensor("wstream", [NL, P, SLOT], F32, kind="ExternalInput").ap()
    pproj_d = nc.dram_tensor("pproj", [P, 2 * D], F32, kind="ExternalInput").ap()
    gbc_d = nc.dram_tensor("gbc", [4, P, D], F32, kind="ExternalInput").ap()
    lngb_d = nc.dram_tensor("lngb", [2, P, 1024], F32, kind="ExternalInput").ap()
    wsT_d = nc.dram_tensor("wsT", [P, 1024], F32, kind="ExternalInput").ap()
    cst_d = nc.dram_tensor("cst", [P, CST_W], F32, kind="ExternalInput").ap()
    bsrep_d = nc.dram_tensor("bsrep", [P, 1024], F32, kind="ExternalInput").ap()
    y = nc.dram_tensor("y", [TO, D], F32, kind="ExternalOutput").ap()

    es = ExitStack()
    with es:
        ARENA_B = 165888
        arena = es.enter_context(nc.sbuf_tensor("arena", [P, ARENA_B // 2], BF16))
        ring = es.enter_context(nc.sbuf_tensor("ring", [P, NS, SLOT], BF16))
        cst = es.enter_context(nc.sbuf_tensor("cst_sb", [P, CST_W], F32))
        cbf = es.enter_context(nc.sbuf_tensor("cbf", [P, 4, 128], BF16))
        st = es.enter_context(nc.sbuf_tensor("stats", [P, 160], F32))
        fxob = es.enter_context(nc.sbuf_tensor("fxob", [P, 1024], F32))
        fx = fxob[:, :].rearrange("p (i n) -> p i n", i=8)
        biasK = es.enter_context(nc.sbuf_tensor("biasK", [P, NB, H, 3], F32))
        sel = es.enter_context(nc.sbuf_tensor("sel", [P, H, 128], BF16))
        rsb = es.enter_context(nc.sbuf_tensor("rsb", [P, 4], F32))
        gq2 = es.enter_context(nc.sbuf_tensor("gq2", [P, 2], F32))
        ps = es.enter_context(nc.psum_tensor("ps", [P, 8, 512], F32))

        tr = Tracker(nc, es)

        def bf(lo, n):
            return arena[:, lo // 2: lo // 2 + n]

        def f32(lo, n):
            return arena[:, lo // 2: lo // 2 + 2 * n].bitcast(F32)

        ZA, ZB, ZC, ZD, ZE = 0, 36864, 73728, 110592, 147456
        c32 = cst[:, C_ONE:C_ONE + 128]
        wsb = bf(ZB + 32768, 1024).rearrange("p (g t) -> p g t", g=8)
        bsh = bf(ZB + 28672, 2048).rearrange("p (i n) -> p i n", i=2)

        ident = cbf[:, 0, :]
        ones_bf = cbf[:, 1, :]
        tri_bf = cbf[:, 2, :]

        bank_rr = [0]

        def nbank():
            b = bank_rr[0]
            bank_rr[0] = (b + 1) % 8
            return b

        ring_pos = [0]
        xb_tags = []

        def load_w(expect_kind):
            i = ring_pos[0]
            kind, j = PLAN[i]
            assert kind == expect_kind, (kind, expect_kind)
            r = i % NS
            n = _slot_elems(kind)
            ring_pos[0] += 1
            extra = []
            if i < 4:
                extra = [xb_tags[(0, 1, 3, 5)[i]]]
            tr.dma("pool", "ring%d" % r,
                   lambda e, r=r, i=i, n=n: e.dma_start(out=ring[:, r, 0:n], in_=wstream[i, :, 0:n]),
                   reads=(), writes=[("ring", r)], extra=extra)
            return r

        def act(fn, reads, writes):
            tr.op("act", fn, reads, writes)

        def dve(fn, reads, writes):
            tr.op("dve", fn, reads, writes)

        def pe(fns, reads, writes):
            tr.op("pe", fns, reads, writes)

        def mm(out, lhsT, rhs, start, stop):
            return lambda e: e.matmul(out, lhsT, rhs, start=start, stop=stop)

        def tp(out, in_):
            return lambda e: e.transpose(out, in_, ident)

        tr.dma("sp", "cst", lambda e: e.dma_start(out=cst[:], in_=cst_d[:, :]), writes=[("cst",)])
        dve(lambda e: e.tensor_copy(out=ident, in_=cst[:, C_ID:C_ID + 128]), [("cst",)], [("ident",)])
        def late_setup():
            tr.tensor("wsf", ZC, ZC + 4096)
            tr.tensor("bstmp", ZC + 4096, ZC + 8192)
            tr.tensor("bsf", ZC + 8192, ZC + 12288)
            tr.tensor("bsh", ZB + 28672, ZB + 32768)
            tr.tensor("wsb", ZB + 32768, ZB + 34816)
            wsf = f32(ZC, 1024).rearrange("p (g t) -> p g t", g=8)
            bst = f32(ZC + 4096, 1024)
            bsf = f32(ZC + 8192, 1024)
            tr.dma("sp", "wsf", lambda e: e.dma_start(out=f32(ZC, 1024), in_=wsT_d[:, :]), writes=[("wsf",)])
            tr.dma("sp", "bsf", lambda e: e.dma_start(out=bsf, in_=bsrep_d[:, :]), writes=[("bsf",)])
            dve(lambda e: e.memset(ones_bf, 1.0), [], [("ones_bf",)])
            dve(lambda e: e.tensor_scalar(out=tri_bf, in0=cst[:, C_TRI:C_TRI + 128], scalar1=-1.0, scalar2=30000.0, op0=ALU.add, op1=ALU.mult),
                [("cst",)], [("tri_bf",)])
            for h in range(H):
                dve(lambda e, h=h: e.tensor_scalar(out=sel[:, h, :], in0=c32, scalar1=cst[:, C_ID + h:C_ID + h + 1], scalar2=None, op0=ALU.mult),
                    [("cst",)], [("sel",)])
            for g in range(8):
                dve(lambda e, g=g: e.tensor_tensor(out=wsb[:, g, :], in0=wsf[:, g, :], in1=cst[:, C_TRI:C_TRI + 128], op=ALU.mult),
                    [("wsf",), ("cst",)], [("wsb",)])
            dve(lambda e: e.tensor_copy(out=bsh[:, 0, :], in_=bsf), [("bsf",)], [("bsh", 0)])
            dve(lambda e: e.tensor_tensor(out=bst, in0=bsf, in1=bsh[:, 0, :], op=ALU.subtract), [("bsf",), ("bsh", 0)], [("bstmp",)])
            dve(lambda e: e.tensor_copy(out=bsh[:, 1, :], in_=bst), [("bstmp",)], [("bsh", 1)])
            dve(lambda e: e.tensor_scalar(out=cbf[:, 3, :], in0=c32, scalar1=cst[:, C_ID:C_ID + 1], scalar2=None, op0=ALU.mult),
                [("cst",)], [("e0",)])
            dve(lambda e: e.tensor_scalar(out=gq2[:, 0:1], in0=cst[:, C_GQ:C_GQ + 1], scalar1=float(128 ** -0.5), scalar2=None, op0=ALU.mult),
                [("cst",)], [("gq2",)])
            dve(lambda e: e.tensor_copy(out=gq2[:, 1:2], in_=cst[:, C_GK:C_GK + 1]), [("cst",)], [("gq2",)])


        S_SS, S_RS = 0, 16
        S_V1, S_VM, S_VQ = 32, 68, 80
        S_REC = 96
        S_ES, S_ER = 104, 136

        def norm_A(it):
            if it.get("pre"):
                it["pre"]()
            scol = it["scol"]
            src_ap, xn, junk, gb = it["src"], it["xn"], it["junk"], it["gb"]
            act(lambda e: e.activation(out=junk, in_=src_ap, func=AF.Square, accum_out=st[:, S_SS + scol:S_SS + scol + 1]),
                it["src_atoms"], list(it["junk_atoms"]) + [("st", S_SS + scol)])
            act(lambda e: e.activation(out=st[:, S_RS + scol:S_RS + scol + 1], in_=st[:, S_SS + scol:S_SS + scol + 1],
                                       func=AF.Sqrt, scale=1.0 / D, bias=EPS),
                [("st", S_SS + scol)], [("st", S_RS + scol)])
            dve(lambda e: e.reciprocal(out=st[:, S_RS + scol:S_RS + scol + 1], in_=st[:, S_RS + scol:S_RS + scol + 1]),
                [("st", S_RS + scol)], [("st", S_RS + scol)])
            dve(lambda e: e.scalar_tensor_tensor(out=xn, in0=src_ap, scalar=st[:, S_RS + scol:S_RS + scol + 1], in1=gb,
                                                 op0=ALU.mult, op1=ALU.mult),
                list(it["src_atoms"]) + [("st", S_RS + scol), it["gb_atom"]], list(it["xn_atoms"]))

        def norm_B(it):
            xn, dst_fn, dst_atoms = it["xn"], it["dst_fn"], it["dst_atoms"]
            for half in range(2):
                b = nbank()
                pb = ps[:, b, :].bitcast(BF16)
                pe([tp(pb[:, k * 128:(k + 1) * 128], xn[:, (half * 8 + k) * 128:(half * 8 + k + 1) * 128]) for k in range(8)],
                   list(it["xn_atoms"]) + [("ident",)], [("ps", b)])
                if half == 0:
                    act(lambda e, pb=pb, half=half: e.activation(out=dst_fn(half * 8), in_=pb.rearrange("p (k t) -> p k t", k=8), func=AF.Copy),
                        [("ps", b)], dst_atoms(half * 8))
                else:
                    dve(lambda e, pb=pb, half=half: e.tensor_copy(out=dst_fn(half * 8), in_=pb.rearrange("p (k t) -> p k t", k=8)),
                        [("ps", b)], dst_atoms(half * 8))

        def norm_phase(items, L=2, hook=None):
            n = len(items)
            for i in range(n + L):
                if i < n:
                    norm_A(items[i])
                if i - L >= 0:
                    norm_B(items[i - L])
                if hook is not None:
                    hook(i)

        tr.tensor("hq", ZA, ZA + 36864)
        tr.tensor("hc", ZB, ZB + 28672)
        hq = bf(ZA, KC * TQ).rearrange("p (k t) -> p k t", k=KC)
        hc = bf(ZB, KC * 896).rearrange("p (k t) -> p k t", k=KC)
        tr.tensor("p1tmp", ZD, ZD + 36864)
        tr.tensor("p1tmp2", ZC + 12288, ZC + 24576)
        xblk = [f32(ZD + i * 8192, D) for i in range(3)]
        xnb = [bf(ZD + 24576 + i * 4096, D) for i in range(3)]
        gb1 = f32(ZC + 12288, D)
        junk1 = bf(ZC + 20480, D)
        tr.dma("sp", "gb1", lambda e: e.dma_start(out=gb1, in_=gbc_d[0, :, :]), writes=[("p1tmp2", "gb")])

        def hwin(kc, wb):
            if wb < 7:
                return hc[:, kc, wb * 128:(wb + 1) * 128]
            return hq[:, kc, (wb - 7) * 128:(wb - 6) * 128]

        def hwin_atom(kc, wb):
            return ("hc", kc, wb) if wb < 7 else ("hq", kc, wb - 7)

        items = []
        for n_i, wb in enumerate(list(range(7, NB)) + list(range(0, 7))):
            xi = n_i % 3
            xb = xblk[xi]
            if wb < 7:
                dst = lambda k0, wb=wb: hc[:, k0:k0 + 8, wb * 128:(wb + 1) * 128]
            else:
                dst = lambda k0, wb=wb: hq[:, k0:k0 + 8, (wb - 7) * 128:(wb - 6) * 128]
            items.append(dict(
                pre=lambda xb=xb, wb=wb, xi=xi: (tr.dma("sp", "xb%d" % xi, lambda e: e.dma_start(out=xb, in_=xw[wb * 128:(wb + 1) * 128, :]),
                                                       writes=[("p1tmp", "xb", xi)]), xb_tags.append(tr.lw[("p1tmp", "xb", xi)])),
                src=xb, src_atoms=[("p1tmp", "xb", xi)], gb=gb1, gb_atom=("p1tmp2", "gb"),
                xn=xnb[xi], xn_atoms=[("p1tmp", "xn", xi)], junk=junk1, junk_atoms=[("p1tmp2", "junk")],
                dst_fn=dst, dst_atoms=lambda k0, wb=wb: [hwin_atom(k0 + k, wb) for k in range(8)], scol=n_i))
        norm_phase(items)
        late_setup()

        tr.tensor("gv", ZC, ZC + 36864)
        tr.tensor("vn", ZD, ZD + 18432)
        tr.tensor("lngb", ZE, ZE + 8192)
        gv = f32(ZC, QB * 1024).rearrange("p (b n) -> p b n", b=QB)
        vn = bf(ZD, QB * 1024).rearrange("p (b n) -> p b n", b=QB)
        lnG = f32(ZE, 1024)
        lnB = f32(ZE + 4096, 1024)
        tr.dma("sp", "lng", lambda e: e.dma_start(out=lnG, in_=lngb_d[0, :, :]), writes=[("lngb", 0)])
        tr.dma("sp", "lnb", lambda e: e.dma_start(out=lnB, in_=lngb_d[1, :, :]), writes=[("lngb", 1)])
        S_M2 = 144

        def ln_block(qb):
            gva = [("gv", qb, s) for s in range(4)]
            vm, vq, m2 = st[:, S_VM + qb:S_VM + qb + 1], st[:, S_VQ + qb:S_VQ + qb + 1], st[:, S_M2 + qb:S_M2 + qb + 1]
            dve(lambda e: e.tensor_reduce(out=vm, in_=st[:, S_V1 + qb * 4:S_V1 + qb * 4 + 4], axis=AX.X, op=ALU.add),
                [("st", S_V1 + qb * 4 + s) for s in range(4)], [("st", S_VM + qb)])
            dve(lambda e: e.tensor_scalar(out=vm, in0=vm, scalar1=-1.0 / 1024, scalar2=None, op0=ALU.mult),
                [("st", S_VM + qb)], [("st", S_VM + qb)])
            act(lambda e: e.activation(out=vn[:, qb, :], in_=gv[:, qb, :], func=AF.Square, accum_out=vq),
                gva, [("vn", qb), ("st", S_VQ + qb)])
            dve(lambda e: e.tensor_tensor(out=m2, in0=vm, in1=vm, op=ALU.mult), [("st", S_VM + qb)], [("st", S_M2 + qb)])
            dve(lambda e: e.tensor_scalar(out=vq, in0=vq, scalar1=1.0 / 1024, scalar2=m2, op0=ALU.mult, op1=ALU.subtract),
                [("st", S_VQ + qb), ("st", S_M2 + qb)], [("st", S_VQ + qb)])
            act(lambda e: e.activation(out=vq, in_=vq, func=AF.Sqrt, scale=1.0, bias=EPS), [("st", S_VQ + qb)], [("st", S_VQ + qb)])
            dve(lambda e: e.reciprocal(out=vq, in_=vq), [("st", S_VQ + qb)], [("st", S_VQ + qb)])
            dve(lambda e: e.scalar_tensor_tensor(out=gv[:, qb, :], in0=gv[:, qb, :], scalar=vm, in1=lnG, op0=ALU.add, op1=ALU.mult),
                gva + [("st", S_VM + qb), ("lngb", 0)], gva)
            dve(lambda e: e.scalar_tensor_tensor(out=vn[:, qb, :], in0=gv[:, qb, :], scalar=vq, in1=lnB, op0=ALU.mult, op1=ALU.add),
                gva + [("st", S_VQ + qb), ("lngb", 1)], [("vn", qb)])

        for s in range(4):
            r = load_w("v")
            wv = ring[:, r, :].rearrange("p (k n) -> p k n", k=KC)
            for qb in range(QB):
                b = nbank()
                pe([mm(ps[:, b, 0:256], hq[:, kc, qb * 128:(qb + 1) * 128], wv[:, kc, :], kc == 0, kc == KC - 1) for kc in range(KC)],
                   [("ring", r)] + [("hq", kc, qb) for kc in range(KC)], [("ps", b)])
                act(lambda e, b=b, qb=qb, s=s: e.activation(out=gv[:, qb, s * 256:(s + 1) * 256], in_=ps[:, b, 0:256], func=AF.Gelu,
                                                            accum_out=st[:, S_V1 + qb * 4 + s:S_V1 + qb * 4 + s + 1]),
                    [("ps", b)], [("gv", qb, s), ("st", S_V1 + qb * 4 + s)])
                if s == 3:
                    ln_block(qb)
        tr.tensor("oa", ZE, ZE + 18432)
        oa = bf(ZE, 8 * TQ).rearrange("p (g t) -> p g t", g=8)
        for s in range(4):
            r = load_w("u")
            wu = ring[:, r, :].rearrange("p (k n) -> p k n", k=KC)
            for gg in range(2):
                g = 2 * s + gg
                for ti, (t0, tn) in enumerate(Q_TILES):
                    b = nbank()
                    pe([mm(ps[:, b, 0:tn], wu[:, kc, gg * 128:(gg + 1) * 128], hq[:, kc, t0:t0 + tn], kc == 0, kc == KC - 1) for kc in range(KC)],
                       [("ring", r)] + [("hq", kc, qb) for kc in range(KC) for qb in range(t0 // 128, (t0 + tn) // 128)], [("ps", b)])
                    act(lambda e, b=b, g=g, t0=t0, tn=tn: e.activation(out=oa[:, g, t0:t0 + tn], in_=ps[:, b, 0:tn], func=AF.Gelu),
                        [("ps", b)], [("oa", g, qb) for qb in range(t0 // 128, (t0 + tn) // 128)])

        for qb in range(QB):
            for g0 in (0, 4):
                b = nbank()
                fns = []
                for g in range(g0, g0 + 4):
                    o = ps[:, b, (g - g0) * 128:(g - g0 + 1) * 128]
                    fns.append(mm(o, vn[:, qb, g * 128:(g + 1) * 128], wsb[:, g, :], True, False))
                    fns.append(mm(o, cbf[:, 3, :], bsh[:, 0, g * 128:(g + 1) * 128], False, False))
                    fns.append(mm(o, cbf[:, 3, :], bsh[:, 1, g * 128:(g + 1) * 128], False, True))
                pe(fns, [("vn", qb), ("wsb",), ("bsh", 0), ("bsh", 1), ("e0",)], [("ps", b)])
                dve(lambda e, b=b, g0=g0, qb=qb: e.tensor_tensor(out=oa[:, g0:g0 + 4, qb * 128:(qb + 1) * 128],
                                                                  in0=ps[:, b, :].rearrange("p (g t) -> p g t", g=4),
                                                                  in1=oa[:, g0:g0 + 4, qb * 128:(qb + 1) * 128], op=ALU.mult),
                    [("ps", b)] + [("oa", g, qb) for g in range(g0, g0 + 4)], [("oa", g, qb) for g in range(g0, g0 + 4)])

        tr.tensor("kT", ZC, ZC + 32768)
        tr.tensor("va", ZD, ZD + 33280)
        tr.tensor("rtm", ZC + 32768, ZC + 36864)
        tr.tensor("atmp", ZD + 33280, ZD + 36864)
        tr.tensor("fx", 10 ** 6, 10 ** 6 + 4096)
        kT = bf(ZC, H * TW).rearrange("p (h t) -> p h t", h=H)
        va = bf(ZD, NB * H * 130).rearrange("p (b h d) -> p b h d", b=NB, h=H)
        sqt = [bf(ZD + 33280 + i * 1024, 512) for i in range(2)]
        rtm = [f32(ZC + 32768 + i * 2048, 512) for i in range(2)]
        ptb = [bf(ZD + 35328 + i * 256, 128) for i in range(6)]
        W_TILES = [(0, 512), (512, 384)]

        qk_pend = [None]
        qk_cnt = [0]

        def qk_norm_tile(b, n, gcol, out_ap, out_atoms, cnt=None):
            i2 = qk_cnt[0] % 2
            qk_cnt[0] += 1
            act(lambda e: e.activation(out=sqt[i2][:, 0:n], in_=ps[:, b, 0:n], func=AF.Square), [("ps", b)], [("atmp", "sq", i2)])

            def tail():
                b2 = nbank()
                pe([mm(ps[:, b2, 0:n], ones_bf, sqt[i2][:, 0:n], True, True)], [("atmp", "sq", i2), ("ones_bf",)], [("ps", b2)])
                act(lambda e: e.activation(out=rtm[i2][:, 0:n], in_=ps[:, b2, 0:n], func=AF.Ln, scale=1.0 / 128, bias=EPS),
                    [("ps", b2)], [("rtm", i2)])
                act(lambda e: e.activation(out=rtm[i2][:, 0:n], in_=rtm[i2][:, 0:n], func=AF.Exp, scale=-0.5), [("rtm", i2)], [("rtm", i2)])
                dve(lambda e: e.scalar_tensor_tensor(out=out_ap, in0=ps[:, b, 0:n], scalar=gq2[:, gcol:gcol + 1], in1=rtm[i2][:, 0:n],
                                                     op0=ALU.mult, op1=ALU.mult),
                    [("ps", b), ("rtm", i2), ("gq2",)], out_atoms)

            if qk_pend[0] is not None:
                qk_pend[0]()
            qk_pend[0] = tail

        def qk_flush():
            if qk_pend[0] is not None:
                qk_pend[0]()
                qk_pend[0] = None

        cnt = 0
        for s in range(4):
            r = load_w("k")
            wk = ring[:, r, :].rearrange("p (k n) -> p k n", k=KC)
            for hh in range(2):
                h = 2 * s + hh
                tiles = [("c", t0, tn) for (t0, tn) in W_TILES] + [("q", t0, tn) for (t0, tn) in Q_TILES]
                for (src, t0, tn) in tiles:
                    b = nbank()
                    if src == "c":
                        rhs = lambda kc: hc[:, kc, t0:t0 + tn]
                        ratoms = [("hc", kc, wb) for kc in range(KC) for wb in range(t0 // 128, (t0 + tn) // 128)]
                        w0 = t0
                    else:
                        rhs = lambda kc: hq[:, kc, t0:t0 + tn]
                        ratoms = [("hq", kc, qb) for kc in range(KC) for qb in range(t0 // 128, (t0 + tn) // 128)]
                        w0 = 896 + t0
                    pe([mm(ps[:, b, 0:tn], wk[:, kc, hh * 128:(hh + 1) * 128], rhs(kc), kc == 0, kc == KC - 1) for kc in range(KC)],
                       [("ring", r)] + ratoms, [("ps", b)])
                    qk_norm_tile(b, tn, 1, kT[:, h, w0:w0 + tn], [("kT", h, wb) for wb in range(w0 // 128, (w0 + tn) // 128)], cnt)
                    cnt += 1
        qk_flush()
        dve(lambda e: e.memset(va[:, :, :, 128:130], 1.0), [], [("va", "ones")])
        for s in range(4):
            r = load_w("vv")
            wvv = ring[:, r, :].rearrange("p (k n) -> p k n", k=KC)
            for wb in range(NB):
                b = nbank()
                pe([mm(ps[:, b, 0:256], hwin(kc, wb), wvv[:, kc, :], kc == 0, kc == KC - 1) for kc in range(KC)],
                   [("ring", r)] + [hwin_atom(kc, wb) for kc in range(KC)], [("ps", b)])
                act(lambda e, b=b, wb=wb, s=s: e.activation(out=va[:, wb, 2 * s:2 * s + 2, 0:128],
                                                            in_=ps[:, b, 0:256].rearrange("p (h d) -> p h d", h=2), func=AF.Copy),
                    [("ps", b)], [("va", wb, 2 * s), ("va", wb, 2 * s + 1)])
        r = load_w("f")
        wf = ring[:, r, 0:128].rearrange("p (k n) -> p k n", k=KC)
        bF = nbank()
        fns = []
        for wb in range(NB):
            for kc in range(KC):
                fns.append(mm(ps[:, bF, wb * 8:(wb + 1) * 8], hwin(kc, wb), wf[:, kc, :], kc == 0, kc == KC - 1))
        pe(fns, [("ring", r)] + [hwin_atom(kc, wb) for kc in range(KC) for wb in range(NB)], [("ps", bF)])
        nl, cin, tot, mid, pre, cumN, refN = (fx[:, i, :] for i in range(7))
        dve(lambda e: e.tensor_tensor(out=nl, in0=ps[:, bF, 0:128], in1=cst[:, C_BF:C_BF + 128], op=ALU.add), [("ps", bF), ("cst",)], [("fx", 0)])
        act(lambda e: e.activation(out=nl, in_=nl, func=AF.Exp, scale=-1.0), [("fx", 0)], [("fx", 0)])
        act(lambda e: e.activation(out=nl, in_=nl, func=AF.Ln, bias=1.0), [("fx", 0)], [("fx", 0)])
        for i, lhs in ((1, cst[:, C_TRI:C_TRI + 128]), (2, c32), (3, cst[:, C_E63:C_E63 + 128])):
            b = nbank()
            pe([mm(ps[:, b, 0:128], lhs, nl, True, True)], [("fx", 0), ("cst",)], [("ps", b)])
            act(lambda e, b=b, i=i: e.activation(out=fx[:, i, :], in_=ps[:, b, 0:128], func=AF.Copy), [("ps", b)], [("fx", i)])
        dve(lambda e: e.memset(pre[:, 0:8], 0.0), [], [("fx", 4)])
        for wb in range(1, NB):
            dve(lambda e, wb=wb: e.tensor_tensor(out=pre[:, wb * 8:(wb + 1) * 8], in0=pre[:, (wb - 1) * 8:wb * 8], in1=tot[:, (wb - 1) * 8:wb * 8], op=ALU.add),
                [("fx", 4), ("fx", 2)], [("fx", 4)])
        dve(lambda e: e.tensor_tensor(out=cumN, in0=cin, in1=pre, op=ALU.add), [("fx", 1), ("fx", 4)], [("fx", 5)])
        dve(lambda e: e.tensor_tensor(out=refN, in0=mid, in1=pre, op=ALU.add), [("fx", 3), ("fx", 4)], [("fx", 6)])
        WBMID = [7, 10, 14]
        cum3 = cumN.rearrange("p (i h) -> p i h", h=H)
        for T in range(3):
            dve(lambda e, T=T: e.tensor_tensor(out=biasK[:, :, :, T], in0=cum3,
                                               in1=refN[:, WBMID[T] * 8:(WBMID[T] + 1) * 8].unsqueeze(1).to_broadcast([P, NB, H]), op=ALU.subtract),
                [("fx", 5), ("fx", 6)], [("biasK", T)])
            if T >= 1:
                dve(lambda e, T=T: e.tensor_scalar(out=biasK[:, 0:8, :, T], in0=biasK[:, 0:8, :, T], scalar1=cst[:, C_FL:C_FL + 1], scalar2=None, op0=ALU.add),
                    [("biasK", T), ("cst",)], [("biasK", T)])

        tr.tensor("qT", ZB, ZB + 18432)
        qT = bf(ZB, H * TQ).rearrange("p (h t) -> p h t", h=H)
        for s in range(4):
            r = load_w("q")
            wq = ring[:, r, :].rearrange("p (k n) -> p k n", k=KC)
            for hh in range(2):
                h = 2 * s + hh
                for (t0, tn) in Q_TILES:
                    b = nbank()
                    pe([mm(ps[:, b, 0:tn], wq[:, kc, hh * 128:(hh + 1) * 128], hq[:, kc, t0:t0 + tn], kc == 0, kc == KC - 1) for kc in range(KC)],
                       [("ring", r)] + [("hq", kc, qb) for kc in range(KC) for qb in range(t0 // 128, (t0 + tn) // 128)], [("ps", b)])
                    qk_norm_tile(b, tn, 0, qT[:, h, t0:t0 + tn], [("qT", h, qb) for qb in range(t0 // 128, (t0 + tn) // 128)], cnt)
                    cnt += 1

        qk_flush()
        tr.tensor("obT", ZB + 18432, ZB + 36864)
        tr.tensor("crow", ZD + 33280, ZD + 35584)
        tr.tensor("obh", ZD + 35584, ZD + 36608)
        tr.tensor("pt", ZC + 32768, ZC + 35840)
        obT = bf(ZB + 18432, H * TQ).rearrange("p (h t) -> p h t", h=H)
        crow = bf(ZD + 33280, TQ)
        obh = [bf(ZD + 35584 + i * 256, 128) for i in range(4)]
        ptb = [bf(ZC + 32768 + i * 1024, 512) for i in range(3)]
        TILES = [(0, 1), (1, 5), (5, 9)]
        idf = cst[:, C_ID:C_ID + 128]

        def tp32(out, in_):
            return lambda e: e.transpose(out, in_, idf)

        dve(lambda e: e.memset(crow, 0.0), [], [("crow",)])
        ba, bb, bc = nbank(), nbank(), nbank()
        pe([tp32(ps[0:8, ba, k * 128:(k + 1) * 128], cumN[:, (7 + k) * 8:(8 + k) * 8]) for k in range(4)], [("fx", 5), ("cst",)], [("ps", ba)])
        pe([tp32(ps[0:8, bb, k * 128:(k + 1) * 128], cumN[:, (11 + k) * 8:(12 + k) * 8]) for k in range(4)], [("fx", 5), ("cst",)], [("ps", bb)])
        pe([tp32(ps[0:8, bc, 0:128], cumN[:, 15 * 8:16 * 8])]
           + [tp32(ps[0:8, bc, (1 + T) * 128:(2 + T) * 128], refN[:, WBMID[T] * 8:(WBMID[T] + 1) * 8]) for T in range(3)],
           [("fx", 5), ("fx", 6), ("cst",)], [("ps", bc)])
        dve(lambda e: e.tensor_copy(out=rsb[0:8, 0:3], in_=ps[0:8, bc, 128:512].rearrange("p (t n) -> p t n", t=3)[:, :, 0]), [("ps", bc)], [("rsb",)])
        for qb in range(QB):
            bank, k = (ba, qb) if qb < 4 else ((bb, qb - 4) if qb < 8 else (bc, 0))
            T = 0 if qb == 0 else (1 if qb < 5 else 2)
            dve(lambda e, bank=bank, k=k, T=T, qb=qb: e.tensor_scalar(out=crow[0:8, qb * 128:(qb + 1) * 128], in0=ps[0:8, bank, k * 128:(k + 1) * 128],
                                                                       scalar1=-1.0, scalar2=rsb[0:8, T:T + 1], op0=ALU.mult, op1=ALU.add),
                [("ps", bank), ("rsb",), ("crow",)], [("crow",)])

        units = []
        for T, (q0, q1) in enumerate(TILES):
            for h in range(H):
                for i in range(7 + q1):
                    units.append((T, h, i, max(q0, i - 7), q1, q0))
        sbank = {}
        s_rr = [0]
        p_rr = [0]
        o_rr = [0]
        t_rr = [0]
        S_BANKS = (0, 1, 7)
        pbt6 = ps[:, 6, :].bitcast(BF16)
        for k in range(8):
            tr.merge(("pst", k), [("ps", 6)])

        def emit_S(u):
            T, h, i, qlo, q1, q0 = u
            n = (q1 - qlo) * 128
            b = S_BANKS[s_rr[0] % 3]
            s_rr[0] += 1
            sbank[u] = b
            diag = (i - 7) >= q0
            fns = [mm(ps[:, b, 0:n], kT[:, h, i * 128:(i + 1) * 128], qT[:, h, qlo * 128:q1 * 128], True, False),
                   mm(ps[:, b, 0:n], sel[:, h, :], crow[:, qlo * 128:q1 * 128], False, not diag)]
            if diag:
                fns.append(mm(ps[:, b, 0:128], ident, tri_bf, False, True))
            pe(fns, [("kT", h, i), ("crow",), ("sel",), ("ident",), ("tri_bf",)] + [("qT", h, qb) for qb in range(qlo, q1)], [("ps", b)])

        def emit_PV(u):
            T, h, i, qlo, q1, q0 = u
            n = (q1 - qlo) * 128
            b = sbank[u]
            pi = p_rr[0] % 3
            p_rr[0] += 1
            act(lambda e: e.activation(out=ptb[pi][:, 0:n], in_=ps[:, b, 0:n], func=AF.Exp, bias=biasK[:, i, h, T:T + 1]),
                [("ps", b), ("biasK", T)], [("pt", pi)])
            for qb in range(qlo, q1):
                j = 7 + qb
                bo = 2 + (qb - q0)
                pe([mm(ps[:, bo, 0:129], ptb[pi][:, (qb - qlo) * 128:(qb - qlo + 1) * 128], va[:, i, h, 0:129], i == 0, i == j)],
                   [("pt", pi), ("va", i, h), ("va", "ones")], [("ps", bo)])
                if i == j:
                    rc = S_REC + o_rr[0] % 8
                    oi = o_rr[0] % 4
                    o_rr[0] += 1
                    dve(lambda e, bo=bo, rc=rc: e.reciprocal(out=st[:, rc:rc + 1], in_=ps[:, bo, 128:129]), [("ps", bo)], [("st", rc)])
                    dve(lambda e, bo=bo, rc=rc, oi=oi: e.tensor_scalar(out=obh[oi], in0=ps[:, bo, 0:128], scalar1=st[:, rc:rc + 1], scalar2=None, op0=ALU.mult),
                        [("ps", bo), ("st", rc)], [("obh", oi)])
                    k = t_rr[0] % 8
                    t_rr[0] += 1

                    def fin(k=k, oi=oi, h=h, qb=qb):
                        pe([tp(pbt6[:, k * 128:(k + 1) * 128], obh[oi])], [("obh", oi), ("ident",)], [("pst", k)])
                        dve(lambda e: e.tensor_copy(out=obT[:, h, qb * 128:(qb + 1) * 128], in_=pbt6[:, k * 128:(k + 1) * 128]),
                            [("pst", k)], [("obT", h, qb)])
                    fin_pend.append(fin)

        fin_pend = []
        emit_S(units[0])
        emit_S(units[1])
        for ui, u in enumerate(units):
            if ui + 2 < len(units):
                emit_S(units[ui + 2])
            ready = list(fin_pend)
            del fin_pend[:]
            emit_PV(u)
            for f_ in ready:
                f_()
        for f_ in fin_pend:
            f_()
        tr.merge(("ps", 6), [("pst", k) for k in range(8)])

        tr.tensor("yT", ZC, ZC + 36864)
        tr.tensor("gtmp", ZB, ZB + 8192)
        yT = bf(ZC, KC * TQ).rearrange("p (k t) -> p k t", k=KC)
        sgt = [f32(ZB + i * 2048, 512) for i in range(4)]
        gcnt = 0
        for c in range(16):
            ra = load_w("ga")
            rb = load_w("gb")
            wa = ring[:, ra, 0:3072].rearrange("p (k n) -> p k n", k=24)
            wb_ = ring[:, rb, 0:3072].rearrange("p (k n) -> p k n", k=24)
            for (t0, tn) in ((126, 342), (468, 342), (810, 342)):
                qbs = range(t0 // 128, (t0 + tn - 1) // 128 + 1)
                hqa = [("hq", kc, qb) for kc in range(KC) for qb in qbs]
                b1, b2, b3, b4 = nbank(), nbank(), nbank(), nbank()
                pe([mm(ps[:, b1, 0:tn], wa[:, kc, :], hq[:, kc, t0:t0 + tn], kc == 0, kc == KC - 1) for kc in range(KC)],
                   [("ring", ra)] + hqa, [("ps", b1)])
                pe([mm(ps[:, b2, 0:tn], wa[:, 16 + kc, :], oa[:, kc, t0:t0 + tn], kc == 0, kc == 7) for kc in range(8)],
                   [("ring", ra)] + [("oa", g, qb) for g in range(8) for qb in qbs], [("ps", b2)])
                pe([mm(ps[:, b3, 0:tn], wb_[:, kc, :], hq[:, kc, t0:t0 + tn], kc == 0, kc == KC - 1) for kc in range(KC)],
                   [("ring", rb)] + hqa, [("ps", b3)])
                pe([mm(ps[:, b4, 0:tn], wb_[:, 16 + kc, :], obT[:, kc, t0:t0 + tn], kc == 0, kc == 7) for kc in range(8)],
                   [("ring", rb)] + [("obT", hh, qb) for hh in range(8) for qb in qbs], [("ps", b4)])
                s1, s2 = sgt[(gcnt % 2) * 2], sgt[(gcnt % 2) * 2 + 1]
                a1, a2 = ("gtmp", (gcnt % 2) * 2), ("gtmp", (gcnt % 2) * 2 + 1)
                gcnt += 1
                act(lambda e, b1=b1, s1=s1, tn=tn: e.activation(out=s1[:, 0:tn], in_=ps[:, b1, 0:tn], func=AF.Sigmoid), [("ps", b1)], [a1])
                act(lambda e, b3=b3, s2=s2, tn=tn: e.activation(out=s2[:, 0:tn], in_=ps[:, b3, 0:tn], func=AF.Sigmoid), [("ps", b3)], [a2])
                dve(lambda e, b2=b2, s1=s1, tn=tn: e.tensor_tensor(out=s1[:, 0:tn], in0=ps[:, b2, 0:tn], in1=s1[:, 0:tn], op=ALU.mult), [("ps", b2), a1], [a1])
                dve(lambda e, b4=b4, s2=s2, tn=tn: e.tensor_tensor(out=s2[:, 0:tn], in0=ps[:, b4, 0:tn], in1=s2[:, 0:tn], op=ALU.mult), [("ps", b4), a2], [a2])
                dve(lambda e, s1=s1, s2=s2, c=c, t0=t0, tn=tn: e.tensor_tensor(out=yT[:, c, t0:t0 + tn], in0=s1[:, 0:tn], in1=s2[:, 0:tn], op=ALU.add),
                    [a1, a2], [("yT", c, qb) for qb in qbs])

        tr.tensor("x1", ZA, ZA + 65536)
        tr.tensor("xh", ZA + 65536, ZA + 73728)
        tr.tensor("xp", ZE, ZE + 8192)
        x1 = f32(ZA, 8 * D).rearrange("p (b n) -> p b n", b=8)
        xh = f32(ZA + 65536, D)
        xp = [f32(ZE + i * 2048, 512) for i in range(4)]
        xc = 0
        for cg in range(4):
            r0 = load_w("wo")
            r1 = load_w("wo")
            wo = [ring[:, r0, :].rearrange("p (k n) -> p k n", k=8), ring[:, r1, :].rearrange("p (k n) -> p k n", k=8)]
            for qb in range(QB):
                xi = xc % 4
                xc += 1
                tr.dma("sp", "xp%d" % xi,
                       lambda e, xi=xi, qb=qb, cg=cg: e.dma_start(out=xp[xi], in_=xw[(7 + qb) * 128:(8 + qb) * 128, cg * 512:(cg + 1) * 512]),
                       writes=[("xp", xi)])
                b = nbank()
                pe([mm(ps[:, b, :], yT[:, kc, qb * 128:(qb + 1) * 128], wo[kc // 8][:, kc % 8, :], kc == 0, kc == KC - 1) for kc in range(KC)],
                   [("ring", r0), ("ring", r1)] + [("yT", kc, qb) for kc in range(KC)], [("ps", b)])
                if qb == 0:
                    o, oat = xh[:, cg * 512:(cg + 1) * 512], ("xh", cg)
                else:
                    o, oat = x1[:, qb - 1, cg * 512:(cg + 1) * 512], ("x1", qb - 1, cg)
                dve(lambda e, b=b, o=o, xi=xi: e.tensor_tensor(out=o, in0=ps[:, b, :], in1=xp[xi], op=ALU.add), [("ps", b), ("xp", xi)], [oat])

        tr.tensor("h2", ZD, ZD + 36864)
        tr.tensor("ctmp", ZC, ZC + 36864)
        h2 = bf(ZD, KC * TQ).rearrange("p (k t) -> p k t", k=KC)
        xn2 = [bf(ZC, D), bf(ZC + 4096, D), bf(ZC + 24608 + 4096, D)]
        xn2_atoms = [[("ctmp", "xn", 0)], [("ctmp", "xn", 1)], [("ctmp", "tA", 1)]]
        junk2 = bf(ZC + 24608, D)
        gb2 = f32(ZC + 8192, D)
        tr.dma("sp", "gb2", lambda e: e.dma_start(out=gb2, in_=gbc_d[1, :, :]), writes=[("ctmp", "gb")])
        items = []
        for blk in range(QB):
            src = xh if blk == 0 else x1[:, blk - 1, :]
            satoms = [("xh", cg) for cg in range(4)] if blk == 0 else [("x1", blk - 1, cg) for cg in range(4)]
            items.append(dict(src=src, src_atoms=satoms, gb=gb2, gb_atom=("ctmp", "gb"), xn=xn2[blk % 3], xn_atoms=xn2_atoms[blk % 3],
                              junk=junk2, junk_atoms=[("ctmp", "tA", 0)],
                              dst_fn=lambda k0, blk=blk: h2[:, k0:k0 + 8, blk * 128:(blk + 1) * 128],
                              dst_atoms=lambda k0, blk=blk: [("h2", k0 + k, blk) for k in range(8)], scol=blk))
        norm_phase(items)

        a_sb = [f32(ZC + 16384 + i * 4112, 1028) for i in range(2)]
        tA = [f32(ZC + 24608 + i * 4096, 1024) for i in range(2)]
        tr.tensor("hid", ZE + 8192, ZE + 16384)
        hid = [bf(ZE + 8192 + i * 4096, 2048).rearrange("p (c t) -> p c t", c=2) for i in range(2)]
        tB = [f32(ZE + i * 4096, 1024) for i in range(2)]
        tr.tensor("tB", ZE, ZE + 8192)
        cw = cst[:, C_CW:C_CW + 132].rearrange("p (c j) -> p c j", j=3)
        cb = cst[:, C_CB:C_CB + 44]
        fcnt = [0]

        BA0, BA1, BAH, BB0, BB1 = 0, 1, 2, 3, 4
        d_rr = [0]

        def U_chunk_steps(g, cc):
            c = 2 * g + cc
            hb = hid[g % 2]
            i2 = c % 2
            asb, ta, tb_ = a_sb[i2], tA[i2], tB[i2]
            aat, tat, tbt = ("ctmp", "a", i2), ("ctmp", "tA", i2), ("tB", i2)
            box = {}

            def agroup(b, t0, tn):
                w = box["w"]
                pe([mm(ps[:, b, 0:tn], w[:, kc, :], h2[:, kc, t0:t0 + tn], kc == 0, kc == KC - 1) for kc in range(KC)],
                   [("ring", box["r"])] + [("h2", kc, blk) for kc in range(KC) for blk in range(t0 // 128, (t0 + tn - 1) // 128 + 1)], [("ps", b)])

            def bgroup(b, t0, tn):
                w = box["w"]
                pe([mm(ps[:, b, 0:tn], w[:, 16 + kc, :], h2[:, kc, t0:t0 + tn], kc == 0, kc == KC - 1) for kc in range(KC)],
                   [("ring", box["r"])] + [("h2", kc, blk) for kc in range(KC) for blk in range(t0 // 128, (t0 + tn) // 128)], [("ps", b)])

            def s0():
                box["r"] = load_w("uc")
                box["w"] = ring[:, box["r"], :].rearrange("p (k n) -> p k n", k=32)
                agroup(BA0, 126, 342)
                agroup(BA1, 468, 342)
                act(lambda e: e.activation(out=asb[:, 0:342], in_=ps[:, BA0, 0:342], func=AF.Copy), [("ps", BA0)], [aat])
                act(lambda e: e.activation(out=asb[:, 0:2], in_=asb[:, 0:2], func=AF.Copy, scale=cst[:, C_FL + 1:C_FL + 2]), [aat, ("cst",)], [aat])
                act(lambda e: e.activation(out=asb[:, 342:684], in_=ps[:, BA1, 0:342], func=AF.Copy), [("ps", BA1)], [aat])

            def s1():
                agroup(BAH, 810, 342)
                act(lambda e: e.activation(out=asb[:, 684:1026], in_=ps[:, BAH, 0:342], func=AF.Copy), [("ps", BAH)], [aat])

            def s1post():
                dve(lambda e: e.tensor_scalar(out=ta, in0=asb[:, 2:1026], scalar1=cw[:, c, 2:3], scalar2=cb[:, c:c + 1], op0=ALU.mult, op1=ALU.add),
                    [aat, ("cst",)], [tat])
                dve(lambda e: e.scalar_tensor_tensor(out=tb_, in0=asb[:, 1:1025], scalar=cw[:, c, 1:2], in1=ta, op0=ALU.mult, op1=ALU.add),
                    [aat, tat, ("cst",)], [tbt])
                dve(lambda e: e.scalar_tensor_tensor(out=ta, in0=asb[:, 0:1024], scalar=cw[:, c, 0:1], in1=tb_, op0=ALU.mult, op1=ALU.add),
                    [aat, tbt, ("cst",)], [tat])
                act(lambda e: e.activation(out=tb_, in_=ta, func=AF.Gelu_apprx_tanh), [tat], [tbt])

            def s2():
                bgroup(BB0, 128, 512)

            def s2post():
                dve(lambda e: e.tensor_tensor(out=hb[:, cc, 0:512], in0=ps[:, BB0, :], in1=tb_[:, 0:512], op=ALU.mult),
                    [("ps", BB0), tbt], [("hid", g % 2, cc, 0)])

            def s3():
                bgroup(BB1, 640, 512)

            def s3post():
                dve(lambda e: e.tensor_tensor(out=hb[:, cc, 512:1024], in0=ps[:, BB1, :], in1=tb_[:, 512:1024], op=ALU.mult),
                    [("ps", BB1), tbt], [("hid", g % 2, cc, 1)])

            return [(s0, None), (s1, s1post), (s2, s2post), (s3, s3post)]

        def D_steps(g):
            box = {}
            hb = hid[g % 2]

            def preload():
                if "r" not in box:
                    box["r"] = load_w("dn")
                    box["w"] = ring[:, box["r"], :].rearrange("p (c n) -> p c n", c=2)

            def piece(tb, cg):
                def f():
                    preload()
                    wd = box["w"]
                    b = 5 + d_rr[0] % 3
                    d_rr[0] += 1
                    pe([mm(ps[:, b, :], hb[:, cc, tb * 128:(tb + 1) * 128], wd[:, cc, cg * 512:(cg + 1) * 512], cc == 0, cc == 1) for cc in range(2)],
                       [("ring", box["r"])] + [("hid", g % 2, cc, tb // 4) for cc in range(2)], [("ps", b)])
                    dve(lambda e: e.tensor_tensor(out=x1[:, tb, cg * 512:(cg + 1) * 512], in0=ps[:, b, :],
                                                  in1=x1[:, tb, cg * 512:(cg + 1) * 512], op=ALU.add),
                        [("ps", b), ("x1", tb, cg)], [("x1", tb, cg)])
                return f
            return [preload] + [piece(tb, cg) for tb in range(8) for cg in range(4)]

        def p12_prelude():
            tr.tensor("h3", ZD, ZD + 32768)
            tr.tensor("c12", ZC, ZC + 36864)
            tr.tensor("pp", ZE, ZE + 8192)
            tr.dma("sp", "gb3", lambda e: e.dma_start(out=gb3, in_=gbc_d[2, :, :]), writes=[("c12", "gb")])
            tr.dma("pool", "pp", lambda e: e.dma_start(out=bf(ZE, 2 * D), in_=pproj_d[:, :]), writes=[("pp",)])
            tr.dma("pool", "pT", lambda e: e.dma_start(out=bf(ZC + 16384, 2 * TO), in_=pw[:, :]), writes=[("c12", "pT", tb) for tb in range(8)])

        h3 = bf(ZD, KC * TO).rearrange("p (k t) -> p k t", k=KC)
        xn3 = [bf(ZC, D), bf(ZC + 4096, D), bf(ZC + 27648, D)]
        xn3_atoms = [[("c12", "xn", 0)], [("c12", "xn", 1)], [("c12", "te", 0), ("c12", "te", 1)]]
        junk3 = bf(ZC + 23552, D)
        gb3 = f32(ZC + 8192, D)
        pT = bf(ZC + 16384, 2 * TO).rearrange("p (k t) -> p k t", k=2)
        pf = [f32(ZC + 20480 + i * 1024, 256) for i in range(2)]
        pb_ = [bf(ZC + 22528 + i * 512, 256) for i in range(2)]
        sg = [f32(ZC + 23552 + i * 2048, 512) for i in range(2)]
        te = [f32(ZC + 27648 + i * 2048, 512) for i in range(2)]
        gpl = [f32(ZC + 31744 + i * 2048, 512) for i in range(2)]
        ppj = bf(ZE, 2 * D).rearrange("p (k n) -> p k n", k=2)

        for g in range(NG + 1):
            usteps = (U_chunk_steps(g, 0) + U_chunk_steps(g, 1)) if g < NG else []
            dsteps = D_steps(g - 1) if g >= 1 else []
            if dsteps:
                dpre, dsteps = dsteps[0], dsteps[1:]
            if g == NG:
                dpre()
                p12_prelude()
            if usteps:
                per = len(dsteps) // len(usteps)
                for i, (u, post) in enumerate(usteps):
                    u()
                    for d in dsteps[i * per:(i + 1) * per]:
                        d()
                    if post is not None:
                        post()
            else:
                for d in dsteps:
                    d()

        def e_stats(tb):
            for cg in range(4):
                b = nbank()
                pe([mm(ps[:, b, :], pT[:, k, tb * 128:(tb + 1) * 128], ppj[:, k, cg * 512:(cg + 1) * 512], k == 0, k == 1) for k in range(2)],
                   [("c12", "pT", tb), ("pp",)], [("ps", b)])
                i2 = (tb * 4 + cg) % 2
                act(lambda e, b=b, tb=tb, cg=cg, i2=i2: e.activation(out=gpl[i2], in_=ps[:, b, :], func=AF.Square,
                                                                      accum_out=st[:, S_ES + tb * 4 + cg:S_ES + tb * 4 + cg + 1]),
                    [("ps", b)], [("c12", "gpl", i2), ("st", S_ES + tb * 4 + cg)])
            dve(lambda e, tb=tb: e.tensor_reduce(out=st[:, S_ER + tb:S_ER + tb + 1], in_=st[:, S_ES + tb * 4:S_ES + tb * 4 + 4], axis=AX.X, op=ALU.add),
                [("st", S_ES + tb * 4 + cg) for cg in range(4)], [("st", S_ER + tb)])
            act(lambda e, tb=tb: e.activation(out=st[:, S_ER + tb:S_ER + tb + 1], in_=st[:, S_ER + tb:S_ER + tb + 1], func=AF.Sqrt, scale=1.0 / D, bias=EPS),
                [("st", S_ER + tb)], [("st", S_ER + tb)])
            dve(lambda e, tb=tb: e.reciprocal(out=st[:, S_ER + tb:S_ER + tb + 1], in_=st[:, S_ER + tb:S_ER + tb + 1]), [("st", S_ER + tb)], [("st", S_ER + tb)])
        items = []
        for tb in range(8):
            items.append(dict(src=x1[:, tb, :], src_atoms=[("x1", tb, cg) for cg in range(4)], gb=gb3, gb_atom=("c12", "gb"),
                              xn=xn3[tb % 3], xn_atoms=xn3_atoms[tb % 3], junk=junk3, junk_atoms=[("c12", "sg", 0), ("c12", "sg", 1)],
                              dst_fn=lambda k0, tb=tb: h3[:, k0:k0 + 8, tb * 128:(tb + 1) * 128],
                              dst_atoms=lambda k0, tb=tb: [("h3", k0 + k, tb) for k in range(8)], scol=tb))
        norm_phase(items, hook=lambda i: e_stats(i) if i < 8 else None)
        for cg in range(4):
            r0 = load_w("wg")
            r1 = load_w("wg")
            wg = [ring[:, r0, :].rearrange("p (k n) -> p k n", k=8), ring[:, r1, :].rearrange("p (k n) -> p k n", k=8)]
            gi = cg % 2
            tr.dma("sp", "gpl%d" % gi, lambda e, gi=gi, cg=cg: e.dma_start(out=gpl[gi], in_=gbc_d[3, :, cg * 512:(cg + 1) * 512]), writes=[("c12", "gpl", gi)])
            for tb in range(8):
                b1, b2 = nbank(), nbank()
                pe([mm(ps[:, b1, :], h3[:, kc, tb * 128:(tb + 1) * 128], wg[kc // 8][:, kc % 8, :], kc == 0, kc == KC - 1) for kc in range(KC)],
                   [("ring", r0), ("ring", r1)] + [("h3", kc, tb) for kc in range(KC)], [("ps", b1)])
                pe([mm(ps[:, b2, :], pT[:, k, tb * 128:(tb + 1) * 128], ppj[:, k, cg * 512:(cg + 1) * 512], k == 0, k == 1) for k in range(2)],
                   [("c12", "pT", tb), ("pp",)], [("ps", b2)])
                i2 = tb % 2
                act(lambda e, b1=b1, i2=i2: e.activation(out=sg[i2], in_=ps[:, b1, :], func=AF.Sigmoid), [("ps", b1)], [("c12", "sg", i2)])
                dve(lambda e, b2=b2, i2=i2, tb=tb, gi=gi: e.scalar_tensor_tensor(out=te[i2], in0=ps[:, b2, :], scalar=st[:, S_ER + tb:S_ER + tb + 1], in1=gpl[gi],
                                                                                 op0=ALU.mult, op1=ALU.mult),
                    [("ps", b2), ("st", S_ER + tb), ("c12", "gpl", gi)], [("c12", "te", i2)])
                dve(lambda e, i2=i2: e.tensor_tensor(out=te[i2], in0=te[i2], in1=sg[i2], op=ALU.mult), [("c12", "te", i2), ("c12", "sg", i2)], [("c12", "te", i2)])
                dve(lambda e, i2=i2, tb=tb, cg=cg: e.tensor_tensor(out=x1[:, tb, cg * 512:(cg + 1) * 512], in0=x1[:, tb, cg * 512:(cg + 1) * 512], in1=te[i2], op=ALU.add),
                    [("c12", "te", i2), ("x1", tb, cg)], [("x1", tb, cg)])
                if cg == 3:
                    tr.dma("sp", "out%d" % tb, lambda e, tb=tb: e.dma_start(out=y[tb * 128:(tb + 1) * 128, :], in_=x1[:, tb, :]),
                           reads=[("x1", tb, c4) for c4 in range(4)], writes=[("yout", tb)])
        tr.wait_all("sp", [("yout", tb) for tb in range(8)])
        assert ring_pos[0] == NL

        with nc.Block() as block:
            @block.sync
            def _(e):
                tr.replay("sp", e)

            @block.gpsimd
            def _(e):
                tr.replay("pool", e)

            @block.tensor
            def _(e):
                tr.replay("pe", e)

            @block.scalar
            def _(e):
                tr.replay("act", e)

            @block.vector
            def _(e):
                tr.replay("dve", e)
    return nc


_CACHE = {}


def kernel(x, p, norm_mix_g, w_in, gmlp_ln_g, gmlp_ln_b, gmlp_w_s, gmlp_b_s, fox_b_f,
           q_norm_g, k_norm_g, w_branch_a, w_branch_b, w_out, norm_ffn_g, w_up,
           conv_w, conv_b, w_down, ple_proj, ple_norm_g, ple_gate_norm_g, w_ple_gate):
    f = lambda a: np.ascontiguousarray(np.asarray(a, dtype=np.float32))
    x, p = f(x), f(p)
    w_in, w_branch_a, w_branch_b, w_out = f(w_in)[0], f(w_branch_a)[0], f(w_branch_b)[0], f(w_out)[0]
    w_up, w_down, w_ple_gate, ple_proj = f(w_up)[0], f(w_down)[0], f(w_ple_gate)[0], f(ple_proj)[0]

    ws = _build_wstream(w_in, w_branch_a, w_branch_b, w_out, w_up, w_down, w_ple_gate)
    pproj = np.ascontiguousarray(ple_proj.reshape(2, P, D).transpose(1, 0, 2).reshape(P, 2 * D))
    rep = lambda v: np.ascontiguousarray(np.broadcast_to(np.asarray(v, np.float32).reshape(1, -1), (P, np.asarray(v).size)))
    gbc = np.stack([rep(f(norm_mix_g)[0]), rep(f(norm_ffn_g)[0]), rep(f(ple_gate_norm_g)[0]), rep(f(ple_norm_g)[0])])
    lngb = np.stack([rep(f(gmlp_ln_g)[0]), rep(f(gmlp_ln_b)[0])])
    wsT = np.ascontiguousarray(f(gmlp_w_s)[0].transpose(2, 0, 1).reshape(P, 1024))
    cst = np.zeros((P, CST_W), np.float32)
    ii = np.arange(P)
    cst[:, C_TRI:C_TRI + 128] = (ii[None, :] >= ii[:, None]).astype(np.float32)
    cst[:, C_E63:C_E63 + 128] = (ii[:, None] <= 63).astype(np.float32)
    cst[:, C_ID:C_ID + 128] = np.eye(P, dtype=np.float32)
    cst[:, C_ONE:C_ONE + 128] = 1.0
    bsrep = rep(f(gmlp_b_s)[0].reshape(-1))
    cst[:, C_BF:C_BF + 128] = rep(np.tile(f(fox_b_f)[0], NB))
    cst[:, C_GQ] = f(q_norm_g)[0]
    cst[:, C_GK] = f(k_norm_g)[0]
    cst[:, C_CW:C_CW + 132] = f(conv_w)[0].reshape(3, NFC, P).transpose(2, 1, 0).reshape(P, 132)
    cst[:, C_CB:C_CB + 44] = f(conv_b)[0].reshape(NFC, P).T

    in_maps = []
    for c in range(NCORES):
        b, hf = c // 2, c % 2
        if hf == 1:
            xwc = x[b]
        else:
            xwc = np.concatenate([x[b, :1024], x[b, :1024]], axis=0)
        cc = cst.copy()
        cc[:, C_FL] = 0.0 if hf == 1 else NEG
        cc[:, C_FL + 1] = 1.0 if hf == 1 else 0.0
        in_maps.append({
            "xw": np.ascontiguousarray(xwc), "pw": np.ascontiguousarray(p[0, b, hf * 1024:(hf + 1) * 1024].T.reshape(2, P, TO).transpose(1, 0, 2).reshape(P, 2 * TO)),
            "wstream": ws, "pproj": pproj, "gbc": gbc, "lngb": lngb, "wsT": wsT, "cst": cc, "bsrep": bsrep,
        })
    if "nc" not in _CACHE:
        _CACHE["nc"] = build_program()
    res = run_bass_kernel_spmd(_CACHE["nc"], in_maps, core_ids=list(range(NCORES)))
    out = np.empty((4, 2048, D), np.float32)
    for c in range(NCORES):
        b, hf = c // 2, c % 2
        out[b, hf * 1024:(hf + 1) * 1024] = res.results[c]["y"]
    return out
```
